# Optimizing a Trainium2 kernel written in Bass

```python
import jax, jax.numpy as jnp
from jax import lax
import numpy as np

D_MODEL = 1024
BATCH = 2
SEQ = 8192
DEPTH = 1
DEC_BATCH = 4
DEC_SEQ = 8192
PAST_LEN = 128

N_META = 16
D_A = D_MODEL
H_A = 8
BW_A = D_A // H_A
CONV_W = 4
CONV_PAD = (2, 1)
RG_C = 8.0
D_B = D_MODEL
HGRN_EXPAND = 128
H_B = D_B // HGRN_EXPAND
DK_B = D_B // H_B
DV_B = D_B // H_B
CHUNK = 64
D_FF = 4 * D_MODEL
EPS = 1e-6
N_IN = 2 * D_A + 5 * D_B + 2 * D_MODEL

kernel_name = "hybrid_rglru_hgrn2_encoder"


def rmsnorm(x, g):
    xf = x.astype(jnp.float32)
    y = xf * lax.rsqrt(jnp.mean(xf * xf, axis=-1, keepdims=True) + EPS)
    return (y * g.astype(jnp.float32)).astype(x.dtype)


def centred_depthwise_conv(x, w, b):
    y = lax.conv_general_dilated(
        x, w[:, None, :].astype(x.dtype), window_strides=(1,), padding=[CONV_PAD],
        dimension_numbers=("NWC", "WIO", "NWC"), feature_group_count=x.shape[-1])
    return y + b.astype(x.dtype)


def _linear_combine(left, right):
    a1, b1 = left
    a2, b2 = right
    return a1 * a2, a2 * b1 + b2


def rglru_direction(xc, wa, ba, wx, bx, lam, reverse):
    B, L, _ = xc.shape
    xh = xc.reshape(B, L, H_A, BW_A)
    r = jax.nn.sigmoid(jnp.einsum("blhi,hij->blhj", xh, wa.astype(jnp.float32)).reshape(B, L, D_A)
                       + ba.astype(jnp.float32))
    i = jax.nn.sigmoid(jnp.einsum("blhi,hij->blhj", xh, wx.astype(jnp.float32)).reshape(B, L, D_A)
                       + bx.astype(jnp.float32))
    log_a = -RG_C * jax.nn.softplus(-lam.astype(jnp.float32)) * r
    a = jnp.exp(log_a)
    mult = jnp.sqrt(-jnp.expm1(2.0 * log_a))
    u = mult * (i * xc)
    _, h = lax.associative_scan(_linear_combine, (a, u), axis=1, reverse=reverse)
    return h


def gla_chunk_scan(q, k, v, logf, s0, chunk):
    B, T, H, K = q.shape
    V = v.shape[-1]
    n = T // chunk

    def to_chunks(t):
        return jnp.moveaxis(t.reshape(B, n, chunk, H, t.shape[-1]), 1, 0)

    qc, kc, vc, gc = (to_chunks(t) for t in (q, k, v, logf))
    bc = jnp.cumsum(gc, axis=2)
    causal = jnp.tril(jnp.ones((chunk, chunk), dtype=bool))[None, :, :, None, None]

    def step(S, inp):
        qi, ki, vi, bi = inp
        o_inter = jnp.einsum("bthk,bhkv->bthv", qi * jnp.exp(bi), S)
        diff = bi[:, :, None] - bi[:, None, :]
        decay = jnp.exp(jnp.where(causal, diff, -jnp.inf))
        scores = jnp.einsum("bthk,bshk,btshk->bhts", qi, ki, decay)
        o_intra = jnp.einsum("bhts,bshv->bthv", scores, vi)
        b_last = bi[:, -1]
        S_new = jnp.exp(b_last)[..., None] * S + jnp.einsum(
            "bshk,bshv->bhkv", ki * jnp.exp(b_last[:, None] - bi), vi)
        return S_new, o_inter + o_intra

    S_T, o = lax.scan(step, s0, (qc, kc, vc, bc))
    o = jnp.moveaxis(o, 0, 1).reshape(B, T, H, V)
    return o, S_T


def hgrn2_bidirectional(q, k_f, k_b, v, g_f, g_b):
    B = q.shape[0]
    m = N_META
    zero = jnp.zeros((B, H_B, DK_B, DV_B), jnp.float32)
    flip = lambda t: jnp.flip(t, axis=1)
    o_meta_f, s_meta = gla_chunk_scan(q[:, :m], k_f[:, :m], v[:, :m], g_f[:, :m], zero, m)
    o_real_f, _ = gla_chunk_scan(q[:, m:], k_f[:, m:], v[:, m:], g_f[:, m:], s_meta, CHUNK)
    o_real_b, s_real = gla_chunk_scan(flip(q[:, m:]), flip(k_b[:, m:]), flip(v[:, m:]),
                                      flip(g_b[:, m:]), zero, CHUNK)
    o_meta_b, _ = gla_chunk_scan(flip(q[:, :m]), flip(k_b[:, :m]), flip(v[:, :m]),
                                 flip(g_b[:, :m]), s_real, m)
    o_f = jnp.concatenate([o_meta_f, o_real_f], axis=1)
    o_b = jnp.concatenate([flip(o_meta_b), flip(o_real_b)], axis=1)
    return o_f + o_b


def hybrid_layer(x, lb, norm_mix_g, w_in, conv_w, conv_b, rg_wa, rg_ba, rg_wx, rg_bx, rg_lambda,
                 hg_norm_g, w_branch_a, w_branch_b, w_out, norm_mlp_g, w_mlp1, w_mlp2):
    B, L, _ = x.shape
    f32 = jnp.float32
    h = rmsnorm(x, norm_mix_g)
    z = h @ w_in.astype(h.dtype)
    sizes = [D_A, D_A, D_B, D_B, D_B, D_B, D_B, D_MODEL, D_MODEL]
    idx = np.cumsum(sizes)[:-1].tolist()
    xa, ya, q_raw, ff_raw, fb_raw, v_raw, og_raw, gate_a, gate_b = jnp.split(z, idx, axis=-1)

    xc = centred_depthwise_conv(xa, conv_w, conv_b).astype(f32)
    h_rnn = (rglru_direction(xc, rg_wa[0], rg_ba[0], rg_wx[0], rg_bx[0], rg_lambda[0], False)
             + rglru_direction(xc, rg_wa[1], rg_ba[1], rg_wx[1], rg_bx[1], rg_lambda[1], True))
    branch_a = (h_rnn * jax.nn.gelu(ya.astype(f32))).astype(x.dtype)

    heads = lambda t: t.astype(f32).reshape(B, L, H_B, -1)
    qh = heads(jax.nn.silu(q_raw.astype(f32)))
    vh = heads(v_raw)

    def forget(f_raw, lbd):
        f = lbd + (1.0 - lbd) * jax.nn.sigmoid(f_raw.astype(f32))
        return heads(1.0 - f), heads(jnp.log(f))

    k_f, g_f = forget(ff_raw, lb[0])
    k_b, g_b = forget(fb_raw, lb[1])
    o = hgrn2_bidirectional(qh, k_f, k_b, vh, g_f, g_b)
    o = o * lax.rsqrt(jnp.mean(o * o, axis=-1, keepdims=True) + EPS)
    o = o.reshape(B, L, D_B) * hg_norm_g.astype(f32) * jax.nn.silu(og_raw.astype(f32))
    branch_b = o.astype(x.dtype)

    merged = (jax.nn.sigmoid(gate_a) * (branch_a @ w_branch_a.astype(x.dtype))
              + jax.nn.sigmoid(gate_b) * (branch_b @ w_branch_b.astype(x.dtype)))
    x = x + merged @ w_out.astype(x.dtype)

    hm = rmsnorm(x, norm_mlp_g) @ w_mlp1.astype(x.dtype)
    x = x + jnp.square(jax.nn.relu(hm)) @ w_mlp2.astype(x.dtype)
    return x


def encode(x, meta_tokens, hg_lb_logits, norm_mix_g, w_in, conv_w, conv_b, rg_wa, rg_ba, rg_wx, rg_bx,
           rg_lambda, hg_norm_g, w_branch_a, w_branch_b, w_out, norm_mlp_g, w_mlp1, w_mlp2, final_norm_g):
    B = x.shape[0]
    meta = jnp.broadcast_to(meta_tokens.astype(x.dtype)[None], (B, N_META, D_MODEL))
    h = jnp.concatenate([meta, x], axis=1)
    lb_all = jnp.cumsum(jax.nn.softmax(hg_lb_logits.astype(jnp.float32), axis=0), axis=0)
    for l in range(DEPTH):
        h = hybrid_layer(h, lb_all[l], norm_mix_g[l], w_in[l], conv_w[l], conv_b[l], rg_wa[l], rg_ba[l],
                         rg_wx[l], rg_bx[l], rg_lambda[l], hg_norm_g[l], w_branch_a[l], w_branch_b[l],
                         w_out[l], norm_mlp_g[l], w_mlp1[l], w_mlp2[l])
    return rmsnorm(h[:, N_META:], final_norm_g)


def setup_inputs(seed: int = 0) -> dict:
    key = jax.random.key(seed)
    ks = jax.random.split(key, 24)
    nrm = lambda k, shape, s: jax.random.normal(k, shape, jnp.float32) * s
    u = jax.random.uniform(ks[12], (DEPTH, 2, D_A), jnp.float32, minval=0.9, maxval=0.999)
    root = u ** (1.0 / RG_C)
    rg_lambda = jnp.log(root) - jnp.log1p(-root)
    return {
        "x_prompt": nrm(ks[0], (BATCH, SEQ, D_MODEL), 1.0),
        "x_sample": nrm(ks[1], (DEC_BATCH, DEC_SEQ, D_MODEL), 1.0),
        "meta_tokens": nrm(ks[2], (N_META, D_MODEL), 1.0),
        "hg_lb_logits": nrm(ks[3], (DEPTH + 1, 2, D_B), 0.5),
        "norm_mix_g": 1.0 + nrm(ks[4], (DEPTH, D_MODEL), 0.02),
        "w_in": nrm(ks[5], (DEPTH, D_MODEL, N_IN), D_MODEL ** -0.5),
        "conv_w": nrm(ks[6], (DEPTH, CONV_W, D_A), CONV_W ** -0.5),
        "conv_b": nrm(ks[7], (DEPTH, D_A), 0.01),
        "rg_wa": nrm(ks[8], (DEPTH, 2, H_A, BW_A, BW_A), BW_A ** -0.5),
        "rg_ba": nrm(ks[9], (DEPTH, 2, D_A), 0.01),
        "rg_wx": nrm(ks[10], (DEPTH, 2, H_A, BW_A, BW_A), BW_A ** -0.5),
        "rg_bx": nrm(ks[11], (DEPTH, 2, D_A), 0.01),
        "rg_lambda": rg_lambda,
        "hg_norm_g": 1.0 + nrm(ks[13], (DEPTH, D_B), 0.02),
        "w_branch_a": nrm(ks[14], (DEPTH, D_A, D_MODEL), D_A ** -0.5),
        "w_branch_b": nrm(ks[15], (DEPTH, D_B, D_MODEL), D_B ** -0.5),
        "w_out": nrm(ks[16], (DEPTH, D_MODEL, D_MODEL), D_MODEL ** -0.5),
        "norm_mlp_g": 1.0 + nrm(ks[17], (DEPTH, D_MODEL), 0.02),
        "w_mlp1": nrm(ks[18], (DEPTH, D_MODEL, D_FF), D_MODEL ** -0.5),
        "w_mlp2": nrm(ks[19], (DEPTH, D_FF, D_MODEL), D_FF ** -0.5),
        "final_norm_g": 1.0 + nrm(ks[20], (D_MODEL,), 0.02),
    }


def reference(x_prompt, x_sample, meta_tokens, hg_lb_logits, norm_mix_g, w_in, conv_w, conv_b, rg_wa, rg_ba,
              rg_wx, rg_bx, rg_lambda, hg_norm_g, w_branch_a, w_branch_b, w_out, norm_mlp_g, w_mlp1, w_mlp2,
              final_norm_g):
    y_prompt = encode(x_prompt, meta_tokens, hg_lb_logits, norm_mix_g, w_in, conv_w, conv_b, rg_wa, rg_ba,
                      rg_wx, rg_bx, rg_lambda, hg_norm_g, w_branch_a, w_branch_b, w_out, norm_mlp_g, w_mlp1,
                      w_mlp2, final_norm_g)
    y_sample = encode(x_sample, meta_tokens, hg_lb_logits, norm_mix_g, w_in, conv_w, conv_b, rg_wa, rg_ba,
                      rg_wx, rg_bx, rg_lambda, hg_norm_g, w_branch_a, w_branch_b, w_out, norm_mlp_g, w_mlp1,
                      w_mlp2, final_norm_g)
    return (y_prompt, y_sample)
```

```python
import numpy as np
import concourse.bass as bass
import concourse.mybir as mybir
from concourse.bass_utils import run_bass_kernel_spmd
from concourse.ap import AP

F32 = mybir.dt.float32
BF16 = mybir.dt.bfloat16
U8 = mybir.dt.uint8
ALU = mybir.AluOpType
AF = mybir.ActivationFunctionType

D = 1024
NMETA = 16
TN = 512
EPS = 1e-6
RG_C = 8.0
NCORES = 8
NDSEM = 24

PP_HEAD = 16
PP_GMIX = 128
PP_GMLP = 136
PP_N = 144
DV_SP = 0
DV_SP2 = 16
DV_LB = 32
DV_OML = 48
DV_HGG = 64
DV_EPS128 = 72
DV_EPSD = 73
DV_TINY = 74
DV_N = 80


def rev_ap(a):
    aps = [list(x) for x in a.ap]
    step, cnt = aps[-1]
    aps[-1] = [-step, cnt]
    return AP(a.tensor, a.offset + step * (cnt - 1), aps)


class Prog:
    def __init__(self):
        self.ops = []
        self.last_w = {}
        self.readers = {}
        self.last_barrier = 0

    def add(self, eng, fn, reads=(), writes=(), dma=False):
        i = len(self.ops)
        deps = set()
        for r in reads:
            if r in self.last_w:
                deps.add(self.last_w[r])
        for w in writes:
            if w in self.last_w:
                deps.add(self.last_w[w])
            deps.update(self.readers.get(w, ()))
        for r in reads:
            self.readers.setdefault(r, []).append(i)
        for w in writes:
            self.last_w[w] = i
            self.readers[w] = []
        self.ops.append(dict(eng=eng, fn=fn, deps=deps, dma=dma, sig=False))
        return i

    def barrier(self):
        n = len(self.ops)
        deps = set()
        last = {}
        for i in range(self.last_barrier, n):
            op = self.ops[i]
            if op['dma']:
                deps.add(i)
            elif op['fn'] is not None:
                last[op['eng']] = i
        deps.update(last.values())
        for e in ('pe', 'act', 'dve', 'pool', 'sp'):
            self.ops.append(dict(eng=e, fn=None, deps=set(deps), dma=False, sig=False))
        self.last_barrier = len(self.ops)
        self.last_w = {}
        self.readers = {}

    def finalize(self, nc, sems, dsems):
        ops = self.ops
        for q, ring in dsems.items():
            dma_idx = [i for i, o in enumerate(ops) if o['dma'] and o['eng'] == q]
            nr = len(ring)
            for j, i in enumerate(dma_idx):
                ops[i]['sem'] = ring[j % nr]
                ops[i]['val'] = 16 * (j // nr + 1)
                ops[i]['sig'] = True
                if j >= nr:
                    ops[i]['deps'].add(dma_idx[j - nr])
        for i, o in enumerate(ops):
            nd = set()
            for d in o['deps']:
                od = ops[d]
                if od['fn'] is None:
                    continue
                if (not od['dma']) and od['eng'] == o['eng'] and o['eng'] == 'pe' and not o['dma']:
                    continue
                nd.add(d)
                od['sig'] = True
            o['deps'] = nd
        cnt = {}
        for o in ops:
            if o['dma'] or o['fn'] is None:
                continue
            if o['sig']:
                cnt[o['eng']] = cnt.get(o['eng'], 0) + 1
                o['sem'] = sems[o['eng']]
                o['val'] = cnt[o['eng']]
        self.n_sig = cnt

    def emit(self, eng_name, e):
        known = {}
        for o in self.ops:
            if o['eng'] != eng_name:
                continue
            waits = {}
            for d in o['deps']:
                od = self.ops[d]
                s, v = od['sem'], od['val']
                k = id(s)
                if known.get(k, 0) >= v:
                    continue
                if k not in waits or waits[k][1] < v:
                    waits[k] = (s, v)
            for k, (s, v) in waits.items():
                e.wait_ge(s, v)
                known[k] = v
            if o['fn'] is None:
                continue
            ins = o['fn'](e)
            if o['sig']:
                ins.then_inc(o['sem'], 16 if o['dma'] else 1)


class Arena:
    def __init__(self, base_ap, nbytes):
        self.base = base_ap
        self.nbytes = nbytes
        self.off = 0
        self.mark_ = 0

    def alloc(self, free_elems, dtype, shape3=None):
        sz = free_elems * (4 if dtype == F32 else 2)
        sz_al = (sz + 31) // 32 * 32
        assert self.off + sz_al <= self.nbytes, f"SBUF arena overflow {self.off + sz_al} > {self.nbytes}"
        a = self.base[:, self.off:self.off + sz].bitcast(dtype)
        self.off += sz_al
        if shape3 is not None:
            a = a.rearrange("p (a b) -> p a b", b=shape3)
        return a

    def mark(self):
        return self.off

    def reset(self, m):
        self.off = m


class Ring:
    def __init__(self, bufs, name, keys=None):
        self.bufs = bufs
        self.name = name
        self.keys = keys
        self.i = 0

    def next(self):
        k = self.i % len(self.bufs)
        self.i += 1
        return self.bufs[k], (self.keys[k] if self.keys else (self.name, k))


def build(NT, stop_after=3):
    T = NT * TN
    L = T + NMETA
    nc = bass.Bass("TRN2", target_bir_lowering=False)
    P = Prog()

    xs = nc.dram_tensor("xs", [L, D], F32, kind="ExternalInput").ap()
    w_mix = nc.dram_tensor("w_mix", [5, 128, 8 * 1024], F32, kind="ExternalInput").ap()
    w_gate = nc.dram_tensor("w_gate", [128, 4 * 8 * 128], F32, kind="ExternalInput").ap()
    w_f2 = nc.dram_tensor("w_f2", [80, 128, 1024], F32, kind="ExternalInput").ap()
    w_out = nc.dram_tensor("w_out", [128, 8 * 1024], F32, kind="ExternalInput").ap()
    w_2 = nc.dram_tensor("w_2", [32, 128, 1024], F32, kind="ExternalInput").ap()
    pp_d = nc.dram_tensor("pp", [128, PP_N], F32, kind="ExternalInput").ap()
    gfin_d = nc.dram_tensor("gfin", [128, D], F32, kind="ExternalInput").ap()
    consts_d = nc.dram_tensor("consts", [128, 128 * 4 + 512 * 2], F32, kind="ExternalInput").ap()
    y_d = nc.dram_tensor("y", [T, D], F32, kind="ExternalOutput").ap()

    hb_scr = nc.dram_tensor("hb_scr", [8, 128, T], F32, kind="Internal").ap()
    ob_scr = nc.dram_tensor("ob_scr", [8, 128, T], F32, kind="Internal").ap()
    hs_scr = nc.dram_tensor("hs_scr", [8, 128, T], F32, kind="Internal").ap()
    on_scr = nc.dram_tensor("on_scr", [8, 128, T], F32, kind="Internal").ap()
    xc_scr = nc.dram_tensor("xc_scr", [8, 128, T], F32, kind="Internal").ap()
    qs_scr = nc.dram_tensor("qs_scr", [8, 128, T], F32, kind="Internal").ap()
    xcb_scr = nc.dram_tensor("xcb_scr", [8, 128, T], BF16, kind="Internal").ap()
    vb_scr = nc.dram_tensor("vb_scr", [8, NT, 128, TN], BF16, kind="Internal").ap()
    hT_scr = nc.dram_tensor("hT_scr", [NT, 128, 8, TN], BF16, kind="Internal").ap()
    wf2_bf = nc.dram_tensor("wf2_bf", [80, 128, 1024], BF16, kind="Internal").ap()
    w2_bf = nc.dram_tensor("w2_bf", [32, 128, 1024], BF16, kind="Internal").ap()

    ARENA_BYTES = 206 * 1024
    arena_t = nc.alloc_sbuf_tensor("arena", [128, ARENA_BYTES], U8).ap()
    A = Arena(arena_t, ARENA_BYTES)
    banks = [nc.alloc_psum_tensor(f"bank{i}", [128, 512], F32).ap() for i in range(8)]

    pp = A.alloc(PP_N, F32)
    dv = A.alloc(DV_N, F32)
    ident = A.alloc(128, BF16)
    ones_f = A.alloc(128, F32)
    mscF = A.alloc(128, F32)
    mscR = A.alloc(128, F32)
    maskF = A.alloc(512, F32)
    maskR = A.alloc(512, F32)
    junk = A.alloc(1024, BF16)
    ss = A.alloc(8, F32)
    rs = A.alloc(8, F32)
    base_mark = A.mark()
    ctmp = A.alloc(128 * 4 + 1024, F32)

    def col(t, c):
        return t[:, c:c + 1]

    def dma(q, out, in_, reads, writes):
        return P.add(q, lambda e: e.dma_start(out=out, in_=in_), reads, writes, dma=True)

    dma('sp', pp, pp_d, [], ['pp'])
    dma('sp', ctmp, consts_d, [], ['ctmp'])
    P.add('dve', lambda e: e.tensor_copy(out=ident, in_=ctmp[:, 0:128]), ['ctmp'], ['ident'])
    P.add('dve', lambda e: e.tensor_copy(out=ones_f, in_=ctmp[:, 128:256]), ['ctmp'], ['ones'])
    P.add('dve', lambda e: e.tensor_copy(out=mscF, in_=ctmp[:, 256:384]), ['ctmp'], ['mscF'])
    P.add('dve', lambda e: e.tensor_copy(out=mscR, in_=ctmp[:, 384:512]), ['ctmp'], ['mscR'])
    P.add('dve', lambda e: e.tensor_copy(out=maskF, in_=ctmp[:, 512:1024]), ['ctmp'], ['maskF'])
    P.add('dve', lambda e: e.tensor_copy(out=maskR, in_=ctmp[:, 1024:1536]), ['ctmp'], ['maskR'])
    for c in range(0, 80, 8):
        dma('pool', wf2_bf[c:c + 8], w_f2[c:c + 8], [], [('wf2', c)])
    for c in range(0, 32, 8):
        dma('pool', w2_bf[c:c + 8], w_2[c:c + 8], [], [('w2s', c)])

    dtmp = A.alloc(64, F32)
    for h in range(8):
        b0 = h * PP_HEAD
        for dr in range(2):
            k = dr * 8 + h
            lam = col(pp, b0 + 9 + dr)
            P.add('act', lambda e, lam=lam, k=k: e.activation(out=col(dtmp, k), in_=lam, func=AF.Exp, scale=-1.0),
                  ['pp'], [('dtmp', k)])
            P.add('act', lambda e, k=k: e.activation(out=col(dtmp, k), in_=col(dtmp, k), func=AF.Ln, bias=1.0),
                  [('dtmp', k)], [('dtmp', k)])
            P.add('dve', lambda e, k=k: e.tensor_scalar(out=col(dv, DV_SP + k), in0=col(dtmp, k), scalar1=-RG_C,
                                                        scalar2=None, op0=ALU.mult), [('dtmp', k)], ['dv'])
            P.add('dve', lambda e, k=k: e.tensor_scalar(out=col(dv, DV_SP2 + k), in0=col(dtmp, k),
                                                        scalar1=-2.0 * RG_C, scalar2=None, op0=ALU.mult),
                  [('dtmp', k)], ['dv'])
            l0 = col(pp, b0 + 11 + dr)
            l1 = col(pp, b0 + 13 + dr)
            P.add('dve', lambda e, l0=l0, l1=l1, k=k: e.tensor_tensor(out=col(dtmp, 16 + k), in0=l0, in1=l1,
                                                                     op=ALU.subtract), ['pp'], [('dtmp', 16 + k)])
        P.add('dve', lambda e, h=h, b0=b0: e.tensor_scalar(out=col(dv, DV_HGG + h), in0=col(pp, b0 + 15),
                                                           scalar1=float(np.sqrt(128.0)), scalar2=None,
                                                           op0=ALU.mult), ['pp'], ['dv'])
    P.add('act', lambda e: e.activation(out=dv[:, DV_LB:DV_LB + 16], in_=dtmp[:, 16:32], func=AF.Sigmoid),
          [('dtmp', 16 + k) for k in range(16)], ['dv'])
    P.add('dve', lambda e: e.tensor_scalar(out=dv[:, DV_OML:DV_OML + 16], in0=dv[:, DV_LB:DV_LB + 16],
                                           scalar1=-1.0, scalar2=1.0, op0=ALU.mult, op1=ALU.add), ['dv'], ['dv'])
    P.add('pool', lambda e: e.memset(col(dv, DV_EPS128), float(128.0 * EPS)), [], ['dv'])
    P.add('pool', lambda e: e.memset(col(dv, DV_EPSD), float(D * EPS)), [], ['dv'])
    P.add('pool', lambda e: e.memset(col(dv, DV_TINY), 1e-30), [], ['dv'])
    P.barrier()
    A.reset(base_mark)

    def stage_x(p0, N, xt_list, hT, hT_key, g_col0, halo, psbank, psbank_key, hcol0=0):
        nsub = (N + 127) // 128
        psT = psbank.bitcast(BF16).rearrange("p (a b) -> p a b", b=128)
        gap = pp[:, g_col0:g_col0 + 8]
        for s in range(nsub):
            npk = min(128, N - 128 * s)
            xt, xk = xt_list[s]
            dma('sp', xt[:npk, :], xs[p0 + 128 * s:p0 + 128 * s + npk, :], [], [xk])
            P.add('pool', lambda e, s=s: e.memset(col(ss, s), 0.0), [], [('ss', s)])
            P.add('act', lambda e, xt=xt, npk=npk, s=s: e.activation(
                out=junk[:npk, :], in_=xt[:npk, :], func=AF.Square, accum_out=ss[:npk, s:s + 1]),
                [xk, ('ss', s)], ['junk', ('ss', s)])
            P.add('act', lambda e, npk=npk, s=s: e.activation(out=rs[:npk, s:s + 1], in_=ss[:npk, s:s + 1], func=AF.Ln, bias=dv[:npk, DV_EPSD:DV_EPSD + 1]), [('ss', s), 'dv'], [('rs', s)])
            P.add('act', lambda e, npk=npk, s=s: e.activation(out=rs[:npk, s:s + 1], in_=rs[:npk, s:s + 1], func=AF.Exp, scale=-0.5), [('rs', s)], [('rs', s)])
            xn, xnk = xn_ring.next()
            P.add('dve', lambda e, xt=xt, xn=xn, npk=npk, s=s: e.tensor_scalar(
                out=xn[:npk, :], in0=xt[:npk, :], scalar1=rs[:npk, s:s + 1], scalar2=32.0,
                op0=ALU.mult, op1=ALU.mult), [xk, ('rs', s)], [xnk])

            def tr(e, xn=xn, npk=npk):
                ins = None
                for kc in range(8):
                    ins = e.transpose(out=psT[:, kc, 0:npk], in_=xn[:npk, kc * 128:(kc + 1) * 128],
                                      identity=ident[:npk, :npk])
                return ins
            P.add('pe', tr, [xnk, 'ident'], [psbank_key])
            c0 = hcol0 + 128 * s
            P.add('dve', lambda e, npk=npk, c0=c0: e.tensor_tensor(
                out=hT[:, :, c0:c0 + npk], in0=psT[:, :, 0:npk],
                in1=gap.unsqueeze(2).broadcast_to([128, 8, npk]), op=ALU.mult),
                [psbank_key, 'pp'], [hT_key])
        if halo:
            xh, xhk = xt_ring.next()
            xnh, xnhk = xn_ring.next()
            P.add('pool', lambda e: e.memset(xh[0:3, :], 0.0), [], [xhk])
            if p0 >= 2:
                dma('sp', xh[0:2, :], xs[p0 - 2:p0, :], [], [xhk])
            if p0 + N < L:
                dma('sp', xh[2:3, :], xs[p0 + N:p0 + N + 1, :], [], [xhk])
            P.add('pool', lambda e: e.memset(ss[0:3, 7:8], 0.0), [], [('ss', 7)])
            P.add('act', lambda e: e.activation(out=junk[0:3, :], in_=xh[0:3, :], func=AF.Square,
                                                accum_out=ss[0:3, 7:8]), [xhk, ('ss', 7)], ['junk', ('ss', 7)])
            P.add('act', lambda e: e.activation(out=rs[0:3, 7:8], in_=ss[0:3, 7:8], func=AF.Ln,
                                                bias=dv[0:3, DV_EPSD:DV_EPSD + 1]), [('ss', 7), 'dv'], [('rs', 7)])
            P.add('act', lambda e: e.activation(out=rs[0:3, 7:8], in_=rs[0:3, 7:8], func=AF.Exp, scale=-0.5),
                  [('rs', 7)], [('rs', 7)])
            P.add('dve', lambda e: e.tensor_scalar(out=xnh[0:3, :], in0=xh[0:3, :], scalar1=rs[0:3, 7:8],
                                                   scalar2=32.0, op0=ALU.mult, op1=ALU.mult),
                  [xhk, ('rs', 7)], [xnhk])

            def trh(e):
                ins = None
                for kc in range(8):
                    ins = e.transpose(out=psT[:, kc, 0:3], in_=xnh[0:3, kc * 128:(kc + 1) * 128],
                                      identity=ident[0:3, 0:3])
                return ins
            P.add('pe', trh, [xnhk, 'ident'], [psbank_key])
            P.add('dve', lambda e: e.tensor_tensor(
                out=hT[:, :, N:N + 3], in0=psT[:, :, 0:3],
                in1=gap.unsqueeze(2).broadcast_to([128, 8, 3]), op=ALU.mult),
                [psbank_key, 'pp'], [hT_key])

    wmix = [A.alloc(8 * 1024, BF16, shape3=1024) for _ in range(4)]
    wg = A.alloc(2 * 8 * 128, BF16).rearrange("p (g h j) -> p g h j", g=2, h=8)
    GXA, GQ, GF, GV = range(4)
    for g, src in ((GXA, 0), (GQ, 1), (GV, 4)):
        dma('pool', wmix[g].rearrange("p a b -> p (a b)"), w_mix[src], [], [('wmix', g)])

    xt_ring = Ring([A.alloc(1024, F32) for _ in range(2)], 'xt')
    xn_ring = Ring([A.alloc(1024, BF16) for _ in range(1)], 'xn')
    hT_bufs = [A.alloc(8 * (TN + 3), BF16, shape3=TN + 3) for _ in range(1)]
    carryA = A.alloc(8, F32)
    S_f = A.alloc(8 * 128, F32, shape3=128)
    S_b = A.alloc(8 * 128, BF16, shape3=128)

    def tmp(n=TN, dt=F32, cnt=2, name=None):
        return Ring([A.alloc(n, dt) for _ in range(cnt)], name)
    t_ext = tmp(TN + 3, F32, 2, 'ext')
    t_xc = tmp(name='xc')
    t_xcb = tmp(dt=BF16, name='xcb')
    t_r = tmp(name='r')
    t_i = tmp(name='i')
    t_a = tmp(name='a')
    t_a2 = tmp(cnt=2, name='a2')
    t_u = tmp(cnt=2, name='u')
    t_h = tmp(name='h')
    t_hb = tmp(cnt=2, name='hbl')
    t_sg = tmp(cnt=1, name='sg')
    t_f = tmp(name='f')
    t_qs = tmp(cnt=2, name='qs')
    t_g = tmp(cnt=1, name='g')
    t_b = tmp(cnt=2, name='b')
    t_eb = tmp(cnt=4, name='eb')
    t_enb = tmp(cnt=2, name='enb')
    t_qt = tmp(dt=BF16, cnt=4, name='qt')
    t_kt = tmp(dt=BF16, cnt=4, name='kt')
    t_kh = tmp(dt=BF16, cnt=4, name='kh')
    t_vb = tmp(dt=BF16, cnt=4, name='vb')
    khT4 = [[A.alloc(128, BF16) for _ in range(4)] for _ in range(2)]
    PT4 = [[A.alloc(128, BF16) for _ in range(4)] for _ in range(2)]
    t_ob = [tmp(cnt=1, name='obl0'), tmp(cnt=1, name='obl1')]
    t_os = [tmp(cnt=1, name='os0'), tmp(cnt=1, name='os1')]
    t_osq = [tmp(cnt=1, name='osq0'), tmp(cnt=1, name='osq1')]
    t_rso = [tmp(cnt=1, name='rso0'), tmp(cnt=1, name='rso1')]
    mix_mark_end = A.mark()

    gen_ring = Ring(banks[0:3], 'psg')
    halo_ring = Ring([banks[3][:, 0:4], banks[3][:, 4:8]], 'pshalo', keys=['bank3', 'bank3'])
    kT_slots = [banks[3][:, 64:128].bitcast(BF16), banks[3][:, 128:192].bitcast(BF16)]
    sc_slots = [banks[3][:, 256:384], banks[3][:, 384:512]]
    ch_o = [(banks[4], 'bank4'), (banks[6], 'bank6')]
    ch_m = [(banks[5], 'bank5'), (banks[7], 'bank7')]

    def stage1_steps(dirn, p0, N, hT, hT_key, meta, h, st):
        rv = (dirn == 1)
        R = (lambda a: rev_ap(a)) if rv else (lambda a: a)
        nsub = (N + 127) // 128
        CL = min(64, N)
        pr0 = p0 - NMETA
        combine = (dirn == 0) and not meta
        hc = slice(h * 128, (h + 1) * 128)
        b0 = h * PP_HEAD
        npv = min(128, N)
        msk = maskR if rv else maskF
        mkey = 'maskR' if rv else 'maskF'
        nch = N // CL
        lc = 0 if rv else CL - 1
        lastc = 0 if rv else N - 1
        V = {}
        steps = []

        def step(f):
            steps.append(f)
            return f

        cload = combine
        cstore = (dirn == 1)

        @step
        def s_xa():
            if combine:
                V['hbl'], V['khbl'] = t_hb.next()
                dma('sp', V['hbl'][:, 0:N], hb_scr[h, :, pr0:pr0 + N], [], [V['khbl']])
            if cload:
                xcb, kxcb = t_xcb.next()
                xc, kxc = t_xc.next()
                qs, kqs = t_qs.next()
                vb, kvb = t_vb.next()
                V.update(xcb=xcb, kxcb=kxcb, xc=xc, kxc=kxc, qs=qs, kqs=kqs, vb=vb, kvb=kvb)
                dma('sp', xcb[:, 0:N], xcb_scr[h, :, pr0:pr0 + N], [], [kxcb])
                dma('sp', xc[:, 0:N], xc_scr[h, :, pr0:pr0 + N], [], [kxc])
                dma('sp', qs[:, 0:N], qs_scr[h, :, pr0:pr0 + N], [], [kqs])
                dma('sp', vb[:, 0:N], vb_scr[h, pr0 // TN], [], [kvb])
                return
            ps_xa, kxa = gen_ring.next()
            ps_hl, khl = halo_ring.next()
            V.update(ps_xa=ps_xa, kxa=kxa, ps_hl=ps_hl, khl=khl)

            def mm_xa(e):
                ins = None
                for kc in range(8):
                    e.matmul(ps_xa[:, 0:N], lhsT=wmix[GXA][:, kc, hc], rhs=hT[:, kc, 0:N],
                             start=(kc == 0), stop=(kc == 7))
                for kc in range(8):
                    ins = e.matmul(ps_hl[:, 0:3], lhsT=wmix[GXA][:, kc, hc], rhs=hT[:, kc, N:N + 3],
                                   start=(kc == 0), stop=(kc == 7))
                return ins
            P.add('pe', mm_xa, [hT_key, ('wmix', GXA)], [kxa, khl])

        @step
        def s_ext():
            if cload:
                return
            ext, kext = t_ext.next()
            V.update(ext=ext, kext=kext)
            ps_xa, ps_hl = V['ps_xa'], V['ps_hl']
            P.add('act', lambda e: e.activation(out=ext[:, 2:2 + N], in_=ps_xa[:, 0:N], func=AF.Copy),
                  [V['kxa']], [kext])
            P.add('dve', lambda e: e.tensor_copy(out=ext[:, 0:2], in_=ps_hl[:, 0:2]), [V['khl']], [kext])
            P.add('dve', lambda e: e.tensor_copy(out=ext[:, N + 2:N + 3], in_=ps_hl[:, 2:3]), [V['khl']], [kext])

        @step
        def s_q():
            if cload:
                return
            ps_q, kq_ = gen_ring.next()
            V.update(ps_q=ps_q, kq_=kq_)

            def mm_q(e):
                ins = None
                for kc in range(8):
                    ins = e.matmul(ps_q[:, 0:N], lhsT=wmix[GQ][:, kc, hc], rhs=hT[:, kc, 0:N],
                                   start=(kc == 0), stop=(kc == 7))
                return ins
            P.add('pe', mm_q, [hT_key, ('wmix', GQ)], [kq_])

        @step
        def s_sg():
            if cload:
                return
            sg, ksg = t_sg.next()
            qs, kqs = t_qs.next()
            xc, kxc = t_xc.next()
            V.update(qs=qs, kqs=kqs, xc=xc, kxc=kxc)
            ps_q, ext = V['ps_q'], V['ext']
            P.add('act', lambda e: e.activation(out=sg[:, 0:N], in_=ps_q[:, 0:N], func=AF.Sigmoid),
                  [V['kq_']], [ksg])
            P.add('dve', lambda e: e.tensor_scalar(
                out=xc[:, 0:N], in0=ext[:, 0:N], scalar1=col(pp, b0 + 0), scalar2=col(pp, b0 + 4),
                op0=ALU.mult, op1=ALU.add), [V['kext'], 'pp'], [kxc])
            P.add('dve', lambda e: e.tensor_tensor(out=qs[:, 0:N], in0=sg[:, 0:N], in1=ps_q[:, 0:N], op=ALU.mult),
                  [ksg, V['kq_']], [kqs])

        @step
        def s_f():
            ps_f, kf_ = gen_ring.next()
            V.update(ps_f=ps_f, kf_=kf_)

            def mm_f(e):
                ins = None
                for kc in range(8):
                    ins = e.matmul(ps_f[:, 0:N], lhsT=wmix[GF][:, kc, hc], rhs=hT[:, kc, 0:N],
                                   start=(kc == 0), stop=(kc == 7))
                return ins
            P.add('pe', mm_f, [hT_key, ('wmix', GF)], [kf_])

        @step
        def s_sf():
            f_, kff = t_f.next()
            V.update(f_=f_, kff=kff)
            ps_f = V['ps_f']
            P.add('act', lambda e: e.activation(out=f_[:, 0:N], in_=ps_f[:, 0:N], func=AF.Sigmoid),
                  [V['kf_']], [kff])
            if cload:
                return
            ext, xc, kxc = V['ext'], V['xc'], V['kxc']
            P.add('dve', lambda e: e.scalar_tensor_tensor(
                out=xc[:, 0:N], in0=ext[:, 1:1 + N], scalar=col(pp, b0 + 1), in1=xc[:, 0:N],
                op0=ALU.mult, op1=ALU.add), [V['kext'], kxc, 'pp'], [kxc])

        @step
        def s_v():
            if cload:
                return
            ps_v, kv_ = gen_ring.next()
            V.update(ps_v=ps_v, kv_=kv_)

            def mm_v(e):
                ins = None
                for s in range(nsub):
                    npk = min(128, N - 128 * s)
                    for kc in range(8):
                        ins = e.matmul(ps_v[:npk, s * 128:(s + 1) * 128], lhsT=hT[:, kc, 128 * s:128 * s + npk],
                                       rhs=wmix[GV][:, kc, hc], start=(kc == 0), stop=(kc == 7))
                return ins
            P.add('pe', mm_v, [hT_key, ('wmix', GV)], [kv_])

        @step
        def s_vb():
            f_, kff = V['f_'], V['kff']
            P.add('dve', lambda e: e.tensor_scalar(
                out=f_[:, 0:N], in0=f_[:, 0:N], scalar1=col(dv, DV_OML + dirn * 8 + h),
                scalar2=col(dv, DV_LB + dirn * 8 + h), op0=ALU.mult, op1=ALU.add), [kff, 'dv'], [kff])
            if cload:
                return
            vb, kvb = t_vb.next()
            V.update(vb=vb, kvb=kvb)
            ps_v, ext, xc, kxc = V['ps_v'], V['ext'], V['xc'], V['kxc']
            P.add('act', lambda e: e.activation(out=vb[:npv, 0:nsub * 128], in_=ps_v[:npv, 0:nsub * 128],
                                                func=AF.Copy), [V['kv_']], [kvb])
            P.add('dve', lambda e: e.scalar_tensor_tensor(
                out=xc[:, 0:N], in0=ext[:, 2:2 + N], scalar=col(pp, b0 + 2), in1=xc[:, 0:N],
                op0=ALU.mult, op1=ALU.add), [V['kext'], kxc, 'pp'], [kxc])

        @step
        def s_conv3():
            if cload:
                return
            ext, xc, kxc = V['ext'], V['xc'], V['kxc']
            xcb, kxcb = t_xcb.next()
            V.update(xcb=xcb, kxcb=kxcb)
            P.add('dve', lambda e: e.scalar_tensor_tensor(
                out=xc[:, 0:N], in0=ext[:, 3:3 + N], scalar=col(pp, b0 + 3), in1=xc[:, 0:N],
                op0=ALU.mult, op1=ALU.add), [V['kext'], kxc, 'pp'], [kxc])
            P.add('dve', lambda e: e.tensor_copy(out=xcb[:, 0:N], in_=xc[:, 0:N]), [kxc], [kxcb])
            if cstore:
                dma('sp', xcb_scr[h, :, pr0:pr0 + N], xcb[:, 0:N], [kxcb], [])
                dma('sp', xc_scr[h, :, pr0:pr0 + N], xc[:, 0:N], [kxc], [])
                dma('sp', qs_scr[h, :, pr0:pr0 + N], V['qs'][:, 0:N], [V['kqs']], [])
                dma('sp', vb_scr[h, pr0 // TN], V['vb'][:, 0:N], [V['kvb']], [])

        @step
        def s_gr():
            ps_r, kr_ = gen_ring.next()
            V.update(ps_r=ps_r, kr_=kr_)
            xcb = V['xcb']
            P.add('pe', lambda e: e.matmul(ps_r[:, 0:N], lhsT=wg[:, 0, h, :], rhs=xcb[:, 0:N],
                                           start=True, stop=True), [V['kxcb'], 'wg'], [kr_])

        @step
        def s_r():
            r_, krr = t_r.next()
            V.update(r_=r_, krr=krr)
            ps_r = V['ps_r']
            P.add('act', lambda e: e.activation(out=r_[:, 0:N], in_=ps_r[:, 0:N], func=AF.Sigmoid,
                                                bias=col(pp, b0 + 5 + dirn)), [V['kr_'], 'pp'], [krr])

        @step
        def s_gi():
            ps_i, ki_ = gen_ring.next()
            V.update(ps_i=ps_i, ki_=ki_)
            xcb = V['xcb']
            P.add('pe', lambda e: e.matmul(ps_i[:, 0:N], lhsT=wg[:, 1, h, :], rhs=xcb[:, 0:N],
                                           start=True, stop=True), [V['kxcb'], 'wg'], [ki_])

        @step
        def s_i():
            i_, kii = t_i.next()
            V.update(i_=i_, kii=kii)
            ps_i = V['ps_i']
            P.add('act', lambda e: e.activation(out=i_[:, 0:N], in_=ps_i[:, 0:N], func=AF.Sigmoid,
                                                bias=col(pp, b0 + 7 + dirn)), [V['ki_'], 'pp'], [kii])

        @step
        def s_g():
            g_, kgg = t_g.next()
            b_, kbb = t_b.next()
            V.update(b_=b_, kbb=kbb)
            f_ = V['f_']
            P.add('act', lambda e: e.activation(out=g_[:, 0:N], in_=f_[:, 0:N], func=AF.Ln), [V['kff']], [kgg])
            P.add('dve', lambda e: e.tensor_tensor_scan(
                out=R(b_[:, 0:N]), data0=R(msk[:, 0:N]), data1=R(g_[:, 0:N]), initial=0.0,
                op0=ALU.mult, op1=ALU.add), [kgg, mkey], [kbb])

        @step
        def s_a2():
            a2, ka2 = t_a2.next()
            V.update(a2=a2, ka2=ka2)
            r_, i_, xc = V['r_'], V['i_'], V['xc']
            P.add('act', lambda e: e.activation(out=a2[:, 0:N], in_=r_[:, 0:N], func=AF.Exp,
                                                scale=col(dv, DV_SP2 + dirn * 8 + h)), [V['krr'], 'dv'], [ka2])
            P.add('dve', lambda e: e.tensor_tensor(out=i_[:, 0:N], in0=i_[:, 0:N], in1=xc[:, 0:N], op=ALU.mult),
                  [V['kii'], V['kxc']], [V['kii']])

        @step
        def s_abs():
            a2, ka2 = V['a2'], V['ka2']
            P.add('act', lambda e: e.activation(out=a2[:, 0:N], in_=a2[:, 0:N], func=AF.Abs, scale=-1.0, bias=1.0),
                  [ka2], [ka2])

        @step
        def s_eb():
            eb, keb = t_eb.next()
            V.update(eb=eb, keb=keb)
            b_ = V['b_']
            P.add('act', lambda e: e.activation(out=eb[:, 0:N], in_=b_[:, 0:N], func=AF.Exp), [V['kbb']], [keb])

        @step
        def s_ln():
            a2, ka2 = V['a2'], V['ka2']
            P.add('act', lambda e: e.activation(out=a2[:, 0:N], in_=a2[:, 0:N], func=AF.Ln, bias=col(dv, DV_TINY)),
                  [ka2, 'dv'], [ka2])

        @step
        def s_enb():
            enb, kenb = t_enb.next()
            qt, kqt = t_qt.next()
            V.update(enb=enb, kenb=kenb, qt=qt, kqt=kqt)
            b_, qs, eb = V['b_'], V['qs'], V['eb']
            P.add('act', lambda e: e.activation(out=enb[:, 0:N], in_=b_[:, 0:N], func=AF.Exp, scale=-1.0),
                  [V['kbb']], [kenb])
            P.add('pool', lambda e: e.tensor_tensor(out=qt[:, 0:N], in0=qs[:, 0:N], in1=eb[:, 0:N], op=ALU.mult),
                  [V['kqs'], V['keb']], [kqt])

        @step
        def s_sqrt():
            a2, ka2 = V['a2'], V['ka2']
            P.add('act', lambda e: e.activation(out=a2[:, 0:N], in_=a2[:, 0:N], func=AF.Exp, scale=0.5),
                  [ka2], [ka2])

        @step
        def s_a():
            a_, kaa = t_a.next()
            kt, kkt = t_kt.next()
            V.update(a_=a_, kaa=kaa, kt=kt, kkt=kkt)
            r_, f_, enb = V['r_'], V['f_'], V['enb']
            P.add('act', lambda e: e.activation(out=a_[:, 0:N], in_=r_[:, 0:N], func=AF.Exp,
                                                scale=col(dv, DV_SP + dirn * 8 + h)), [V['krr'], 'dv'], [kaa])
            P.add('dve', lambda e: e.scalar_tensor_tensor(
                out=kt[:, 0:N], in0=f_[:, 0:N], scalar=1.0, in1=enb[:, 0:N], op0=ALU.subtract, op1=ALU.mult),
                [V['kff'], V['kenb']], [kkt])

        @step
        def s_u():
            u_, kuu = t_u.next()
            V.update(u_=u_, kuu=kuu)
            a2, i_ = V['a2'], V['i_']
            P.add('dve', lambda e: e.tensor_tensor(out=u_[:, 0:N], in0=a2[:, 0:N], in1=i_[:, 0:N], op=ALU.mult),
                  [V['ka2'], V['kii']], [kuu])

        @step
        def s_kh():
            kh, kkh = t_kh.next()
            kt, eb = V['kt'], V['eb']
            eb3 = eb[:, 0:N].rearrange("p (c t) -> p c t", t=CL)
            P.add('dve', lambda e: e.tensor_tensor(
                out=kh[:, 0:N].rearrange("p (c t) -> p c t", t=CL),
                in0=kt[:, 0:N].rearrange("p (c t) -> p c t", t=CL),
                in1=eb3[:, :, lc:lc + 1].broadcast_to([128, nch, CL]), op=ALU.mult), [V['kkt'], V['keb']], [kkh])
            st.update(qt=V['qt'], kqt=V['kqt'], kt=kt, kkt=V['kkt'], kh=kh, kkh=kkh, vb=V['vb'], kvb=V['kvb'],
                      eb=eb, keb=V['keb'])

        @step
        def s_scan():
            hh, khh = t_h.next()
            V.update(hh=hh, khh=khh)
            a_, u_ = V['a_'], V['u_']
            P.add('dve', lambda e: e.tensor_tensor_scan(
                out=R(hh[:, 0:N]), data0=R(a_[:, 0:N]), data1=R(u_[:, 0:N]), initial=col(carryA, h),
                op0=ALU.mult, op1=ALU.add), [V['kaa'], V['kuu'], ('carryA', h)], [khh])
            P.add('pool', lambda e: e.tensor_copy(out=col(carryA, h), in_=hh[:, lastc:lastc + 1]),
                  [khh], [('carryA', h)])

        @step
        def s_out():
            hh, khh = V['hh'], V['khh']
            if dirn == 1:
                dma('sp', hb_scr[h, :, pr0:pr0 + N], hh[:, 0:N], [khh], [])
            elif combine:
                hbl, khbl = V['hbl'], V['khbl']
                P.add('pool', lambda e: e.tensor_tensor(out=hh[:, 0:N], in0=hh[:, 0:N], in1=hbl[:, 0:N], op=ALU.add),
                      [khh, khbl], [khh])
                dma('sp', hs_scr[h, :, pr0:pr0 + N], hh[:, 0:N], [khh], [])
        return steps

    def stage1_pair(dirn, p0, N, hT, hT_key, meta, h0, sts):
        sa = stage1_steps(dirn, p0, N, hT, hT_key, meta, h0, sts[0])
        sb = stage1_steps(dirn, p0, N, hT, hT_key, meta, h0 + 1, sts[1])
        for fa, fb in zip(sa, sb):
            fa()
            yield
            fb()
            yield

    def stage2(dirn, p0, N, meta, h, st, chain):
        rv = (dirn == 1)
        nsub = (N + 127) // 128
        CL = min(64, N)
        pr0 = p0 - NMETA
        combine = (dirn == 0) and not meta
        lc = 0 if rv else CL - 1
        qt, kqt, kt, kkt, kh, kkh = st['qt'], st['kqt'], st['kt'], st['kkt'], st['kh'], st['kkh']
        vb, kvb, eb, keb = st['vb'], st['kvb'], st['eb'], st['keb']
        ps_o, kpo = ch_o[chain]
        mbank, kmb = ch_m[chain]
        if combine:
            ob, kob = t_ob[chain].next()
            dma('sp', ob[:, 0:N], ob_scr[h, :, pr0:pr0 + N], [], [kob])
        msc = mscR if rv else mscF
        msck = 'mscR' if rv else 'mscF'
        sub_order = range(nsub - 1, -1, -1) if rv else range(nsub)
        for s in sub_order:
            npk = min(128, N - 128 * s)
            t0 = 128 * s
            ps_kT, kkT = mbank[:, 256:320].bitcast(BF16), kmb
            P.add('pe', lambda e, ps_kT=ps_kT, t0=t0, npk=npk: e.transpose(
                out=ps_kT[:npk, 0:128], in_=kh[:, t0:t0 + npk], identity=ident), [kkh, 'ident'], [kkT])
            khT, kkhT = khT4[chain][s], ('khT4', chain, s)
            P.add('act', lambda e, khT=khT, ps_kT=ps_kT, npk=npk: e.activation(
                out=khT[:npk, :], in_=ps_kT[:npk, 0:128], func=AF.Copy), [kkT], [kkhT])
            ps_sc, ksc = mbank[:, 0:128], kmb
            P.add('pe', lambda e, ps_sc=ps_sc, t0=t0, npk=npk: e.matmul(
                ps_sc[:npk, 0:npk], lhsT=kt[:, t0:t0 + npk], rhs=qt[:, t0:t0 + npk], start=True, stop=True),
                [kkt, kqt], [ksc])
            yield
            PT, kPT = PT4[chain][s], ('PT4', chain, s)
            P.add('dve', lambda e, PT=PT, ps_sc=ps_sc, npk=npk: e.tensor_tensor(
                out=PT[:npk, 0:npk], in0=ps_sc[:npk, 0:npk], in1=msc[:npk, 0:npk], op=ALU.mult),
                [ksc, msck], [kPT])
            yield
            ncs = npk // CL
            ch_order = range(ncs - 1, -1, -1) if rv else range(ncs)
            for c in ch_order:
                c0 = c * CL

                def mm_o(e, PT=PT, s=s, t0=t0, c0=c0, npk=npk):
                    e.matmul(ps_o[:, t0 + c0:t0 + c0 + CL], lhsT=vb[:npk, s * 128:(s + 1) * 128],
                             rhs=PT[:npk, c0:c0 + CL], start=True, stop=False)
                    return e.matmul(ps_o[:, t0 + c0:t0 + c0 + CL], lhsT=S_b[:, h, :],
                                    rhs=qt[:, t0 + c0:t0 + c0 + CL], start=False, stop=True)
                P.add('pe', mm_o, [kPT, kvb, kqt, ('Sb', h)], [kpo])
                ps_dS, kdS = mbank[:, 128:256], kmb
                P.add('pe', lambda e, ps_dS=ps_dS, khT=khT, s=s, c0=c0: e.matmul(
                    ps_dS[:, 0:128], lhsT=khT[c0:c0 + CL, :], rhs=vb[c0:c0 + CL, s * 128:(s + 1) * 128],
                    start=True, stop=True), [kkhT, kvb], [kdS])
                yield
                dcol = t0 + c0 + lc
                P.add('dve', lambda e, ps_dS=ps_dS, dcol=dcol: e.scalar_tensor_tensor(
                    out=S_b[:, h, :], in0=S_f[:, h, :], scalar=eb[:, dcol:dcol + 1], in1=ps_dS[:, 0:128],
                    op0=ALU.mult, op1=ALU.subtract), [kdS, keb, ('Sf', h)], [('Sb', h)])
                P.add('dve', lambda e, ps_dS=ps_dS, dcol=dcol: e.scalar_tensor_tensor(
                    out=S_f[:, h, :], in0=S_f[:, h, :], scalar=eb[:, dcol:dcol + 1], in1=ps_dS[:, 0:128],
                    op0=ALU.mult, op1=ALU.subtract), [kdS, keb, ('Sf', h)], [('Sf', h)])
                yield
        if dirn == 1:
            ob, kob = t_ob[chain].next()
            P.add('act', lambda e: e.activation(out=ob[:, 0:N], in_=ps_o[:, 0:N], func=AF.Copy), [kpo], [kob])
            dma('sp', ob_scr[h, :, pr0:pr0 + N], ob[:, 0:N], [kob], [])
            yield
        elif combine:
            osm, kos = t_os[chain].next()
            P.add('dve', lambda e: e.tensor_tensor(out=osm[:, 0:N], in0=ps_o[:, 0:N], in1=ob[:, 0:N], op=ALU.add),
                  [kpo, kob], [kos])
            yield
            osq, kosq = t_osq[chain].next()
            P.add('act', lambda e: e.activation(out=osq[:, 0:N], in_=osm[:, 0:N], func=AF.Square), [kos], [kosq])
            yield
            ps_ss, kpss = mbank, kmb
            P.add('pe', lambda e: e.matmul(ps_ss[:, 0:N], lhsT=ones_f, rhs=osq[:, 0:N], start=True, stop=True),
                  [kosq, 'ones'], [kpss])
            yield
            rso, krso = t_rso[chain].next()
            P.add('act', lambda e: e.activation(out=rso[:, 0:N], in_=ps_ss[:, 0:N], func=AF.Ln,
                                                bias=col(dv, DV_EPS128)), [kpss, 'dv'], [krso])
            yield
            P.add('act', lambda e: e.activation(out=rso[:, 0:N], in_=rso[:, 0:N], func=AF.Exp, scale=-0.5),
                  [krso], [krso])
            yield
            P.add('pool', lambda e: e.tensor_tensor(out=osm[:, 0:N], in0=osm[:, 0:N], in1=rso[:, 0:N], op=ALU.mult),
                  [kos, krso], [kos])
            dma('sp', on_scr[h, :, pr0:pr0 + N], osm[:, 0:N], [kos], [])
            yield

    def interleave(*gens):
        gens = [g for g in gens if g is not None]
        while gens:
            for g in list(gens):
                try:
                    next(g)
                except StopIteration:
                    gens.remove(g)

    def stage_x_gen(*a, **k):
        stage_x(*a, **k)
        yield

    def init_states():
        P.add('pool', lambda e: e.memset(carryA, 0.0), [], [('carryA', h) for h in range(8)])
        P.add('pool', lambda e: e.memset(S_f.rearrange("p a b -> p (a b)"), 0.0), [], [('Sf', h) for h in range(8)])
        P.add('pool', lambda e: e.memset(S_b.rearrange("p a b -> p (a b)"), 0.0), [], [('Sb', h) for h in range(8)])

    def run_mixer_pass(dirn, tiles):
        hT, hT_key = hT_bufs[0], ('hT', 0)
        dma('pool', wg.rearrange("p g h j -> p (g h j)"), w_gate[:, dirn * 2048:(dirn + 1) * 2048], [], ['wg'])
        dma('pool', wmix[GF].rearrange("p a b -> p (a b)"), w_mix[3 if dirn == 1 else 2], [], [('wmix', GF)])

        def do_x(p0, N):
            if dirn == 0 and p0 > 0:
                dma('sp', hT[:, :, 0:N], hT_scr[(p0 - NMETA) // TN], [], [hT_key])
                return
            nsub = (N + 127) // 128
            xl = [xt_ring.next() for _ in range(nsub)]
            pb, pbk = gen_ring.next()
            stage_x(p0, N, xl, hT, hT_key, PP_GMIX, True, pb, pbk)
            if dirn == 1:
                dma('sp', hT_scr[(p0 - NMETA) // TN], hT[:, :, 0:N], [hT_key], [])

        def s1_pair(p0, N, meta, h0, sts, with_x):
            if with_x:
                do_x(p0, N)
                yield
            yield from stage1_pair(dirn, p0, N, hT, hT_key, meta, h0, sts)
        p0, N, meta = tiles[0]
        sts = [{}, {}]
        interleave(s1_pair(p0, N, meta, 0, sts, True))
        for ti, (p0, N, meta) in enumerate(tiles):
            for pr in range(4):
                sts_next = [{}, {}]
                if pr < 3:
                    nxt = s1_pair(p0, N, meta, 2 * pr + 2, sts_next, False)
                elif ti + 1 < len(tiles):
                    nxt = s1_pair(*tiles[ti + 1], 0, sts_next, True)
                else:
                    nxt = None
                interleave(stage2(dirn, p0, N, meta, 2 * pr, sts[0], 0),
                           stage2(dirn, p0, N, meta, 2 * pr + 1, sts[1], 1), nxt)
                sts = sts_next

    init_states()
    if stop_after >= 1:
        run_mixer_pass(1, [(NMETA + ti * TN, TN, False) for ti in range(NT - 1, -1, -1)])
    P.barrier()
    init_states()
    if stop_after >= 2:
        run_mixer_pass(0, [(0, NMETA, True)] + [(NMETA + ti * TN, TN, False) for ti in range(NT)])
    P.barrier()

    A.reset(base_mark)
    wout_sb = A.alloc(8 * 1024, BF16, shape3=1024)
    gfin = A.alloc(1024, F32)
    dma('sp', gfin, gfin_d, [], ['gfin'])
    P.add('pool', lambda e: e.tensor_scalar(out=gfin, in0=gfin, scalar1=32.0, scalar2=None, op0=ALU.mult),
          ['gfin'], ['gfin'])
    dma('pool', wout_sb.rearrange("p a b -> p (a b)"), w_out, [], ['wout'])
    xt4 = [A.alloc(1024, F32) for _ in range(4)]
    xn_ring = Ring([A.alloc(1024, BF16) for _ in range(2)], 'xn2')
    hT2 = A.alloc(8 * TN, BF16, shape3=TN)
    hs_ring = tmp(name='hsl')
    on_ring = tmp(name='onl')
    braT = A.alloc(8 * TN, BF16, shape3=TN)
    brbT = A.alloc(8 * TN, BF16, shape3=TN)
    mrgT = A.alloc(8 * TN, BF16, shape3=TN)
    h2T = A.alloc(8 * TN, BF16, shape3=TN)
    actT = A.alloc(32 * TN, BF16, shape3=TN)
    wring = Ring([A.alloc(1024, BF16, shape3=128) for _ in range(8)], 'wch')
    w2ring = Ring([A.alloc(512, BF16) for _ in range(12)], 'w2ch')
    t_e = tmp(name='e')
    t_t = tmp(cnt=1, name='t')
    t_so = tmp(name='so')
    t_t2 = tmp(cnt=1, name='t2')
    t_sga = tmp(name='sga')
    t_sgb = tmp(name='sgb')
    t_m1 = tmp(cnt=1, name='m1')
    t_m2 = tmp(cnt=1, name='m2')
    t_rl = tmp(dt=BF16, name='rl')
    gen2 = Ring(banks[0:4], 'psg2')
    acc_keys = [('psacc', i) for i in range(4)]
    acc = banks[4:8]

    def wchunk(cid):
        w, wk = wring.next()
        dma('sp', w.rearrange("p a b -> p (a b)"), wf2_bf[cid], [('wf2', cid // 8 * 8)], [wk])
        return w, wk

    def proj(w, wk, src, src_key, ps, psk):
        def mm(e):
            ins = None
            for kc in range(8):
                ins = e.matmul(ps[:, 0:TN], lhsT=w[:, kc, :], rhs=src[:, kc, 0:TN], start=(kc == 0), stop=(kc == 7))
            return ins
        P.add('pe', mm, [wk, src_key], [psk])

    for ti in range(NT if stop_after >= 3 else 0):
        p0 = NMETA + ti * TN
        pr0 = ti * TN
        dma('sp', hT2, hT_scr[ti], [], ['hT2'])
        for s_ in range(4):
            dma('sp', xt4[s_], xs[p0 + 128 * s_:p0 + 128 * (s_ + 1), :], [], [('xt4', s_)])
        for j in range(8):
            w, wk = wchunk(0 + j)
            ps, psk = gen2.next()
            proj(w, wk, hT2, 'hT2', ps, psk)
            e_, ke = t_e.next()
            P.add('act', lambda e, e_=e_, ps=ps: e.activation(out=e_, in_=ps[:, 0:TN], func=AF.Gelu), [psk], [ke])
            hsl, khsl = hs_ring.next()
            dma('sp', hsl, hs_scr[j, :, pr0:pr0 + TN], [], [khsl])
            P.add('pool', lambda e, e_=e_, j=j, hsl=hsl: e.tensor_tensor(
                out=braT[:, j, :], in0=hsl, in1=e_, op=ALU.mult), [khsl, ke], [('braT', j)])
        for j in range(8):
            w, wk = wchunk(8 + j)
            ps, psk = gen2.next()
            proj(w, wk, hT2, 'hT2', ps, psk)
            so, kso = t_so.next()
            P.add('act', lambda e, so=so, ps=ps: e.activation(out=so, in_=ps[:, 0:TN], func=AF.Sigmoid),
                  [psk], [kso])
            t2, kt2 = t_t2.next()
            P.add('dve', lambda e, t2=t2, so=so, ps=ps, j=j: e.scalar_tensor_tensor(
                out=t2, in0=so, scalar=col(dv, DV_HGG + j), in1=ps[:, 0:TN], op0=ALU.mult, op1=ALU.mult),
                [kso, psk, 'dv'], [kt2])
            onl, konl = on_ring.next()
            dma('sp', onl, on_scr[j, :, pr0:pr0 + TN], [], [konl])
            P.add('pool', lambda e, t2=t2, j=j, onl=onl: e.tensor_tensor(
                out=brbT[:, j, :], in0=onl, in1=t2, op=ALU.mult), [konl, kt2], [('brbT', j)])
        bra_keys = [('braT', j) for j in range(8)]
        brb_keys = [('brbT', j) for j in range(8)]
        for j in range(8):
            w, wk = wchunk(16 + j)
            ps_ga, kga = gen2.next()
            proj(w, wk, hT2, 'hT2', ps_ga, kga)
            sga, ksga = t_sga.next()
            P.add('act', lambda e, sga=sga, ps_ga=ps_ga: e.activation(out=sga, in_=ps_ga[:, 0:TN], func=AF.Sigmoid),
                  [kga], [ksga])
            w, wk = wchunk(24 + j)
            ps_gb, kgb = gen2.next()
            proj(w, wk, hT2, 'hT2', ps_gb, kgb)
            sgb, ksgb = t_sgb.next()
            P.add('act', lambda e, sgb=sgb, ps_gb=ps_gb: e.activation(out=sgb, in_=ps_gb[:, 0:TN], func=AF.Sigmoid),
                  [kgb], [ksgb])
            w, wk = wchunk(32 + j)
            ps_pa, kpa = gen2.next()

            def mm_pa(e, w=w, ps_pa=ps_pa):
                ins = None
                for kc in range(8):
                    ins = e.matmul(ps_pa[:, 0:TN], lhsT=w[:, kc, :], rhs=braT[:, kc, :], start=(kc == 0), stop=(kc == 7))
                return ins
            P.add('pe', mm_pa, [wk] + bra_keys, [kpa])
            m1, km1 = t_m1.next()
            P.add('dve', lambda e, m1=m1, ps_pa=ps_pa, sga=sga: e.tensor_tensor(
                out=m1, in0=ps_pa[:, 0:TN], in1=sga, op=ALU.mult), [kpa, ksga], [km1])
            w, wk = wchunk(40 + j)
            ps_pb, kpb = gen2.next()

            def mm_pb(e, w=w, ps_pb=ps_pb):
                ins = None
                for kc in range(8):
                    ins = e.matmul(ps_pb[:, 0:TN], lhsT=w[:, kc, :], rhs=brbT[:, kc, :], start=(kc == 0), stop=(kc == 7))
                return ins
            P.add('pe', mm_pb, [wk] + brb_keys, [kpb])
            m2, km2 = t_m2.next()
            P.add('dve', lambda e, m2=m2, ps_pb=ps_pb, sgb=sgb: e.tensor_tensor(
                out=m2, in0=ps_pb[:, 0:TN], in1=sgb, op=ALU.mult), [kpb, ksgb], [km2])
            P.add('pool', lambda e, m1=m1, m2=m2, j=j: e.tensor_tensor(
                out=mrgT[:, j, :], in0=m1, in1=m2, op=ALU.add), [km1, km2], [('mrgT', j)])
        mrg_keys = [('mrgT', j) for j in range(8)]
        for s in range(4):
            for hf in range(2):
                pa_, pak = acc[(s * 2 + hf) % 4], acc_keys[(s * 2 + hf) % 4]

                def mm_wo(e, s=s, hf=hf, pa_=pa_):
                    ins = None
                    for kc in range(8):
                        ins = e.matmul(pa_[:, 0:512], lhsT=mrgT[:, kc, s * 128:(s + 1) * 128],
                                       rhs=wout_sb[:, kc, hf * 512:(hf + 1) * 512], start=(kc == 0), stop=(kc == 7))
                    return ins
                P.add('pe', mm_wo, mrg_keys + ['wout'], [pak])
                P.add('dve', lambda e, s=s, hf=hf, pa_=pa_: e.tensor_tensor(
                    out=xt4[s][:, hf * 512:(hf + 1) * 512], in0=pa_[:, 0:512],
                    in1=xt4[s][:, hf * 512:(hf + 1) * 512], op=ALU.add), [pak, ('xt4', s)], [('xt4', s)])
        for s in range(4):
            xt, xk = xt4[s], ('xt4', s)
            P.add('pool', lambda e, s=s: e.memset(col(ss, s), 0.0), [], [('ss', s)])
            P.add('act', lambda e, xt=xt, s=s: e.activation(out=junk, in_=xt, func=AF.Square,
                                                            accum_out=ss[:, s:s + 1]), [xk, ('ss', s)],
                  ['junk', ('ss', s)])
            P.add('act', lambda e, s=s: e.activation(out=rs[:, s:s + 1], in_=ss[:, s:s + 1], func=AF.Ln, bias=dv[:, DV_EPSD:DV_EPSD + 1]), [('ss', s), 'dv'], [('rs', s)])
            P.add('act', lambda e, s=s: e.activation(out=rs[:, s:s + 1], in_=rs[:, s:s + 1], func=AF.Exp, scale=-0.5), [('rs', s)], [('rs', s)])
            xn, xnk = xn_ring.next()
            P.add('dve', lambda e, xt=xt, xn=xn, s=s: e.tensor_scalar(
                out=xn, in0=xt, scalar1=rs[:, s:s + 1], scalar2=32.0, op0=ALU.mult, op1=ALU.mult),
                [xk, ('rs', s)], [xnk])
            pb, pbk = gen2.next()
            psT = pb.bitcast(BF16).rearrange("p (a b) -> p a b", b=128)

            def tr2(e, xn=xn, psT=psT):
                ins = None
                for kc in range(8):
                    ins = e.transpose(out=psT[:, kc, :], in_=xn[:, kc * 128:(kc + 1) * 128], identity=ident)
                return ins
            P.add('pe', tr2, [xnk, 'ident'], [pbk])
            P.add('dve', lambda e, psT=psT, s=s: e.tensor_tensor(
                out=h2T[:, :, s * 128:(s + 1) * 128], in0=psT,
                in1=pp[:, PP_GMLP:PP_GMLP + 8].unsqueeze(2).broadcast_to([128, 8, 128]), op=ALU.mult),
                [pbk, 'pp'], ['h2T'])
        for m in range(32):
            w, wk = wchunk(48 + m)
            ps, psk = gen2.next()
            proj(w, wk, h2T, 'h2T', ps, psk)
            rl, krl = t_rl.next()
            P.add('act', lambda e, rl=rl, ps=ps: e.activation(out=rl, in_=ps[:, 0:TN], func=AF.Relu), [psk], [krl])
            P.add('dve' if m % 2 == 0 else 'pool', lambda e, rl=rl, m=m: e.tensor_tensor(
                out=actT[:, m, :], in0=rl, in1=rl, op=ALU.mult), [krl], [('actT', m)])
        for hf in range(2):
            for m in range(32):
                w2c, w2k = w2ring.next()
                dma('sp', w2c, w2_bf[m][:, hf * 512:(hf + 1) * 512], [('w2s', m // 8 * 8)], [w2k])

                def mm_2(e, m=m, hf=hf, w2c=w2c):
                    ins = None
                    for s in range(4):
                        ins = e.matmul(acc[s][:, 0:512], lhsT=actT[:, m, s * 128:(s + 1) * 128],
                                       rhs=w2c, start=(m == 0), stop=(m == 31))
                    return ins
                P.add('pe', mm_2, [w2k, ('actT', m)], acc_keys)
            for s in range(4):
                P.add('dve', lambda e, s=s, hf=hf: e.tensor_tensor(
                    out=xt4[s][:, hf * 512:(hf + 1) * 512], in0=acc[s][:, 0:512],
                    in1=xt4[s][:, hf * 512:(hf + 1) * 512], op=ALU.add), [acc_keys[s], ('xt4', s)], [('xt4', s)])
        for s in range(4):
            xt, xk = xt4[s], ('xt4', s)
            P.add('pool', lambda e, s=s: e.memset(col(ss, 4 + s % 2), 0.0), [], [('ss', 4 + s % 2)])
            P.add('act', lambda e, xt=xt, s=s: e.activation(out=junk, in_=xt, func=AF.Square,
                                                            accum_out=ss[:, 4 + s % 2:5 + s % 2]),
                  [xk, ('ss', 4 + s % 2)], ['junk', ('ss', 4 + s % 2)])
            P.add('act', lambda e, s=s: e.activation(out=rs[:, 4 + s % 2:5 + s % 2], in_=ss[:, 4 + s % 2:5 + s % 2], func=AF.Ln, bias=dv[:, DV_EPSD:DV_EPSD + 1]), [('ss', 4 + s % 2), 'dv'], [('rs', 4 + s % 2)])
            P.add('act', lambda e, s=s: e.activation(out=rs[:, 4 + s % 2:5 + s % 2], in_=rs[:, 4 + s % 2:5 + s % 2], func=AF.Exp, scale=-0.5), [('rs', 4 + s % 2)], [('rs', 4 + s % 2)])
            P.add('dve', lambda e, xt=xt, s=s: e.scalar_tensor_tensor(
                out=xt, in0=xt, scalar=rs[:, 4 + s % 2:5 + s % 2], in1=gfin, op0=ALU.mult, op1=ALU.mult),
                [xk, ('rs', 4 + s % 2), 'gfin'], [xk])
            dma('sp', y_d[pr0 + s * 128:pr0 + (s + 1) * 128, :], xt, [xk], [('yout', ti, s)])
    P.barrier()

    with (nc.semaphore("s_pe") as s_pe, nc.semaphore("s_act") as s_act, nc.semaphore("s_dve") as s_dve,
          nc.semaphore("s_pool") as s_pool):
        import contextlib
        with contextlib.ExitStack() as st:
            dsems = dict(sp=[st.enter_context(nc.semaphore(f"s_dma{i}")) for i in range(NDSEM)],
                         pool=[st.enter_context(nc.semaphore(f"s_dmap{i}")) for i in range(8)])
            sems = dict(pe=s_pe, act=s_act, dve=s_dve, pool=s_pool)
            P.finalize(nc, sems, dsems)
            with nc.Block() as block:
                @block.sync
                def _(e):
                    P.emit('sp', e)

                @block.tensor
                def _(e):
                    P.emit('pe', e)

                @block.scalar
                def _(e):
                    P.emit('act', e)

                @block.vector
                def _(e):
                    P.emit('dve', e)

                @block.gpsimd
                def _(e):
                    P.emit('pool', e)
    return nc


def _pack_params(conv_w, conv_b, rg_ba, rg_bx, rg_lambda, hg_lb_logits, hg_norm_g, norm_mix_g, norm_mlp_g):
    pp = np.zeros((128, PP_N), np.float32)
    for h in range(8):
        sl = slice(h * 128, (h + 1) * 128)
        b0 = h * PP_HEAD
        for j in range(4):
            pp[:, b0 + j] = conv_w[0, j, sl]
        pp[:, b0 + 4] = conv_b[0, sl]
        for dr in range(2):
            pp[:, b0 + 5 + dr] = rg_ba[0, dr, sl]
            pp[:, b0 + 7 + dr] = rg_bx[0, dr, sl]
            pp[:, b0 + 9 + dr] = rg_lambda[0, dr, sl]
            pp[:, b0 + 11 + dr] = hg_lb_logits[0, dr, sl]
            pp[:, b0 + 13 + dr] = hg_lb_logits[1, dr, sl]
        pp[:, b0 + 15] = hg_norm_g[0, sl]
    for kc in range(8):
        pp[:, PP_GMIX + kc] = norm_mix_g[0, kc * 128:(kc + 1) * 128]
        pp[:, PP_GMLP + kc] = norm_mlp_g[0, kc * 128:(kc + 1) * 128]
    return pp


def _consts():
    c = np.zeros((128, 128 * 4 + 1024), np.float32)
    c[:, 0:128] = np.eye(128, dtype=np.float32)
    c[:, 128:256] = 1.0
    s = np.arange(128)[:, None]
    t = np.arange(128)[None, :]
    same = (s // 64) == (t // 64)
    c[:, 256:384] = -1.0 * (same & (s <= t))
    c[:, 384:512] = -1.0 * (same & (s >= t))
    mF = np.ones(512, np.float32)
    mF[0::64] = 0.0
    mR = np.ones(512, np.float32)
    mR[63::64] = 0.0
    c[:, 512:1024] = mF[None, :]
    c[:, 1024:1536] = mR[None, :]
    return c


def _kc_layout(w):
    return np.ascontiguousarray(w.reshape(8, 128, -1).transpose(1, 0, 2))


def kernel(x_prompt, x_sample, meta_tokens, hg_lb_logits, norm_mix_g, w_in, conv_w, conv_b, rg_wa, rg_ba,
           rg_wx, rg_bx, rg_lambda, hg_norm_g, w_branch_a, w_branch_b, w_out, norm_mlp_g, w_mlp1, w_mlp2,
           final_norm_g):
    f = lambda a: np.asarray(a, dtype=np.float32)
    x_prompt, x_sample, meta_tokens = f(x_prompt), f(x_sample), f(meta_tokens)
    T = x_prompt.shape[1]
    NT = T // TN
    seqs = [x_prompt[i] for i in range(x_prompt.shape[0])] + [x_sample[i] for i in range(x_sample.shape[0])]
    assert len(seqs) <= NCORES
    win = f(w_in)[0]
    grp = lambda g: win[:, g * 1024:(g + 1) * 1024]
    w_mix = np.stack([_kc_layout(grp(g)).reshape(128, 8 * 1024) for g in (0, 2, 3, 4, 5)])
    wa, wx = f(rg_wa)[0], f(rg_wx)[0]
    wgate = np.stack([wa[0], wx[0], wa[1], wx[1]])
    wgate = np.ascontiguousarray(wgate.transpose(2, 0, 1, 3)).reshape(128, 4 * 8 * 128)

    def chunks(w):
        k = _kc_layout(w)
        C = k.shape[2]
        return np.ascontiguousarray(k.reshape(128, 8, C // 128, 128).transpose(2, 0, 1, 3)).reshape(C // 128, 128, 1024)
    w_f2 = np.concatenate([chunks(grp(1)), chunks(grp(6)), chunks(grp(7)), chunks(grp(8)),
                           chunks(f(w_branch_a)[0]), chunks(f(w_branch_b)[0]), chunks(f(w_mlp1)[0])], axis=0)
    wout_l = _kc_layout(f(w_out)[0]).reshape(128, 8 * 1024)
    w2_l = np.ascontiguousarray(f(w_mlp2)[0].reshape(32, 128, 1024))
    pp = _pack_params(f(conv_w), f(conv_b), f(rg_ba), f(rg_bx), f(rg_lambda), f(hg_lb_logits), f(hg_norm_g),
                      f(norm_mix_g), f(norm_mlp_g))
    gfin = np.ascontiguousarray(np.broadcast_to(f(final_norm_g)[None, :], (128, D)))
    consts = _consts()
    shared = dict(w_mix=w_mix, w_gate=wgate, w_f2=w_f2, w_out=wout_l, w_2=w2_l, pp=pp, gfin=gfin, consts=consts)
    in_maps = []
    for c in range(NCORES):
        if c < len(seqs):
            xs = np.concatenate([meta_tokens, seqs[c]], axis=0)
        else:
            xs = np.zeros((T + NMETA, D), np.float32)
        m = dict(shared)
        m["xs"] = np.ascontiguousarray(xs)
        in_maps.append(m)
    nc = build(NT)
    res = run_bass_kernel_spmd(nc, in_maps, core_ids=list(range(NCORES)))
    outs = [np.asarray(res.results[c]["y"], dtype=np.float32) for c in range(len(seqs))]
    nb = x_prompt.shape[0]
    y_prompt = np.stack(outs[:nb])
    y_sample = np.stack(outs[nb:])
    return (y_prompt, y_sample)
```

```python
import numpy as np
import concourse.bass as bass
import concourse.mybir as mybir
from concourse.bass_utils import run_bass_kernel_spmd
from concourse.ap import AP

F32 = mybir.dt.float32
BF16 = mybir.dt.bfloat16
U8 = mybir.dt.uint8
ALU = mybir.AluOpType
AF = mybir.ActivationFunctionType

D = 1024
NMETA = 16
TN = 512
EPS = 1e-6
RG_C = 8.0
NCORES = 8
NDSEM = 24

PP_HEAD = 16
PP_GMIX = 128
PP_GMLP = 136
PP_N = 144
DV_SP = 0
DV_SP2 = 16
DV_LB = 32
DV_OML = 48
DV_HGG = 64
DV_EPS128 = 72
DV_EPSD = 73
DV_TINY = 74
DV_N = 80


def rev_ap(a):
    aps = [list(x) for x in a.ap]
    step, cnt = aps[-1]
    aps[-1] = [-step, cnt]
    return AP(a.tensor, a.offset + step * (cnt - 1), aps)


class Prog:
    def __init__(self):
        self.ops = []
        self.last_w = {}
        self.readers = {}
        self.last_barrier = 0

    def add(self, eng, fn, reads=(), writes=(), dma=False):
        i = len(self.ops)
        deps = set()
        for r in reads:
            if r in self.last_w:
                deps.add(self.last_w[r])
        for w in writes:
            if w in self.last_w:
                deps.add(self.last_w[w])
            deps.update(self.readers.get(w, ()))
        for r in reads:
            self.readers.setdefault(r, []).append(i)
        for w in writes:
            self.last_w[w] = i
            self.readers[w] = []
        self.ops.append(dict(eng=eng, fn=fn, deps=deps, dma=dma, sig=False))
        return i

    def barrier(self):
        n = len(self.ops)
        deps = set()
        last = {}
        for i in range(self.last_barrier, n):
            op = self.ops[i]
            if op['dma']:
                deps.add(i)
            elif op['fn'] is not None:
                last[op['eng']] = i
        deps.update(last.values())
        for e in ('pe', 'act', 'dve', 'pool', 'sp'):
            self.ops.append(dict(eng=e, fn=None, deps=set(deps), dma=False, sig=False))
        self.last_barrier = len(self.ops)
        self.last_w = {}
        self.readers = {}

    def finalize(self, nc, sems, dsems):
        ops = self.ops
        for q, ring in dsems.items():
            dma_idx = [i for i, o in enumerate(ops) if o['dma'] and o['eng'] == q]
            nr = len(ring)
            for j, i in enumerate(dma_idx):
                ops[i]['sem'] = ring[j % nr]
                ops[i]['val'] = 16 * (j // nr + 1)
                ops[i]['sig'] = True
                if j >= nr:
                    ops[i]['deps'].add(dma_idx[j - nr])
        for i, o in enumerate(ops):
            nd = set()
            for d in o['deps']:
                od = ops[d]
                if od['fn'] is None:
                    continue
                if (not od['dma']) and od['eng'] == o['eng'] and o['eng'] == 'pe' and not o['dma']:
                    continue
                nd.add(d)
                od['sig'] = True
            o['deps'] = nd
        cnt = {}
        for o in ops:
            if o['dma'] or o['fn'] is None:
                continue
            if o['sig']:
                cnt[o['eng']] = cnt.get(o['eng'], 0) + 1
                o['sem'] = sems[o['eng']]
                o['val'] = cnt[o['eng']]
        self.n_sig = cnt

    def emit(self, eng_name, e):
        known = {}
        for o in self.ops:
            if o['eng'] != eng_name:
                continue
            waits = {}
            for d in o['deps']:
                od = self.ops[d]
                s, v = od['sem'], od['val']
                k = id(s)
                if known.get(k, 0) >= v:
                    continue
                if k not in waits or waits[k][1] < v:
                    waits[k] = (s, v)
            for k, (s, v) in waits.items():
                e.wait_ge(s, v)
                known[k] = v
            if o['fn'] is None:
                continue
            ins = o['fn'](e)
            if o['sig']:
                ins.then_inc(o['sem'], 16 if o['dma'] else 1)


class Arena:
    def __init__(self, base_ap, nbytes):
        self.base = base_ap
        self.nbytes = nbytes
        self.off = 0
        self.mark_ = 0

    def alloc(self, free_elems, dtype, shape3=None):
        sz = free_elems * (4 if dtype == F32 else 2)
        sz_al = (sz + 31) // 32 * 32
        assert self.off + sz_al <= self.nbytes, f"SBUF arena overflow {self.off + sz_al} > {self.nbytes}"
        a = self.base[:, self.off:self.off + sz].bitcast(dtype)
        self.off += sz_al
        if shape3 is not None:
            a = a.rearrange("p (a b) -> p a b", b=shape3)
        return a

    def mark(self):
        return self.off

    def reset(self, m):
        self.off = m


class Ring:
    def __init__(self, bufs, name, keys=None):
        self.bufs = bufs
        self.name = name
        self.keys = keys
        self.i = 0

    def next(self):
        k = self.i % len(self.bufs)
        self.i += 1
        return self.bufs[k], (self.keys[k] if self.keys else (self.name, k))


def build(NT, stop_after=3):
    T = NT * TN
    L = T + NMETA
    nc = bass.Bass("TRN2", target_bir_lowering=False)
    P = Prog()

    xs = nc.dram_tensor("xs", [L, D], F32, kind="ExternalInput").ap()
    w_mix = nc.dram_tensor("w_mix", [5, 128, 8 * 1024], F32, kind="ExternalInput").ap()
    w_gate = nc.dram_tensor("w_gate", [128, 4 * 8 * 128], F32, kind="ExternalInput").ap()
    w_f2 = nc.dram_tensor("w_f2", [80, 128, 1024], F32, kind="ExternalInput").ap()
    w_out = nc.dram_tensor("w_out", [128, 8 * 1024], F32, kind="ExternalInput").ap()
    w_2 = nc.dram_tensor("w_2", [32, 128, 1024], F32, kind="ExternalInput").ap()
    pp_d = nc.dram_tensor("pp", [128, PP_N], F32, kind="ExternalInput").ap()
    gfin_d = nc.dram_tensor("gfin", [128, D], F32, kind="ExternalInput").ap()
    consts_d = nc.dram_tensor("consts", [128, 128 * 4 + 512 * 2], F32, kind="ExternalInput").ap()
    y_d = nc.dram_tensor("y", [T, D], F32, kind="ExternalOutput").ap()

    hb_scr = nc.dram_tensor("hb_scr", [8, 128, T], F32, kind="Internal").ap()
    ob_scr = nc.dram_tensor("ob_scr", [8, 128, T], F32, kind="Internal").ap()
    hs_scr = nc.dram_tensor("hs_scr", [8, 128, T], F32, kind="Internal").ap()
    on_scr = nc.dram_tensor("on_scr", [8, 128, T], F32, kind="Internal").ap()
    xc_scr = nc.dram_tensor("xc_scr", [8, 128, T], F32, kind="Internal").ap()
    qs_scr = nc.dram_tensor("qs_scr", [8, 128, T], F32, kind="Internal").ap()
    xcb_scr = nc.dram_tensor("xcb_scr", [8, 128, T], BF16, kind="Internal").ap()
    vb_scr = nc.dram_tensor("vb_scr", [8, NT, 128, TN], BF16, kind="Internal").ap()
    hT_scr = nc.dram_tensor("hT_scr", [NT, 128, 8, TN], BF16, kind="Internal").ap()
    wf2_bf = nc.dram_tensor("wf2_bf", [80, 128, 1024], BF16, kind="Internal").ap()
    w2_bf = nc.dram_tensor("w2_bf", [32, 128, 1024], BF16, kind="Internal").ap()

    ARENA_BYTES = 206 * 1024
    arena_t = nc.alloc_sbuf_tensor("arena", [128, ARENA_BYTES], U8).ap()
    A = Arena(arena_t, ARENA_BYTES)
    banks = [nc.alloc_psum_tensor(f"bank{i}", [128, 512], F32).ap() for i in range(8)]

    pp = A.alloc(PP_N, F32)
    dv = A.alloc(DV_N, F32)
    ident = A.alloc(128, BF16)
    ones_f = A.alloc(128, F32)
    mscF = A.alloc(128, F32)
    mscR = A.alloc(128, F32)
    maskF = A.alloc(512, F32)
    maskR = A.alloc(512, F32)
    junk = A.alloc(1024, BF16)
    ss = A.alloc(8, F32)
    rs = A.alloc(8, F32)
    base_mark = A.mark()
    ctmp = A.alloc(128 * 4 + 1024, F32)

    def col(t, c):
        return t[:, c:c + 1]

    def dma(q, out, in_, reads, writes):
        return P.add(q, lambda e: e.dma_start(out=out, in_=in_), reads, writes, dma=True)

    dma('sp', pp, pp_d, [], ['pp'])
    dma('sp', ctmp, consts_d, [], ['ctmp'])
    P.add('dve', lambda e: e.tensor_copy(out=ident, in_=ctmp[:, 0:128]), ['ctmp'], ['ident'])
    P.add('dve', lambda e: e.tensor_copy(out=ones_f, in_=ctmp[:, 128:256]), ['ctmp'], ['ones'])
    P.add('dve', lambda e: e.tensor_copy(out=mscF, in_=ctmp[:, 256:384]), ['ctmp'], ['mscF'])
    P.add('dve', lambda e: e.tensor_copy(out=mscR, in_=ctmp[:, 384:512]), ['ctmp'], ['mscR'])
    P.add('dve', lambda e: e.tensor_copy(out=maskF, in_=ctmp[:, 512:1024]), ['ctmp'], ['maskF'])
    P.add('dve', lambda e: e.tensor_copy(out=maskR, in_=ctmp[:, 1024:1536]), ['ctmp'], ['maskR'])
    for c in range(0, 80, 8):
        dma('pool', wf2_bf[c:c + 8], w_f2[c:c + 8], [], [('wf2', c)])
    for c in range(0, 32, 8):
        dma('pool', w2_bf[c:c + 8], w_2[c:c + 8], [], [('w2s', c)])

    dtmp = A.alloc(64, F32)
    for h in range(8):
        b0 = h * PP_HEAD
        for dr in range(2):
            k = dr * 8 + h
            lam = col(pp, b0 + 9 + dr)
            P.add('act', lambda e, lam=lam, k=k: e.activation(out=col(dtmp, k), in_=lam, func=AF.Exp, scale=-1.0),
                  ['pp'], [('dtmp', k)])
            P.add('act', lambda e, k=k: e.activation(out=col(dtmp, k), in_=col(dtmp, k), func=AF.Ln, bias=1.0),
                  [('dtmp', k)], [('dtmp', k)])
            P.add('dve', lambda e, k=k: e.tensor_scalar(out=col(dv, DV_SP + k), in0=col(dtmp, k), scalar1=-RG_C,
                                                        scalar2=None, op0=ALU.mult), [('dtmp', k)], ['dv'])
            P.add('dve', lambda e, k=k: e.tensor_scalar(out=col(dv, DV_SP2 + k), in0=col(dtmp, k),
                                                        scalar1=-2.0 * RG_C, scalar2=None, op0=ALU.mult),
                  [('dtmp', k)], ['dv'])
            l0 = col(pp, b0 + 11 + dr)
            l1 = col(pp, b0 + 13 + dr)
            P.add('dve', lambda e, l0=l0, l1=l1, k=k: e.tensor_tensor(out=col(dtmp, 16 + k), in0=l0, in1=l1,
                                                                     op=ALU.subtract), ['pp'], [('dtmp', 16 + k)])
        P.add('dve', lambda e, h=h, b0=b0: e.tensor_scalar(out=col(dv, DV_HGG + h), in0=col(pp, b0 + 15),
                                                           scalar1=float(np.sqrt(128.0)), scalar2=None,
                                                           op0=ALU.mult), ['pp'], ['dv'])
    P.add('act', lambda e: e.activation(out=dv[:, DV_LB:DV_LB + 16], in_=dtmp[:, 16:32], func=AF.Sigmoid),
          [('dtmp', 16 + k) for k in range(16)], ['dv'])
    P.add('dve', lambda e: e.tensor_scalar(out=dv[:, DV_OML:DV_OML + 16], in0=dv[:, DV_LB:DV_LB + 16],
                                           scalar1=-1.0, scalar2=1.0, op0=ALU.mult, op1=ALU.add), ['dv'], ['dv'])
    P.add('pool', lambda e: e.memset(col(dv, DV_EPS128), float(128.0 * EPS)), [], ['dv'])
    P.add('pool', lambda e: e.memset(col(dv, DV_EPSD), float(D * EPS)), [], ['dv'])
    P.add('pool', lambda e: e.memset(col(dv, DV_TINY), 1e-30), [], ['dv'])
    P.barrier()
    A.reset(base_mark)

    def stage_x(p0, N, xt_list, hT, hT_key, g_col0, halo, psbank, psbank_key, hcol0=0):
        nsub = (N + 127) // 128
        psT = psbank.bitcast(BF16).rearrange("p (a b) -> p a b", b=128)
        gap = pp[:, g_col0:g_col0 + 8]
        for s in range(nsub):
            npk = min(128, N - 128 * s)
            xt, xk = xt_list[s]
            dma('sp', xt[:npk, :], xs[p0 + 128 * s:p0 + 128 * s + npk, :], [], [xk])
            P.add('pool', lambda e, s=s: e.memset(col(ss, s), 0.0), [], [('ss', s)])
            P.add('act', lambda e, xt=xt, npk=npk, s=s: e.activation(
                out=junk[:npk, :], in_=xt[:npk, :], func=AF.Square, accum_out=ss[:npk, s:s + 1]),
                [xk, ('ss', s)], ['junk', ('ss', s)])
            P.add('act', lambda e, npk=npk, s=s: e.activation(out=rs[:npk, s:s + 1], in_=ss[:npk, s:s + 1], func=AF.Ln, bias=dv[:npk, DV_EPSD:DV_EPSD + 1]), [('ss', s), 'dv'], [('rs', s)])
            P.add('act', lambda e, npk=npk, s=s: e.activation(out=rs[:npk, s:s + 1], in_=rs[:npk, s:s + 1], func=AF.Exp, scale=-0.5), [('rs', s)], [('rs', s)])
            xn, xnk = xn_ring.next()
            P.add('dve', lambda e, xt=xt, xn=xn, npk=npk, s=s: e.tensor_scalar(
                out=xn[:npk, :], in0=xt[:npk, :], scalar1=rs[:npk, s:s + 1], scalar2=32.0,
                op0=ALU.mult, op1=ALU.mult), [xk, ('rs', s)], [xnk])

            def tr(e, xn=xn, npk=npk):
                ins = None
                for kc in range(8):
                    ins = e.transpose(out=psT[:, kc, 0:npk], in_=xn[:npk, kc * 128:(kc + 1) * 128],
                                      identity=ident[:npk, :npk])
                return ins
            P.add('pe', tr, [xnk, 'ident'], [psbank_key])
            c0 = hcol0 + 128 * s
            P.add('dve', lambda e, npk=npk, c0=c0: e.tensor_tensor(
                out=hT[:, :, c0:c0 + npk], in0=psT[:, :, 0:npk],
                in1=gap.unsqueeze(2).broadcast_to([128, 8, npk]), op=ALU.mult),
                [psbank_key, 'pp'], [hT_key])
        if halo:
            xh, xhk = xt_ring.next()
            xnh, xnhk = xn_ring.next()
            P.add('pool', lambda e: e.memset(xh[0:3, :], 0.0), [], [xhk])
            if p0 >= 2:
                dma('sp', xh[0:2, :], xs[p0 - 2:p0, :], [], [xhk])
            if p0 + N < L:
                dma('sp', xh[2:3, :], xs[p0 + N:p0 + N + 1, :], [], [xhk])
            P.add('pool', lambda e: e.memset(ss[0:3, 7:8], 0.0), [], [('ss', 7)])
            P.add('act', lambda e: e.activation(out=junk[0:3, :], in_=xh[0:3, :], func=AF.Square,
                                                accum_out=ss[0:3, 7:8]), [xhk, ('ss', 7)], ['junk', ('ss', 7)])
            P.add('act', lambda e: e.activation(out=rs[0:3, 7:8], in_=ss[0:3, 7:8], func=AF.Ln,
                                                bias=dv[0:3, DV_EPSD:DV_EPSD + 1]), [('ss', 7), 'dv'], [('rs', 7)])
            P.add('act', lambda e: e.activation(out=rs[0:3, 7:8], in_=rs[0:3, 7:8], func=AF.Exp, scale=-0.5),
                  [('rs', 7)], [('rs', 7)])
            P.add('dve', lambda e: e.tensor_scalar(out=xnh[0:3, :], in0=xh[0:3, :], scalar1=rs[0:3, 7:8],
                                                   scalar2=32.0, op0=ALU.mult, op1=ALU.mult),
                  [xhk, ('rs', 7)], [xnhk])

            def trh(e):
                ins = None
                for kc in range(8):
                    ins = e.transpose(out=psT[:, kc, 0:3], in_=xnh[0:3, kc * 128:(kc + 1) * 128],
                                      identity=ident[0:3, 0:3])
                return ins
            P.add('pe', trh, [xnhk, 'ident'], [psbank_key])
            P.add('dve', lambda e: e.tensor_tensor(
                out=hT[:, :, N:N + 3], in0=psT[:, :, 0:3],
                in1=gap.unsqueeze(2).broadcast_to([128, 8, 3]), op=ALU.mult),
                [psbank_key, 'pp'], [hT_key])

    wmix = [A.alloc(8 * 1024, BF16, shape3=1024) for _ in range(4)]
    wg = A.alloc(2 * 8 * 128, BF16).rearrange("p (g h j) -> p g h j", g=2, h=8)
    GXA, GQ, GF, GV = range(4)
    for g, src in ((GXA, 0), (GQ, 1), (GV, 4)):
        dma('pool', wmix[g].rearrange("p a b -> p (a b)"), w_mix[src], [], [('wmix', g)])

    xt_ring = Ring([A.alloc(1024, F32) for _ in range(2)], 'xt')
    xn_ring = Ring([A.alloc(1024, BF16) for _ in range(1)], 'xn')
    hT_bufs = [A.alloc(8 * (TN + 3), BF16, shape3=TN + 3) for _ in range(1)]
    carryA = A.alloc(8, F32)
    S_f = A.alloc(8 * 128, F32, shape3=128)
    S_b = A.alloc(8 * 128, BF16, shape3=128)

    def tmp(n=TN, dt=F32, cnt=2, name=None):
        return Ring([A.alloc(n, dt) for _ in range(cnt)], name)
    t_ext = tmp(TN + 3, F32, 2, 'ext')
    t_xc = tmp(name='xc')
    t_xcb = tmp(dt=BF16, name='xcb')
    t_r = tmp(name='r')
    t_i = tmp(name='i')
    t_a = tmp(name='a')
    t_a2 = tmp(cnt=2, name='a2')
    t_u = tmp(cnt=2, name='u')
    t_h = tmp(name='h')
    t_hb = tmp(cnt=2, name='hbl')
    t_sg = tmp(cnt=1, name='sg')
    t_f = tmp(name='f')
    t_qs = tmp(cnt=2, name='qs')
    t_g = tmp(cnt=1, name='g')
    t_b = tmp(cnt=2, name='b')
    t_eb = tmp(cnt=4, name='eb')
    t_enb = tmp(cnt=2, name='enb')
    t_qt = tmp(dt=BF16, cnt=4, name='qt')
    t_kt = tmp(dt=BF16, cnt=4, name='kt')
    t_kh = tmp(dt=BF16, cnt=4, name='kh')
    t_vb = tmp(dt=BF16, cnt=4, name='vb')
    khT4 = [[A.alloc(128, BF16) for _ in range(4)] for _ in range(2)]
    PT4 = [[A.alloc(128, BF16) for _ in range(4)] for _ in range(2)]
    t_ob = [tmp(cnt=1, name='obl0'), tmp(cnt=1, name='obl1')]
    t_os = [tmp(cnt=1, name='os0'), tmp(cnt=1, name='os1')]
    t_osq = [tmp(cnt=1, name='osq0'), tmp(cnt=1, name='osq1')]
    t_rso = [tmp(cnt=1, name='rso0'), tmp(cnt=1, name='rso1')]
    mix_mark_end = A.mark()

    gen_ring = Ring(banks[0:3], 'psg')
    halo_ring = Ring([banks[3][:, 0:4], banks[3][:, 4:8]], 'pshalo', keys=['bank3', 'bank3'])
    kT_slots = [banks[3][:, 64:128].bitcast(BF16), banks[3][:, 128:192].bitcast(BF16)]
    sc_slots = [banks[3][:, 256:384], banks[3][:, 384:512]]
    ch_o = [(banks[4], 'bank4'), (banks[6], 'bank6')]
    ch_m = [(banks[5], 'bank5'), (banks[7], 'bank7')]

    def stage1_steps(dirn, p0, N, hT, hT_key, meta, h, st):
        rv = (dirn == 1)
        R = (lambda a: rev_ap(a)) if rv else (lambda a: a)
        nsub = (N + 127) // 128
        CL = min(64, N)
        pr0 = p0 - NMETA
        combine = (dirn == 0) and not meta
        hc = slice(h * 128, (h + 1) * 128)
        b0 = h * PP_HEAD
        npv = min(128, N)
        msk = maskR if rv else maskF
        mkey = 'maskR' if rv else 'maskF'
        nch = N // CL
        lc = 0 if rv else CL - 1
        lastc = 0 if rv else N - 1
        V = {}
        steps = []

        def step(f):
            steps.append(f)
            return f

        cload = combine
        cstore = (dirn == 1)

        @step
        def s_xa():
            if combine:
                V['hbl'], V['khbl'] = t_hb.next()
                dma('sp', V['hbl'][:, 0:N], hb_scr[h, :, pr0:pr0 + N], [], [V['khbl']])
            if cload:
                xcb, kxcb = t_xcb.next()
                xc, kxc = t_xc.next()
                qs, kqs = t_qs.next()
                vb, kvb = t_vb.next()
                V.update(xcb=xcb, kxcb=kxcb, xc=xc, kxc=kxc, qs=qs, kqs=kqs, vb=vb, kvb=kvb)
                dma('sp', xcb[:, 0:N], xcb_scr[h, :, pr0:pr0 + N], [], [kxcb])
                dma('sp', xc[:, 0:N], xc_scr[h, :, pr0:pr0 + N], [], [kxc])
                dma('sp', qs[:, 0:N], qs_scr[h, :, pr0:pr0 + N], [], [kqs])
                dma('sp', vb[:, 0:N], vb_scr[h, pr0 // TN], [], [kvb])
                return
            ps_xa, kxa = gen_ring.next()
            ps_hl, khl = halo_ring.next()
            V.update(ps_xa=ps_xa, kxa=kxa, ps_hl=ps_hl, khl=khl)

            def mm_xa(e):
                ins = None
                for kc in range(8):
                    e.matmul(ps_xa[:, 0:N], lhsT=wmix[GXA][:, kc, hc], rhs=hT[:, kc, 0:N],
                             start=(kc == 0), stop=(kc == 7))
                for kc in range(8):
                    ins = e.matmul(ps_hl[:, 0:3], lhsT=wmix[GXA][:, kc, hc], rhs=hT[:, kc, N:N + 3],
                                   start=(kc == 0), stop=(kc == 7))
                return ins
            P.add('pe', mm_xa, [hT_key, ('wmix', GXA)], [kxa, khl])

        @step
        def s_ext():
            if cload:
                return
            ext, kext = t_ext.next()
            V.update(ext=ext, kext=kext)
            ps_xa, ps_hl = V['ps_xa'], V['ps_hl']
            P.add('act', lambda e: e.activation(out=ext[:, 2:2 + N], in_=ps_xa[:, 0:N], func=AF.Copy),
                  [V['kxa']], [kext])
            P.add('dve', lambda e: e.tensor_copy(out=ext[:, 0:2], in_=ps_hl[:, 0:2]), [V['khl']], [kext])
            P.add('dve', lambda e: e.tensor_copy(out=ext[:, N + 2:N + 3], in_=ps_hl[:, 2:3]), [V['khl']], [kext])

        @step
        def s_q():
            if cload:
                return
            ps_q, kq_ = gen_ring.next()
            V.update(ps_q=ps_q, kq_=kq_)

            def mm_q(e):
                ins = None
                for kc in range(8):
                    ins = e.matmul(ps_q[:, 0:N], lhsT=wmix[GQ][:, kc, hc], rhs=hT[:, kc, 0:N],
                                   start=(kc == 0), stop=(kc == 7))
                return ins
            P.add('pe', mm_q, [hT_key, ('wmix', GQ)], [kq_])

        @step
        def s_sg():
            if cload:
                return
            sg, ksg = t_sg.next()
            qs, kqs = t_qs.next()
            xc, kxc = t_xc.next()
            V.update(qs=qs, kqs=kqs, xc=xc, kxc=kxc)
            ps_q, ext = V['ps_q'], V['ext']
            P.add('act', lambda e: e.activation(out=sg[:, 0:N], in_=ps_q[:, 0:N], func=AF.Sigmoid),
                  [V['kq_']], [ksg])
            P.add('dve', lambda e: e.tensor_scalar(
                out=xc[:, 0:N], in0=ext[:, 0:N], scalar1=col(pp, b0 + 0), scalar2=col(pp, b0 + 4),
                op0=ALU.mult, op1=ALU.add), [V['kext'], 'pp'], [kxc])
            P.add('dve', lambda e: e.tensor_tensor(out=qs[:, 0:N], in0=sg[:, 0:N], in1=ps_q[:, 0:N], op=ALU.mult),
                  [ksg, V['kq_']], [kqs])

        @step
        def s_f():
            ps_f, kf_ = gen_ring.next()
            V.update(ps_f=ps_f, kf_=kf_)

            def mm_f(e):
                ins = None
                for kc in range(8):
                    ins = e.matmul(ps_f[:, 0:N], lhsT=wmix[GF][:, kc, hc], rhs=hT[:, kc, 0:N],
                                   start=(kc == 0), stop=(kc == 7))
                return ins
            P.add('pe', mm_f, [hT_key, ('wmix', GF)], [kf_])

        @step
        def s_sf():
            f_, kff = t_f.next()
            V.update(f_=f_, kff=kff)
            ps_f = V['ps_f']
            P.add('act', lambda e: e.activation(out=f_[:, 0:N], in_=ps_f[:, 0:N], func=AF.Sigmoid),
                  [V['kf_']], [kff])
            if cload:
                return
            ext, xc, kxc = V['ext'], V['xc'], V['kxc']
            P.add('dve', lambda e: e.scalar_tensor_tensor(
                out=xc[:, 0:N], in0=ext[:, 1:1 + N], scalar=col(pp, b0 + 1), in1=xc[:, 0:N],
                op0=ALU.mult, op1=ALU.add), [V['kext'], kxc, 'pp'], [kxc])

        @step
        def s_v():
            if cload:
                return
            ps_v, kv_ = gen_ring.next()
            V.update(ps_v=ps_v, kv_=kv_)

            def mm_v(e):
                ins = None
                for s in range(nsub):
                    npk = min(128, N - 128 * s)
                    for kc in range(8):
                        ins = e.matmul(ps_v[:npk, s * 128:(s + 1) * 128], lhsT=hT[:, kc, 128 * s:128 * s + npk],
                                       rhs=wmix[GV][:, kc, hc], start=(kc == 0), stop=(kc == 7))
                return ins
            P.add('pe', mm_v, [hT_key, ('wmix', GV)], [kv_])

        @step
        def s_vb():
            f_, kff = V['f_'], V['kff']
            P.add('dve', lambda e: e.tensor_scalar(
                out=f_[:, 0:N], in0=f_[:, 0:N], scalar1=col(dv, DV_OML + dirn * 8 + h),
                scalar2=col(dv, DV_LB + dirn * 8 + h), op0=ALU.mult, op1=ALU.add), [kff, 'dv'], [kff])
            if cload:
                return
            vb, kvb = t_vb.next()
            V.update(vb=vb, kvb=kvb)
            ps_v, ext, xc, kxc = V['ps_v'], V['ext'], V['xc'], V['kxc']
            P.add('act', lambda e: e.activation(out=vb[:npv, 0:nsub * 128], in_=ps_v[:npv, 0:nsub * 128],
                                                func=AF.Copy), [V['kv_']], [kvb])
            P.add('dve', lambda e: e.scalar_tensor_tensor(
                out=xc[:, 0:N], in0=ext[:, 2:2 + N], scalar=col(pp, b0 + 2), in1=xc[:, 0:N],
                op0=ALU.mult, op1=ALU.add), [V['kext'], kxc, 'pp'], [kxc])

        @step
        def s_conv3():
            if cload:
                return
            ext, xc, kxc = V['ext'], V['xc'], V['kxc']
            xcb, kxcb = t_xcb.next()
            V.update(xcb=xcb, kxcb=kxcb)
            P.add('dve', lambda e: e.scalar_tensor_tensor(
                out=xc[:, 0:N], in0=ext[:, 3:3 + N], scalar=col(pp, b0 + 3), in1=xc[:, 0:N],
                op0=ALU.mult, op1=ALU.add), [V['kext'], kxc, 'pp'], [kxc])
            P.add('dve', lambda e: e.tensor_copy(out=xcb[:, 0:N], in_=xc[:, 0:N]), [kxc], [kxcb])
            if cstore:
                dma('sp', xcb_scr[h, :, pr0:pr0 + N], xcb[:, 0:N], [kxcb], [])
                dma('sp', xc_scr[h, :, pr0:pr0 + N], xc[:, 0:N], [kxc], [])
                dma('sp', qs_scr[h, :, pr0:pr0 + N], V['qs'][:, 0:N], [V['kqs']], [])
                dma('sp', vb_scr[h, pr0 // TN], V['vb'][:, 0:N], [V['kvb']], [])

        @step
        def s_gr():
            ps_r, kr_ = gen_ring.next()
            V.update(ps_r=ps_r, kr_=kr_)
            xcb = V['xcb']
            P.add('pe', lambda e: e.matmul(ps_r[:, 0:N], lhsT=wg[:, 0, h, :], rhs=xcb[:, 0:N],
                                           start=True, stop=True), [V['kxcb'], 'wg'], [kr_])

        @step
        def s_r():
            r_, krr = t_r.next()
            V.update(r_=r_, krr=krr)
            ps_r = V['ps_r']
            P.add('act', lambda e: e.activation(out=r_[:, 0:N], in_=ps_r[:, 0:N], func=AF.Sigmoid,
                                                bias=col(pp, b0 + 5 + dirn)), [V['kr_'], 'pp'], [krr])

        @step
        def s_gi():
            ps_i, ki_ = gen_ring.next()
            V.update(ps_i=ps_i, ki_=ki_)
            xcb = V['xcb']
            P.add('pe', lambda e: e.matmul(ps_i[:, 0:N], lhsT=wg[:, 1, h, :], rhs=xcb[:, 0:N],
                                           start=True, stop=True), [V['kxcb'], 'wg'], [ki_])

        @step
        def s_i():
            i_, kii = t_i.next()
            V.update(i_=i_, kii=kii)
            ps_i = V['ps_i']
            P.add('act', lambda e: e.activation(out=i_[:, 0:N], in_=ps_i[:, 0:N], func=AF.Sigmoid,
                                                bias=col(pp, b0 + 7 + dirn)), [V['ki_'], 'pp'], [kii])

        @step
        def s_g():
            g_, kgg = t_g.next()
            b_, kbb = t_b.next()
            V.update(b_=b_, kbb=kbb)
            f_ = V['f_']
            P.add('act', lambda e: e.activation(out=g_[:, 0:N], in_=f_[:, 0:N], func=AF.Ln), [V['kff']], [kgg])
            P.add('dve', lambda e: e.tensor_tensor_scan(
                out=R(b_[:, 0:N]), data0=R(msk[:, 0:N]), data1=R(g_[:, 0:N]), initial=0.0,
                op0=ALU.mult, op1=ALU.add), [kgg, mkey], [kbb])

        @step
        def s_a2():
            a2, ka2 = t_a2.next()
            V.update(a2=a2, ka2=ka2)
            r_, i_, xc = V['r_'], V['i_'], V['xc']
            P.add('act', lambda e: e.activation(out=a2[:, 0:N], in_=r_[:, 0:N], func=AF.Exp,
                                                scale=col(dv, DV_SP2 + dirn * 8 + h)), [V['krr'], 'dv'], [ka2])
            P.add('dve', lambda e: e.tensor_tensor(out=i_[:, 0:N], in0=i_[:, 0:N], in1=xc[:, 0:N], op=ALU.mult),
                  [V['kii'], V['kxc']], [V['kii']])

        @step
        def s_abs():
            a2, ka2 = V['a2'], V['ka2']
            P.add('act', lambda e: e.activation(out=a2[:, 0:N], in_=a2[:, 0:N], func=AF.Abs, scale=-1.0, bias=1.0),
                  [ka2], [ka2])

        @step
        def s_eb():
            eb, keb = t_eb.next()
            V.update(eb=eb, keb=keb)
            b_ = V['b_']
            P.add('act', lambda e: e.activation(out=eb[:, 0:N], in_=b_[:, 0:N], func=AF.Exp), [V['kbb']], [keb])

        @step
        def s_ln():
            a2, ka2 = V['a2'], V['ka2']
            P.add('act', lambda e: e.activation(out=a2[:, 0:N], in_=a2[:, 0:N], func=AF.Ln, bias=col(dv, DV_TINY)),
                  [ka2, 'dv'], [ka2])

        @step
        def s_enb():
            enb, kenb = t_enb.next()
            qt, kqt = t_qt.next()
            V.update(enb=enb, kenb=kenb, qt=qt, kqt=kqt)
            b_, qs, eb = V['b_'], V['qs'], V['eb']
            P.add('act', lambda e: e.activation(out=enb[:, 0:N], in_=b_[:, 0:N], func=AF.Exp, scale=-1.0),
                  [V['kbb']], [kenb])
            P.add('pool', lambda e: e.tensor_tensor(out=qt[:, 0:N], in0=qs[:, 0:N], in1=eb[:, 0:N], op=ALU.mult),
                  [V['kqs'], V['keb']], [kqt])

        @step
        def s_sqrt():
            a2, ka2 = V['a2'], V['ka2']
            P.add('act', lambda e: e.activation(out=a2[:, 0:N], in_=a2[:, 0:N], func=AF.Exp, scale=0.5),
                  [ka2], [ka2])

        @step
        def s_a():
            a_, kaa = t_a.next()
            kt, kkt = t_kt.next()
            V.update(a_=a_, kaa=kaa, kt=kt, kkt=kkt)
            r_, f_, enb = V['r_'], V['f_'], V['enb']
            P.add('act', lambda e: e.activation(out=a_[:, 0:N], in_=r_[:, 0:N], func=AF.Exp,
                                                scale=col(dv, DV_SP + dirn * 8 + h)), [V['krr'], 'dv'], [kaa])
            P.add('dve', lambda e: e.scalar_tensor_tensor(
                out=kt[:, 0:N], in0=f_[:, 0:N], scalar=1.0, in1=enb[:, 0:N], op0=ALU.subtract, op1=ALU.mult),
                [V['kff'], V['kenb']], [kkt])

        @step
        def s_u():
            u_, kuu = t_u.next()
            V.update(u_=u_, kuu=kuu)
            a2, i_ = V['a2'], V['i_']
            P.add('dve', lambda e: e.tensor_tensor(out=u_[:, 0:N], in0=a2[:, 0:N], in1=i_[:, 0:N], op=ALU.mult),
                  [V['ka2'], V['kii']], [kuu])

        @step
        def s_kh():
            kh, kkh = t_kh.next()
            kt, eb = V['kt'], V['eb']
            eb3 = eb[:, 0:N].rearrange("p (c t) -> p c t", t=CL)
            P.add('dve', lambda e: e.tensor_tensor(
                out=kh[:, 0:N].rearrange("p (c t) -> p c t", t=CL),
                in0=kt[:, 0:N].rearrange("p (c t) -> p c t", t=CL),
                in1=eb3[:, :, lc:lc + 1].broadcast_to([128, nch, CL]), op=ALU.mult), [V['kkt'], V['keb']], [kkh])
            st.update(qt=V['qt'], kqt=V['kqt'], kt=kt, kkt=V['kkt'], kh=kh, kkh=kkh, vb=V['vb'], kvb=V['kvb'],
                      eb=eb, keb=V['keb'])

        @step
        def s_scan():
            hh, khh = t_h.next()
            V.update(hh=hh, khh=khh)
            a_, u_ = V['a_'], V['u_']
            P.add('dve', lambda e: e.tensor_tensor_scan(
                out=R(hh[:, 0:N]), data0=R(a_[:, 0:N]), data1=R(u_[:, 0:N]), initial=col(carryA, h),
                op0=ALU.mult, op1=ALU.add), [V['kaa'], V['kuu'], ('carryA', h)], [khh])
            P.add('pool', lambda e: e.tensor_copy(out=col(carryA, h), in_=hh[:, lastc:lastc + 1]),
                  [khh], [('carryA', h)])

        @step
        def s_out():
            hh, khh = V['hh'], V['khh']
            if dirn == 1:
                dma('sp', hb_scr[h, :, pr0:pr0 + N], hh[:, 0:N], [khh], [])
            elif combine:
                hbl, khbl = V['hbl'], V['khbl']
                P.add('pool', lambda e: e.tensor_tensor(out=hh[:, 0:N], in0=hh[:, 0:N], in1=hbl[:, 0:N], op=ALU.add),
                      [khh, khbl], [khh])
                dma('sp', hs_scr[h, :, pr0:pr0 + N], hh[:, 0:N], [khh], [])
        return steps

    def stage1_pair(dirn, p0, N, hT, hT_key, meta, h0, sts):
        sa = stage1_steps(dirn, p0, N, hT, hT_key, meta, h0, sts[0])
        sb = stage1_steps(dirn, p0, N, hT, hT_key, meta, h0 + 1, sts[1])
        for fa, fb in zip(sa, sb):
            fa()
            yield
            fb()
            yield

    def stage2(dirn, p0, N, meta, h, st, chain):
        rv = (dirn == 1)
        nsub = (N + 127) // 128
        CL = min(64, N)
        pr0 = p0 - NMETA
        combine = (dirn == 0) and not meta
        lc = 0 if rv else CL - 1
        qt, kqt, kt, kkt, kh, kkh = st['qt'], st['kqt'], st['kt'], st['kkt'], st['kh'], st['kkh']
        vb, kvb, eb, keb = st['vb'], st['kvb'], st['eb'], st['keb']
        ps_o, kpo = ch_o[chain]
        mbank, kmb = ch_m[chain]
        if combine:
            ob, kob = t_ob[chain].next()
            dma('sp', ob[:, 0:N], ob_scr[h, :, pr0:pr0 + N], [], [kob])
        msc = mscR if rv else mscF
        msck = 'mscR' if rv else 'mscF'
        sub_order = range(nsub - 1, -1, -1) if rv else range(nsub)
        for s in sub_order:
            npk = min(128, N - 128 * s)
            t0 = 128 * s
            ps_kT, kkT = mbank[:, 256:320].bitcast(BF16), kmb
            P.add('pe', lambda e, ps_kT=ps_kT, t0=t0, npk=npk: e.transpose(
                out=ps_kT[:npk, 0:128], in_=kh[:, t0:t0 + npk], identity=ident), [kkh, 'ident'], [kkT])
            khT, kkhT = khT4[chain][s], ('khT4', chain, s)
            P.add('act', lambda e, khT=khT, ps_kT=ps_kT, npk=npk: e.activation(
                out=khT[:npk, :], in_=ps_kT[:npk, 0:128], func=AF.Copy), [kkT], [kkhT])
            ps_sc, ksc = mbank[:, 0:128], kmb
            P.add('pe', lambda e, ps_sc=ps_sc, t0=t0, npk=npk: e.matmul(
                ps_sc[:npk, 0:npk], lhsT=kt[:, t0:t0 + npk], rhs=qt[:, t0:t0 + npk], start=True, stop=True),
                [kkt, kqt], [ksc])
            yield
            PT, kPT = PT4[chain][s], ('PT4', chain, s)
            P.add('dve', lambda e, PT=PT, ps_sc=ps_sc, npk=npk: e.tensor_tensor(
                out=PT[:npk, 0:npk], in0=ps_sc[:npk, 0:npk], in1=msc[:npk, 0:npk], op=ALU.mult),
                [ksc, msck], [kPT])
            yield
            ncs = npk // CL
            ch_order = range(ncs - 1, -1, -1) if rv else range(ncs)
            for c in ch_order:
                c0 = c * CL

                def mm_o(e, PT=PT, s=s, t0=t0, c0=c0, npk=npk):
                    e.matmul(ps_o[:, t0 + c0:t0 + c0 + CL], lhsT=vb[:npk, s * 128:(s + 1) * 128],
                             rhs=PT[:npk, c0:c0 + CL], start=True, stop=False)
                    return e.matmul(ps_o[:, t0 + c0:t0 + c0 + CL], lhsT=S_b[:, h, :],
                                    rhs=qt[:, t0 + c0:t0 + c0 + CL], start=False, stop=True)
                P.add('pe', mm_o, [kPT, kvb, kqt, ('Sb', h)], [kpo])
                ps_dS, kdS = mbank[:, 128:256], kmb
                P.add('pe', lambda e, ps_dS=ps_dS, khT=khT, s=s, c0=c0: e.matmul(
                    ps_dS[:, 0:128], lhsT=khT[c0:c0 + CL, :], rhs=vb[c0:c0 + CL, s * 128:(s + 1) * 128],
                    start=True, stop=True), [kkhT, kvb], [kdS])
                yield
                dcol = t0 + c0 + lc
                P.add('dve', lambda e, ps_dS=ps_dS, dcol=dcol: e.scalar_tensor_tensor(
                    out=S_b[:, h, :], in0=S_f[:, h, :], scalar=eb[:, dcol:dcol + 1], in1=ps_dS[:, 0:128],
                    op0=ALU.mult, op1=ALU.subtract), [kdS, keb, ('Sf', h)], [('Sb', h)])
                P.add('dve', lambda e, ps_dS=ps_dS, dcol=dcol: e.scalar_tensor_tensor(
                    out=S_f[:, h, :], in0=S_f[:, h, :], scalar=eb[:, dcol:dcol + 1], in1=ps_dS[:, 0:128],
                    op0=ALU.mult, op1=ALU.subtract), [kdS, keb, ('Sf', h)], [('Sf', h)])
                yield
        if dirn == 1:
            ob, kob = t_ob[chain].next()
            P.add('act', lambda e: e.activation(out=ob[:, 0:N], in_=ps_o[:, 0:N], func=AF.Copy), [kpo], [kob])
            dma('sp', ob_scr[h, :, pr0:pr0 + N], ob[:, 0:N], [kob], [])
            yield
        elif combine:
            osm, kos = t_os[chain].next()
            P.add('dve', lambda e: e.tensor_tensor(out=osm[:, 0:N], in0=ps_o[:, 0:N], in1=ob[:, 0:N], op=ALU.add),
                  [kpo, kob], [kos])
            yield
            osq, kosq = t_osq[chain].next()
            P.add('act', lambda e: e.activation(out=osq[:, 0:N], in_=osm[:, 0:N], func=AF.Square), [kos], [kosq])
            yield
            ps_ss, kpss = mbank, kmb
            P.add('pe', lambda e: e.matmul(ps_ss[:, 0:N], lhsT=ones_f, rhs=osq[:, 0:N], start=True, stop=True),
                  [kosq, 'ones'], [kpss])
            yield
            rso, krso = t_rso[chain].next()
            P.add('act', lambda e: e.activation(out=rso[:, 0:N], in_=ps_ss[:, 0:N], func=AF.Ln,
                                                bias=col(dv, DV_EPS128)), [kpss, 'dv'], [krso])
            yield
            P.add('act', lambda e: e.activation(out=rso[:, 0:N], in_=rso[:, 0:N], func=AF.Exp, scale=-0.5),
                  [krso], [krso])
            yield
            P.add('pool', lambda e: e.tensor_tensor(out=osm[:, 0:N], in0=osm[:, 0:N], in1=rso[:, 0:N], op=ALU.mult),
                  [kos, krso], [kos])
            dma('sp', on_scr[h, :, pr0:pr0 + N], osm[:, 0:N], [kos], [])
            yield

    def interleave(*gens):
        gens = [g for g in gens if g is not None]
        while gens:
            for g in list(gens):
                try:
                    next(g)
                except StopIteration:
                    gens.remove(g)

    def stage_x_gen(*a, **k):
        stage_x(*a, **k)
        yield

    def init_states():
        P.add('pool', lambda e: e.memset(carryA, 0.0), [], [('carryA', h) for h in range(8)])
        P.add('pool', lambda e: e.memset(S_f.rearrange("p a b -> p (a b)"), 0.0), [], [('Sf', h) for h in range(8)])
        P.add('pool', lambda e: e.memset(S_b.rearrange("p a b -> p (a b)"), 0.0), [], [('Sb', h) for h in range(8)])

    def run_mixer_pass(dirn, tiles):
        hT, hT_key = hT_bufs[0], ('hT', 0)
        dma('pool', wg.rearrange("p g h j -> p (g h j)"), w_gate[:, dirn * 2048:(dirn + 1) * 2048], [], ['wg'])
        dma('pool', wmix[GF].rearrange("p a b -> p (a b)"), w_mix[3 if dirn == 1 else 2], [], [('wmix', GF)])

        def do_x(p0, N):
            if dirn == 0 and p0 > 0:
                dma('sp', hT[:, :, 0:N], hT_scr[(p0 - NMETA) // TN], [], [hT_key])
                return
            nsub = (N + 127) // 128
            xl = [xt_ring.next() for _ in range(nsub)]
            pb, pbk = gen_ring.next()
            stage_x(p0, N, xl, hT, hT_key, PP_GMIX, True, pb, pbk)
            if dirn == 1:
                dma('sp', hT_scr[(p0 - NMETA) // TN], hT[:, :, 0:N], [hT_key], [])

        def s1_pair(p0, N, meta, h0, sts, with_x):
            if with_x:
                do_x(p0, N)
                yield
            yield from stage1_pair(dirn, p0, N, hT, hT_key, meta, h0, sts)
        p0, N, meta = tiles[0]
        sts = [{}, {}]
        interleave(s1_pair(p0, N, meta, 0, sts, True))
        for ti, (p0, N, meta) in enumerate(tiles):
            for pr in range(4):
                sts_next = [{}, {}]
                if pr < 3:
                    nxt = s1_pair(p0, N, meta, 2 * pr + 2, sts_next, False)
                elif ti + 1 < len(tiles):
                    nxt = s1_pair(*tiles[ti + 1], 0, sts_next, True)
                else:
                    nxt = None
                interleave(stage2(dirn, p0, N, meta, 2 * pr, sts[0], 0),
                           stage2(dirn, p0, N, meta, 2 * pr + 1, sts[1], 1), nxt)
                sts = sts_next

    init_states()
    if stop_after >= 1:
        run_mixer_pass(1, [(NMETA + ti * TN, TN, False) for ti in range(NT - 1, -1, -1)])
    P.barrier()
    init_states()
    if stop_after >= 2:
        run_mixer_pass(0, [(0, NMETA, True)] + [(NMETA + ti * TN, TN, False) for ti in range(NT)])
    P.barrier()

    A.reset(base_mark)
    wout_sb = A.alloc(8 * 1024, BF16, shape3=1024)
    gfin = A.alloc(1024, F32)
    dma('sp', gfin, gfin_d, [], ['gfin'])
    P.add('pool', lambda e: e.tensor_scalar(out=gfin, in0=gfin, scalar1=32.0, scalar2=None, op0=ALU.mult),
          ['gfin'], ['gfin'])
    dma('pool', wout_sb.rearrange("p a b -> p (a b)"), w_out, [], ['wout'])
    xt4 = [A.alloc(1024, F32) for _ in range(4)]
    xn_ring = Ring([A.alloc(1024, BF16) for _ in range(2)], 'xn2')
    hT2 = A.alloc(8 * TN, BF16, shape3=TN)
    hs_ring = tmp(name='hsl')
    on_ring = tmp(name='onl')
    braT = A.alloc(8 * TN, BF16, shape3=TN)
    brbT = A.alloc(8 * TN, BF16, shape3=TN)
    mrgT = A.alloc(8 * TN, BF16, shape3=TN)
    h2T = A.alloc(8 * TN, BF16, shape3=TN)
    actT = A.alloc(32 * TN, BF16, shape3=TN)
    wring = Ring([A.alloc(1024, BF16, shape3=128) for _ in range(8)], 'wch')
    w2ring = Ring([A.alloc(1024, BF16) for _ in range(6)], 'w2ch')
    t_e = tmp(name='e')
    t_t = tmp(cnt=1, name='t')
    t_so = tmp(name='so')
    t_t2 = tmp(cnt=1, name='t2')
    t_sga = tmp(name='sga')
    t_sgb = tmp(name='sgb')
    t_m1 = tmp(cnt=1, name='m1')
    t_m2 = tmp(cnt=1, name='m2')
    t_rl = tmp(dt=BF16, name='rl')
    gen2 = Ring(banks[0:4], 'psg2')
    acc_keys = [('psacc', i) for i in range(4)]
    acc = banks[4:8]

    def wchunk(cid):
        w, wk = wring.next()
        dma('sp', w.rearrange("p a b -> p (a b)"), wf2_bf[cid], [('wf2', cid // 8 * 8)], [wk])
        return w, wk

    def proj(w, wk, src, src_key, ps, psk):
        def mm(e):
            ins = None
            for kc in range(8):
                ins = e.matmul(ps[:, 0:TN], lhsT=w[:, kc, :], rhs=src[:, kc, 0:TN], start=(kc == 0), stop=(kc == 7))
            return ins
        P.add('pe', mm, [wk, src_key], [psk])

    for ti in range(NT if stop_after >= 3 else 0):
        p0 = NMETA + ti * TN
        pr0 = ti * TN
        dma('sp', hT2, hT_scr[ti], [], ['hT2'])
        for s_ in range(4):
            dma('sp', xt4[s_], xs[p0 + 128 * s_:p0 + 128 * (s_ + 1), :], [], [('xt4', s_)])
        for j in range(8):
            w, wk = wchunk(0 + j)
            ps, psk = gen2.next()
            proj(w, wk, hT2, 'hT2', ps, psk)
            e_, ke = t_e.next()
            P.add('act', lambda e, e_=e_, ps=ps: e.activation(out=e_, in_=ps[:, 0:TN], func=AF.Gelu), [psk], [ke])
            hsl, khsl = hs_ring.next()
            dma('sp', hsl, hs_scr[j, :, pr0:pr0 + TN], [], [khsl])
            P.add('pool', lambda e, e_=e_, j=j, hsl=hsl: e.tensor_tensor(
                out=braT[:, j, :], in0=hsl, in1=e_, op=ALU.mult), [khsl, ke], [('braT', j)])
        for j in range(8):
            w, wk = wchunk(8 + j)
            ps, psk = gen2.next()
            proj(w, wk, hT2, 'hT2', ps, psk)
            so, kso = t_so.next()
            P.add('act', lambda e, so=so, ps=ps: e.activation(out=so, in_=ps[:, 0:TN], func=AF.Sigmoid),
                  [psk], [kso])
            t2, kt2 = t_t2.next()
            P.add('dve', lambda e, t2=t2, so=so, ps=ps, j=j: e.scalar_tensor_tensor(
                out=t2, in0=so, scalar=col(dv, DV_HGG + j), in1=ps[:, 0:TN], op0=ALU.mult, op1=ALU.mult),
                [kso, psk, 'dv'], [kt2])
            onl, konl = on_ring.next()
            dma('sp', onl, on_scr[j, :, pr0:pr0 + TN], [], [konl])
            P.add('pool', lambda e, t2=t2, j=j, onl=onl: e.tensor_tensor(
                out=brbT[:, j, :], in0=onl, in1=t2, op=ALU.mult), [konl, kt2], [('brbT', j)])
        bra_keys = [('braT', j) for j in range(8)]
        brb_keys = [('brbT', j) for j in range(8)]
        for j in range(8):
            w, wk = wchunk(16 + j)
            ps_ga, kga = gen2.next()
            proj(w, wk, hT2, 'hT2', ps_ga, kga)
            sga, ksga = t_sga.next()
            P.add('act', lambda e, sga=sga, ps_ga=ps_ga: e.activation(out=sga, in_=ps_ga[:, 0:TN], func=AF.Sigmoid),
                  [kga], [ksga])
            w, wk = wchunk(24 + j)
            ps_gb, kgb = gen2.next()
            proj(w, wk, hT2, 'hT2', ps_gb, kgb)
            sgb, ksgb = t_sgb.next()
            P.add('act', lambda e, sgb=sgb, ps_gb=ps_gb: e.activation(out=sgb, in_=ps_gb[:, 0:TN], func=AF.Sigmoid),
                  [kgb], [ksgb])
            w, wk = wchunk(32 + j)
            ps_pa, kpa = gen2.next()

            def mm_pa(e, w=w, ps_pa=ps_pa):
                ins = None
                for kc in range(8):
                    ins = e.matmul(ps_pa[:, 0:TN], lhsT=w[:, kc, :], rhs=braT[:, kc, :], start=(kc == 0), stop=(kc == 7))
                return ins
            P.add('pe', mm_pa, [wk] + bra_keys, [kpa])
            m1, km1 = t_m1.next()
            P.add('dve', lambda e, m1=m1, ps_pa=ps_pa, sga=sga: e.tensor_tensor(
                out=m1, in0=ps_pa[:, 0:TN], in1=sga, op=ALU.mult), [kpa, ksga], [km1])
            w, wk = wchunk(40 + j)
            ps_pb, kpb = gen2.next()

            def mm_pb(e, w=w, ps_pb=ps_pb):
                ins = None
                for kc in range(8):
                    ins = e.matmul(ps_pb[:, 0:TN], lhsT=w[:, kc, :], rhs=brbT[:, kc, :], start=(kc == 0), stop=(kc == 7))
                return ins
            P.add('pe', mm_pb, [wk] + brb_keys, [kpb])
            m2, km2 = t_m2.next()
            P.add('dve', lambda e, m2=m2, ps_pb=ps_pb, sgb=sgb: e.tensor_tensor(
                out=m2, in0=ps_pb[:, 0:TN], in1=sgb, op=ALU.mult), [kpb, ksgb], [km2])
            P.add('pool', lambda e, m1=m1, m2=m2, j=j: e.tensor_tensor(
                out=mrgT[:, j, :], in0=m1, in1=m2, op=ALU.add), [km1, km2], [('mrgT', j)])
        mrg_keys = [('mrgT', j) for j in range(8)]
        for s in range(4):
            for hf in range(2):
                pa_, pak = acc[(s * 2 + hf) % 4], acc_keys[(s * 2 + hf) % 4]

                def mm_wo(e, s=s, hf=hf, pa_=pa_):
                    ins = None
                    for kc in range(8):
                        ins = e.matmul(pa_[:, 0:512], lhsT=mrgT[:, kc, s * 128:(s + 1) * 128],
                                       rhs=wout_sb[:, kc, hf * 512:(hf + 1) * 512], start=(kc == 0), stop=(kc == 7))
                    return ins
                P.add('pe', mm_wo, mrg_keys + ['wout'], [pak])
                P.add('dve', lambda e, s=s, hf=hf, pa_=pa_: e.tensor_tensor(
                    out=xt4[s][:, hf * 512:(hf + 1) * 512], in0=pa_[:, 0:512],
                    in1=xt4[s][:, hf * 512:(hf + 1) * 512], op=ALU.add), [pak, ('xt4', s)], [('xt4', s)])
        for s in range(4):
            xt, xk = xt4[s], ('xt4', s)
            P.add('pool', lambda e, s=s: e.memset(col(ss, s), 0.0), [], [('ss', s)])
            P.add('act', lambda e, xt=xt, s=s: e.activation(out=junk, in_=xt, func=AF.Square,
                                                            accum_out=ss[:, s:s + 1]), [xk, ('ss', s)],
                  ['junk', ('ss', s)])
            P.add('act', lambda e, s=s: e.activation(out=rs[:, s:s + 1], in_=ss[:, s:s + 1], func=AF.Ln, bias=dv[:, DV_EPSD:DV_EPSD + 1]), [('ss', s), 'dv'], [('rs', s)])
            P.add('act', lambda e, s=s: e.activation(out=rs[:, s:s + 1], in_=rs[:, s:s + 1], func=AF.Exp, scale=-0.5), [('rs', s)], [('rs', s)])
            xn, xnk = xn_ring.next()
            P.add('dve', lambda e, xt=xt, xn=xn, s=s: e.tensor_scalar(
                out=xn, in0=xt, scalar1=rs[:, s:s + 1], scalar2=32.0, op0=ALU.mult, op1=ALU.mult),
                [xk, ('rs', s)], [xnk])
            pb, pbk = gen2.next()
            psT = pb.bitcast(BF16).rearrange("p (a b) -> p a b", b=128)

            def tr2(e, xn=xn, psT=psT):
                ins = None
                for kc in range(8):
                    ins = e.transpose(out=psT[:, kc, :], in_=xn[:, kc * 128:(kc + 1) * 128], identity=ident)
                return ins
            P.add('pe', tr2, [xnk, 'ident'], [pbk])
            P.add('dve', lambda e, psT=psT, s=s: e.tensor_tensor(
                out=h2T[:, :, s * 128:(s + 1) * 128], in0=psT,
                in1=pp[:, PP_GMLP:PP_GMLP + 8].unsqueeze(2).broadcast_to([128, 8, 128]), op=ALU.mult),
                [pbk, 'pp'], ['h2T'])
        for m in range(32):
            w, wk = wchunk(48 + m)
            ps, psk = gen2.next()
            proj(w, wk, h2T, 'h2T', ps, psk)
            rl, krl = t_rl.next()
            P.add('act', lambda e, rl=rl, ps=ps: e.activation(out=rl, in_=ps[:, 0:TN], func=AF.Relu), [psk], [krl])
            P.add('dve' if m % 2 == 0 else 'pool', lambda e, rl=rl, m=m: e.tensor_tensor(
                out=actT[:, m, :], in0=rl, in1=rl, op=ALU.mult), [krl], [('actT', m)])
        for hf in range(2):
            for mp in range(16):
                w2c, w2k = w2ring.next()
                q = hf * 16 + mp
                dma('sp', w2c, w2_bf[q], [('w2s', q // 8 * 8)], [w2k])
                for mm in range(2):
                    m = 2 * mp + mm

                    def mm_2(e, m=m, mm=mm, w2c=w2c):
                        ins = None
                        for s in range(4):
                            ins = e.matmul(acc[s][:, 0:512], lhsT=actT[:, m, s * 128:(s + 1) * 128],
                                           rhs=w2c[:, mm * 512:(mm + 1) * 512], start=(m == 0), stop=(m == 31))
                        return ins
                    P.add('pe', mm_2, [w2k, ('actT', m)], acc_keys)
            for s in range(4):
                P.add('dve', lambda e, s=s, hf=hf: e.tensor_tensor(
                    out=xt4[s][:, hf * 512:(hf + 1) * 512], in0=acc[s][:, 0:512],
                    in1=xt4[s][:, hf * 512:(hf + 1) * 512], op=ALU.add), [acc_keys[s], ('xt4', s)], [('xt4', s)])
        for s in range(4):
            xt, xk = xt4[s], ('xt4', s)
            P.add('pool', lambda e, s=s: e.memset(col(ss, 4 + s % 2), 0.0), [], [('ss', 4 + s % 2)])
            P.add('act', lambda e, xt=xt, s=s: e.activation(out=junk, in_=xt, func=AF.Square,
                                                            accum_out=ss[:, 4 + s % 2:5 + s % 2]),
                  [xk, ('ss', 4 + s % 2)], ['junk', ('ss', 4 + s % 2)])
            P.add('act', lambda e, s=s: e.activation(out=rs[:, 4 + s % 2:5 + s % 2], in_=ss[:, 4 + s % 2:5 + s % 2], func=AF.Ln, bias=dv[:, DV_EPSD:DV_EPSD + 1]), [('ss', 4 + s % 2), 'dv'], [('rs', 4 + s % 2)])
            P.add('act', lambda e, s=s: e.activation(out=rs[:, 4 + s % 2:5 + s % 2], in_=rs[:, 4 + s % 2:5 + s % 2], func=AF.Exp, scale=-0.5), [('rs', 4 + s % 2)], [('rs', 4 + s % 2)])
            P.add('dve', lambda e, xt=xt, s=s: e.scalar_tensor_tensor(
                out=xt, in0=xt, scalar=rs[:, 4 + s % 2:5 + s % 2], in1=gfin, op0=ALU.mult, op1=ALU.mult),
                [xk, ('rs', 4 + s % 2), 'gfin'], [xk])
            dma('sp', y_d[pr0 + s * 128:pr0 + (s + 1) * 128, :], xt, [xk], [('yout', ti, s)])
    P.barrier()

    with (nc.semaphore("s_pe") as s_pe, nc.semaphore("s_act") as s_act, nc.semaphore("s_dve") as s_dve,
          nc.semaphore("s_pool") as s_pool):
        import contextlib
        with contextlib.ExitStack() as st:
            dsems = dict(sp=[st.enter_context(nc.semaphore(f"s_dma{i}")) for i in range(NDSEM)],
                         pool=[st.enter_context(nc.semaphore(f"s_dmap{i}")) for i in range(8)])
            sems = dict(pe=s_pe, act=s_act, dve=s_dve, pool=s_pool)
            P.finalize(nc, sems, dsems)
            with nc.Block() as block:
                @block.sync
                def _(e):
                    P.emit('sp', e)

                @block.tensor
                def _(e):
                    P.emit('pe', e)

                @block.scalar
                def _(e):
                    P.emit('act', e)

                @block.vector
                def _(e):
                    P.emit('dve', e)

                @block.gpsimd
                def _(e):
                    P.emit('pool', e)
    return nc


def _pack_params(conv_w, conv_b, rg_ba, rg_bx, rg_lambda, hg_lb_logits, hg_norm_g, norm_mix_g, norm_mlp_g):
    pp = np.zeros((128, PP_N), np.float32)
    for h in range(8):
        sl = slice(h * 128, (h + 1) * 128)
        b0 = h * PP_HEAD
        for j in range(4):
            pp[:, b0 + j] = conv_w[0, j, sl]
        pp[:, b0 + 4] = conv_b[0, sl]
        for dr in range(2):
            pp[:, b0 + 5 + dr] = rg_ba[0, dr, sl]
            pp[:, b0 + 7 + dr] = rg_bx[0, dr, sl]
            pp[:, b0 + 9 + dr] = rg_lambda[0, dr, sl]
            pp[:, b0 + 11 + dr] = hg_lb_logits[0, dr, sl]
            pp[:, b0 + 13 + dr] = hg_lb_logits[1, dr, sl]
        pp[:, b0 + 15] = hg_norm_g[0, sl]
    for kc in range(8):
        pp[:, PP_GMIX + kc] = norm_mix_g[0, kc * 128:(kc + 1) * 128]
        pp[:, PP_GMLP + kc] = norm_mlp_g[0, kc * 128:(kc + 1) * 128]
    return pp


def _consts():
    c = np.zeros((128, 128 * 4 + 1024), np.float32)
    c[:, 0:128] = np.eye(128, dtype=np.float32)
    c[:, 128:256] = 1.0
    s = np.arange(128)[:, None]
    t = np.arange(128)[None, :]
    same = (s // 64) == (t // 64)
    c[:, 256:384] = -1.0 * (same & (s <= t))
    c[:, 384:512] = -1.0 * (same & (s >= t))
    mF = np.ones(512, np.float32)
    mF[0::64] = 0.0
    mR = np.ones(512, np.float32)
    mR[63::64] = 0.0
    c[:, 512:1024] = mF[None, :]
    c[:, 1024:1536] = mR[None, :]
    return c


def _kc_layout(w):
    return np.ascontiguousarray(w.reshape(8, 128, -1).transpose(1, 0, 2))


def kernel(x_prompt, x_sample, meta_tokens, hg_lb_logits, norm_mix_g, w_in, conv_w, conv_b, rg_wa, rg_ba,
           rg_wx, rg_bx, rg_lambda, hg_norm_g, w_branch_a, w_branch_b, w_out, norm_mlp_g, w_mlp1, w_mlp2,
           final_norm_g):
    f = lambda a: np.asarray(a, dtype=np.float32)
    x_prompt, x_sample, meta_tokens = f(x_prompt), f(x_sample), f(meta_tokens)
    T = x_prompt.shape[1]
    NT = T // TN
    seqs = [x_prompt[i] for i in range(x_prompt.shape[0])] + [x_sample[i] for i in range(x_sample.shape[0])]
    assert len(seqs) <= NCORES
    win = f(w_in)[0]
    grp = lambda g: win[:, g * 1024:(g + 1) * 1024]
    w_mix = np.stack([_kc_layout(grp(g)).reshape(128, 8 * 1024) for g in (0, 2, 3, 4, 5)])
    wa, wx = f(rg_wa)[0], f(rg_wx)[0]
    wgate = np.stack([wa[0], wx[0], wa[1], wx[1]])
    wgate = np.ascontiguousarray(wgate.transpose(2, 0, 1, 3)).reshape(128, 4 * 8 * 128)

    def chunks(w):
        k = _kc_layout(w)
        C = k.shape[2]
        return np.ascontiguousarray(k.reshape(128, 8, C // 128, 128).transpose(2, 0, 1, 3)).reshape(C // 128, 128, 1024)
    w_f2 = np.concatenate([chunks(grp(1)), chunks(grp(6)), chunks(grp(7)), chunks(grp(8)),
                           chunks(f(w_branch_a)[0]), chunks(f(w_branch_b)[0]), chunks(f(w_mlp1)[0])], axis=0)
    wout_l = _kc_layout(f(w_out)[0]).reshape(128, 8 * 1024)
    w2_l = np.ascontiguousarray(f(w_mlp2)[0].reshape(16, 2, 128, 2, 512).transpose(3, 0, 2, 1, 4)).reshape(32, 128, 1024)
    pp = _pack_params(f(conv_w), f(conv_b), f(rg_ba), f(rg_bx), f(rg_lambda), f(hg_lb_logits), f(hg_norm_g),
                      f(norm_mix_g), f(norm_mlp_g))
    gfin = np.ascontiguousarray(np.broadcast_to(f(final_norm_g)[None, :], (128, D)))
    consts = _consts()
    shared = dict(w_mix=w_mix, w_gate=wgate, w_f2=w_f2, w_out=wout_l, w_2=w2_l, pp=pp, gfin=gfin, consts=consts)
    in_maps = []
    for c in range(NCORES):
        if c < len(seqs):
            xs = np.concatenate([meta_tokens, seqs[c]], axis=0)
        else:
            xs = np.zeros((T + NMETA, D), np.float32)
        m = dict(shared)
        m["xs"] = np.ascontiguousarray(xs)
        in_maps.append(m)
    nc = build(NT)
    res = run_bass_kernel_spmd(nc, in_maps, core_ids=list(range(NCORES)))
    outs = [np.asarray(res.results[c]["y"], dtype=np.float32) for c in range(len(seqs))]
    nb = x_prompt.shape[0]
    y_prompt = np.stack(outs[:nb])
    y_sample = np.stack(outs[nb:])
    return (y_prompt, y_sample)
```

```python
import numpy as np
import concourse.bass as bass
import concourse.mybir as mybir
from concourse.bass_utils import run_bass_kernel_spmd
from concourse.ap import AP

F32 = mybir.dt.float32
BF16 = mybir.dt.bfloat16
U8 = mybir.dt.uint8
ALU = mybir.AluOpType
AF = mybir.ActivationFunctionType

D = 1024
NMETA = 16
TN = 512
EPS = 1e-6
RG_C = 8.0
NCORES = 8
NDSEM = 24

PP_HEAD = 16
PP_GMIX = 128
PP_GMLP = 136
PP_N = 144
DV_SP = 0
DV_SP2 = 16
DV_LB = 32
DV_OML = 48
DV_HGG = 64
DV_EPS128 = 72
DV_EPSD = 73
DV_TINY = 74
DV_N = 80


def rev_ap(a):
    aps = [list(x) for x in a.ap]
    step, cnt = aps[-1]
    aps[-1] = [-step, cnt]
    return AP(a.tensor, a.offset + step * (cnt - 1), aps)


class Prog:
    def __init__(self):
        self.ops = []
        self.last_w = {}
        self.readers = {}
        self.last_barrier = 0

    def add(self, eng, fn, reads=(), writes=(), dma=False):
        i = len(self.ops)
        deps = set()
        for r in reads:
            if r in self.last_w:
                deps.add(self.last_w[r])
        for w in writes:
            if w in self.last_w:
                deps.add(self.last_w[w])
            deps.update(self.readers.get(w, ()))
        for r in reads:
            self.readers.setdefault(r, []).append(i)
        for w in writes:
            self.last_w[w] = i
            self.readers[w] = []
        self.ops.append(dict(eng=eng, fn=fn, deps=deps, dma=dma, sig=False))
        return i

    def barrier(self):
        n = len(self.ops)
        deps = set()
        last = {}
        for i in range(self.last_barrier, n):
            op = self.ops[i]
            if op['dma']:
                deps.add(i)
            elif op['fn'] is not None:
                last[op['eng']] = i
        deps.update(last.values())
        for e in ('pe', 'act', 'dve', 'pool', 'sp'):
            self.ops.append(dict(eng=e, fn=None, deps=set(deps), dma=False, sig=False))
        self.last_barrier = len(self.ops)
        self.last_w = {}
        self.readers = {}

    def finalize(self, nc, sems, dsems):
        ops = self.ops
        for q, ring in dsems.items():
            dma_idx = [i for i, o in enumerate(ops) if o['dma'] and o['eng'] == q]
            nr = len(ring)
            for j, i in enumerate(dma_idx):
                ops[i]['sem'] = ring[j % nr]
                ops[i]['val'] = 16 * (j // nr + 1)
                ops[i]['sig'] = True
                if j >= nr:
                    ops[i]['deps'].add(dma_idx[j - nr])
        for i, o in enumerate(ops):
            nd = set()
            for d in o['deps']:
                od = ops[d]
                if od['fn'] is None:
                    continue
                if (not od['dma']) and od['eng'] == o['eng'] and o['eng'] == 'pe' and not o['dma']:
                    continue
                nd.add(d)
                od['sig'] = True
            o['deps'] = nd
        cnt = {}
        for o in ops:
            if o['dma'] or o['fn'] is None:
                continue
            if o['sig']:
                cnt[o['eng']] = cnt.get(o['eng'], 0) + 1
                o['sem'] = sems[o['eng']]
                o['val'] = cnt[o['eng']]
        self.n_sig = cnt
        last_k = {}
        self.n_waits = 0
        for o in ops:
            K = dict(last_k.get(o['eng'], ()))
            waits = []
            for d in sorted(o['deps'], reverse=True):
                od = ops[d]
                s, v = od['sem'], od['val']
                if K.get(id(s), 0) >= v:
                    continue
                waits.append((s, v))
                K[id(s)] = v
                for k2, v2 in od['K'].items():
                    if K.get(k2, 0) < v2:
                        K[k2] = v2
            best = {}
            for s, v in waits:
                if id(s) not in best or best[id(s)][1] < v:
                    best[id(s)] = (s, v)
            o['waits'] = list(best.values())
            self.n_waits += len(o['waits'])
            o['K'] = K
            last_k[o['eng']] = K

    def emit(self, eng_name, e):
        for o in self.ops:
            if o['eng'] != eng_name:
                continue
            for s, v in o['waits']:
                e.wait_ge(s, v)
            if o['fn'] is None:
                continue
            ins = o['fn'](e)
            if o['sig']:
                ins.then_inc(o['sem'], 16 if o['dma'] else 1)


class Arena:
    def __init__(self, base_ap, nbytes):
        self.base = base_ap
        self.nbytes = nbytes
        self.off = 0
        self.mark_ = 0

    def alloc(self, free_elems, dtype, shape3=None):
        sz = free_elems * (4 if dtype == F32 else 2)
        sz_al = (sz + 31) // 32 * 32
        assert self.off + sz_al <= self.nbytes, f"SBUF arena overflow {self.off + sz_al} > {self.nbytes}"
        a = self.base[:, self.off:self.off + sz].bitcast(dtype)
        self.off += sz_al
        if shape3 is not None:
            a = a.rearrange("p (a b) -> p a b", b=shape3)
        return a

    def mark(self):
        return self.off

    def reset(self, m):
        self.off = m


class Ring:
    def __init__(self, bufs, name, keys=None):
        self.bufs = bufs
        self.name = name
        self.keys = keys
        self.i = 0

    def next(self):
        k = self.i % len(self.bufs)
        self.i += 1
        return self.bufs[k], (self.keys[k] if self.keys else (self.name, k))


def build(NT, stop_after=3):
    T = NT * TN
    L = T + NMETA
    nc = bass.Bass("TRN2", target_bir_lowering=False)
    P = Prog()

    xs = nc.dram_tensor("xs", [L, D], F32, kind="ExternalInput").ap()
    w_mix = nc.dram_tensor("w_mix", [5, 128, 8 * 1024], F32, kind="ExternalInput").ap()
    w_gate = nc.dram_tensor("w_gate", [128, 4 * 8 * 128], F32, kind="ExternalInput").ap()
    w_f2 = nc.dram_tensor("w_f2", [80, 128, 1024], F32, kind="ExternalInput").ap()
    w_out = nc.dram_tensor("w_out", [128, 8 * 1024], F32, kind="ExternalInput").ap()
    w_2 = nc.dram_tensor("w_2", [32, 128, 1024], F32, kind="ExternalInput").ap()
    pp_d = nc.dram_tensor("pp", [128, PP_N], F32, kind="ExternalInput").ap()
    gfin_d = nc.dram_tensor("gfin", [128, D], F32, kind="ExternalInput").ap()
    consts_d = nc.dram_tensor("consts", [128, 128 * 4 + 512 * 2], F32, kind="ExternalInput").ap()
    y_d = nc.dram_tensor("y", [T, D], F32, kind="ExternalOutput").ap()

    hb_scr = nc.dram_tensor("hb_scr", [8, 128, T], F32, kind="Internal").ap()
    ob_scr = nc.dram_tensor("ob_scr", [8, 128, T], F32, kind="Internal").ap()
    hs_scr = nc.dram_tensor("hs_scr", [8, 128, T], F32, kind="Internal").ap()
    on_scr = nc.dram_tensor("on_scr", [8, 128, T], F32, kind="Internal").ap()
    xc_scr = nc.dram_tensor("xc_scr", [8, 128, T], F32, kind="Internal").ap()
    qs_scr = nc.dram_tensor("qs_scr", [8, 128, T], F32, kind="Internal").ap()
    xcb_scr = nc.dram_tensor("xcb_scr", [8, 128, T], BF16, kind="Internal").ap()
    vb_scr = nc.dram_tensor("vb_scr", [8, NT, 128, TN], BF16, kind="Internal").ap()
    hT_scr = nc.dram_tensor("hT_scr", [NT, 128, 8, TN], BF16, kind="Internal").ap()
    wf2_bf = nc.dram_tensor("wf2_bf", [80, 128, 1024], BF16, kind="Internal").ap()
    w2_bf = nc.dram_tensor("w2_bf", [32, 128, 1024], BF16, kind="Internal").ap()

    ARENA_BYTES = 206 * 1024
    arena_t = nc.alloc_sbuf_tensor("arena", [128, ARENA_BYTES], U8).ap()
    A = Arena(arena_t, ARENA_BYTES)
    banks = [nc.alloc_psum_tensor(f"bank{i}", [128, 512], F32).ap() for i in range(8)]

    pp = A.alloc(PP_N, F32)
    dv = A.alloc(DV_N, F32)
    ident = A.alloc(128, BF16)
    ones_f = A.alloc(128, F32)
    mscF = A.alloc(128, F32)
    mscR = A.alloc(128, F32)
    maskF = A.alloc(512, F32)
    maskR = A.alloc(512, F32)
    junk = A.alloc(1024, BF16)
    ss = A.alloc(8, F32)
    rs = A.alloc(8, F32)
    base_mark = A.mark()
    ctmp = A.alloc(128 * 4 + 1024, F32)

    def col(t, c):
        return t[:, c:c + 1]

    def dma(q, out, in_, reads, writes):
        return P.add(q, lambda e: e.dma_start(out=out, in_=in_), reads, writes, dma=True)

    dma('sp', pp, pp_d, [], ['pp'])
    dma('sp', ctmp, consts_d, [], ['ctmp'])
    P.add('dve', lambda e: e.tensor_copy(out=ident, in_=ctmp[:, 0:128]), ['ctmp'], ['ident'])
    P.add('dve', lambda e: e.tensor_copy(out=ones_f, in_=ctmp[:, 128:256]), ['ctmp'], ['ones'])
    P.add('dve', lambda e: e.tensor_copy(out=mscF, in_=ctmp[:, 256:384]), ['ctmp'], ['mscF'])
    P.add('dve', lambda e: e.tensor_copy(out=mscR, in_=ctmp[:, 384:512]), ['ctmp'], ['mscR'])
    P.add('dve', lambda e: e.tensor_copy(out=maskF, in_=ctmp[:, 512:1024]), ['ctmp'], ['maskF'])
    P.add('dve', lambda e: e.tensor_copy(out=maskR, in_=ctmp[:, 1024:1536]), ['ctmp'], ['maskR'])
    for c in range(0, 80, 8):
        dma('pool', wf2_bf[c:c + 8], w_f2[c:c + 8], [], [('wf2', c)])
    for c in range(0, 32, 8):
        dma('pool', w2_bf[c:c + 8], w_2[c:c + 8], [], [('w2s', c)])

    dtmp = A.alloc(64, F32)
    for h in range(8):
        b0 = h * PP_HEAD
        for dr in range(2):
            k = dr * 8 + h
            lam = col(pp, b0 + 9 + dr)
            P.add('act', lambda e, lam=lam, k=k: e.activation(out=col(dtmp, k), in_=lam, func=AF.Exp, scale=-1.0),
                  ['pp'], [('dtmp', k)])
            P.add('act', lambda e, k=k: e.activation(out=col(dtmp, k), in_=col(dtmp, k), func=AF.Ln, bias=1.0),
                  [('dtmp', k)], [('dtmp', k)])
            P.add('dve', lambda e, k=k: e.tensor_scalar(out=col(dv, DV_SP + k), in0=col(dtmp, k), scalar1=-RG_C,
                                                        scalar2=None, op0=ALU.mult), [('dtmp', k)], ['dv'])
            P.add('dve', lambda e, k=k: e.tensor_scalar(out=col(dv, DV_SP2 + k), in0=col(dtmp, k),
                                                        scalar1=-2.0 * RG_C, scalar2=None, op0=ALU.mult),
                  [('dtmp', k)], ['dv'])
            l0 = col(pp, b0 + 11 + dr)
            l1 = col(pp, b0 + 13 + dr)
            P.add('dve', lambda e, l0=l0, l1=l1, k=k: e.tensor_tensor(out=col(dtmp, 16 + k), in0=l0, in1=l1,
                                                                     op=ALU.subtract), ['pp'], [('dtmp', 16 + k)])
        P.add('dve', lambda e, h=h, b0=b0: e.tensor_scalar(out=col(dv, DV_HGG + h), in0=col(pp, b0 + 15),
                                                           scalar1=float(np.sqrt(128.0)), scalar2=None,
                                                           op0=ALU.mult), ['pp'], ['dv'])
    P.add('act', lambda e: e.activation(out=dv[:, DV_LB:DV_LB + 16], in_=dtmp[:, 16:32], func=AF.Sigmoid),
          [('dtmp', 16 + k) for k in range(16)], ['dv'])
    P.add('dve', lambda e: e.tensor_scalar(out=dv[:, DV_OML:DV_OML + 16], in0=dv[:, DV_LB:DV_LB + 16],
                                           scalar1=-1.0, scalar2=1.0, op0=ALU.mult, op1=ALU.add), ['dv'], ['dv'])
    P.add('pool', lambda e: e.memset(col(dv, DV_EPS128), float(128.0 * EPS)), [], ['dv'])
    P.add('pool', lambda e: e.memset(col(dv, DV_EPSD), float(D * EPS)), [], ['dv'])
    P.add('pool', lambda e: e.memset(col(dv, DV_TINY), 1e-30), [], ['dv'])
    P.barrier()
    A.reset(base_mark)

    def stage_x(p0, N, xt_list, hT, hT_key, g_col0, halo, psbank, psbank_key, hcol0=0):
        nsub = (N + 127) // 128
        psT = psbank.bitcast(BF16).rearrange("p (a b) -> p a b", b=128)
        gap = pp[:, g_col0:g_col0 + 8]
        for s in range(nsub):
            npk = min(128, N - 128 * s)
            xt, xk = xt_list[s]
            dma('sp', xt[:npk, :], xs[p0 + 128 * s:p0 + 128 * s + npk, :], [], [xk])
            P.add('pool', lambda e, s=s: e.memset(col(ss, s), 0.0), [], [('ss', s)])
            P.add('act', lambda e, xt=xt, npk=npk, s=s: e.activation(
                out=junk[:npk, :], in_=xt[:npk, :], func=AF.Square, accum_out=ss[:npk, s:s + 1]),
                [xk, ('ss', s)], ['junk', ('ss', s)])
            P.add('act', lambda e, npk=npk, s=s: e.activation(out=rs[:npk, s:s + 1], in_=ss[:npk, s:s + 1], func=AF.Ln, bias=dv[:npk, DV_EPSD:DV_EPSD + 1]), [('ss', s), 'dv'], [('rs', s)])
            P.add('act', lambda e, npk=npk, s=s: e.activation(out=rs[:npk, s:s + 1], in_=rs[:npk, s:s + 1], func=AF.Exp, scale=-0.5), [('rs', s)], [('rs', s)])
            xn, xnk = xn_ring.next()
            P.add('dve', lambda e, xt=xt, xn=xn, npk=npk, s=s: e.tensor_scalar(
                out=xn[:npk, :], in0=xt[:npk, :], scalar1=rs[:npk, s:s + 1], scalar2=32.0,
                op0=ALU.mult, op1=ALU.mult), [xk, ('rs', s)], [xnk])

            def tr(e, xn=xn, npk=npk):
                ins = None
                for kc in range(8):
                    ins = e.transpose(out=psT[:, kc, 0:npk], in_=xn[:npk, kc * 128:(kc + 1) * 128],
                                      identity=ident[:npk, :npk])
                return ins
            P.add('pe', tr, [xnk, 'ident'], [psbank_key])
            c0 = hcol0 + 128 * s
            P.add('dve', lambda e, npk=npk, c0=c0: e.tensor_tensor(
                out=hT[:, :, c0:c0 + npk], in0=psT[:, :, 0:npk],
                in1=gap.unsqueeze(2).broadcast_to([128, 8, npk]), op=ALU.mult),
                [psbank_key, 'pp'], [hT_key])
        if halo:
            xh, xhk = xt_ring.next()
            xnh, xnhk = xn_ring.next()
            P.add('pool', lambda e: e.memset(xh[0:3, :], 0.0), [], [xhk])
            if p0 >= 2:
                dma('sp', xh[0:2, :], xs[p0 - 2:p0, :], [], [xhk])
            if p0 + N < L:
                dma('sp', xh[2:3, :], xs[p0 + N:p0 + N + 1, :], [], [xhk])
            P.add('pool', lambda e: e.memset(ss[0:3, 7:8], 0.0), [], [('ss', 7)])
            P.add('act', lambda e: e.activation(out=junk[0:3, :], in_=xh[0:3, :], func=AF.Square,
                                                accum_out=ss[0:3, 7:8]), [xhk, ('ss', 7)], ['junk', ('ss', 7)])
            P.add('act', lambda e: e.activation(out=rs[0:3, 7:8], in_=ss[0:3, 7:8], func=AF.Ln,
                                                bias=dv[0:3, DV_EPSD:DV_EPSD + 1]), [('ss', 7), 'dv'], [('rs', 7)])
            P.add('act', lambda e: e.activation(out=rs[0:3, 7:8], in_=rs[0:3, 7:8], func=AF.Exp, scale=-0.5),
                  [('rs', 7)], [('rs', 7)])
            P.add('dve', lambda e: e.tensor_scalar(out=xnh[0:3, :], in0=xh[0:3, :], scalar1=rs[0:3, 7:8],
                                                   scalar2=32.0, op0=ALU.mult, op1=ALU.mult),
                  [xhk, ('rs', 7)], [xnhk])

            def trh(e):
                ins = None
                for kc in range(8):
                    ins = e.transpose(out=psT[:, kc, 0:3], in_=xnh[0:3, kc * 128:(kc + 1) * 128],
                                      identity=ident[0:3, 0:3])
                return ins
            P.add('pe', trh, [xnhk, 'ident'], [psbank_key])
            P.add('dve', lambda e: e.tensor_tensor(
                out=hT[:, :, N:N + 3], in0=psT[:, :, 0:3],
                in1=gap.unsqueeze(2).broadcast_to([128, 8, 3]), op=ALU.mult),
                [psbank_key, 'pp'], [hT_key])

    wmix = [A.alloc(8 * 1024, BF16, shape3=1024) for _ in range(4)]
    wg = A.alloc(2 * 8 * 128, BF16).rearrange("p (g h j) -> p g h j", g=2, h=8)
    GXA, GQ, GF, GV = range(4)
    for g, src in ((GXA, 0), (GQ, 1), (GV, 4)):
        dma('pool', wmix[g].rearrange("p a b -> p (a b)"), w_mix[src], [], [('wmix', g)])

    xt_ring = Ring([A.alloc(1024, F32) for _ in range(2)], 'xt')
    xn_ring = Ring([A.alloc(1024, BF16) for _ in range(1)], 'xn')
    hT_bufs = [A.alloc(8 * (TN + 3), BF16, shape3=TN + 3) for _ in range(1)]
    carryA = A.alloc(8, F32)
    S_f = A.alloc(8 * 128, F32, shape3=128)
    S_b = A.alloc(8 * 128, BF16, shape3=128)

    def tmp(n=TN, dt=F32, cnt=2, name=None):
        return Ring([A.alloc(n, dt) for _ in range(cnt)], name)
    t_ext = tmp(TN + 3, F32, 2, 'ext')
    t_xc = tmp(name='xc')
    t_xcb = tmp(dt=BF16, name='xcb')
    t_r = tmp(name='r')
    t_i = tmp(name='i')
    t_a = tmp(name='a')
    t_a2 = tmp(cnt=2, name='a2')
    t_u = tmp(cnt=2, name='u')
    t_h = tmp(name='h')
    t_hb = tmp(cnt=2, name='hbl')
    t_sg = tmp(cnt=1, name='sg')
    t_f = tmp(name='f')
    t_qs = tmp(cnt=2, name='qs')
    t_g = tmp(cnt=1, name='g')
    t_b = tmp(cnt=2, name='b')
    t_eb = tmp(cnt=4, name='eb')
    t_enb = tmp(cnt=2, name='enb')
    t_qt = tmp(dt=BF16, cnt=4, name='qt')
    t_kt = tmp(dt=BF16, cnt=4, name='kt')
    t_kh = tmp(dt=BF16, cnt=4, name='kh')
    t_vb = tmp(dt=BF16, cnt=4, name='vb')
    khT4 = [[A.alloc(128, BF16) for _ in range(4)] for _ in range(2)]
    PT4 = [[A.alloc(128, BF16) for _ in range(4)] for _ in range(2)]
    t_ob = [tmp(cnt=1, name='obl0'), tmp(cnt=1, name='obl1')]
    t_os = [tmp(cnt=1, name='os0'), tmp(cnt=1, name='os1')]
    t_osq = [tmp(cnt=1, name='osq0'), tmp(cnt=1, name='osq1')]
    t_rso = [tmp(cnt=1, name='rso0'), tmp(cnt=1, name='rso1')]
    mix_mark_end = A.mark()

    gen_ring = Ring(banks[0:3], 'psg')
    halo_ring = Ring([banks[3][:, 0:4], banks[3][:, 4:8]], 'pshalo', keys=['bank3', 'bank3'])
    kT_slots = [banks[3][:, 64:128].bitcast(BF16), banks[3][:, 128:192].bitcast(BF16)]
    sc_slots = [banks[3][:, 256:384], banks[3][:, 384:512]]
    ch_o = [(banks[4], 'bank4'), (banks[6], 'bank6')]
    ch_m = [(banks[5], 'bank5'), (banks[7], 'bank7')]

    def stage1_steps(dirn, p0, N, hT, hT_key, meta, h, st):
        rv = (dirn == 1)
        R = (lambda a: rev_ap(a)) if rv else (lambda a: a)
        nsub = (N + 127) // 128
        CL = min(64, N)
        pr0 = p0 - NMETA
        combine = (dirn == 0) and not meta
        hc = slice(h * 128, (h + 1) * 128)
        b0 = h * PP_HEAD
        npv = min(128, N)
        msk = maskR if rv else maskF
        mkey = 'maskR' if rv else 'maskF'
        nch = N // CL
        lc = 0 if rv else CL - 1
        lastc = 0 if rv else N - 1
        V = {}
        steps = []

        def step(f):
            steps.append(f)
            return f

        cload = combine
        cstore = (dirn == 1)

        @step
        def s_xa():
            if combine:
                V['hbl'], V['khbl'] = t_hb.next()
                dma('sp', V['hbl'][:, 0:N], hb_scr[h, :, pr0:pr0 + N], [], [V['khbl']])
            if cload:
                xcb, kxcb = t_xcb.next()
                xc, kxc = t_xc.next()
                qs, kqs = t_qs.next()
                vb, kvb = t_vb.next()
                V.update(xcb=xcb, kxcb=kxcb, xc=xc, kxc=kxc, qs=qs, kqs=kqs, vb=vb, kvb=kvb)
                dma('sp', xcb[:, 0:N], xcb_scr[h, :, pr0:pr0 + N], [], [kxcb])
                dma('sp', xc[:, 0:N], xc_scr[h, :, pr0:pr0 + N], [], [kxc])
                dma('sp', qs[:, 0:N], qs_scr[h, :, pr0:pr0 + N], [], [kqs])
                dma('sp', vb[:, 0:N], vb_scr[h, pr0 // TN], [], [kvb])
                return
            ps_xa, kxa = gen_ring.next()
            ps_hl, khl = halo_ring.next()
            V.update(ps_xa=ps_xa, kxa=kxa, ps_hl=ps_hl, khl=khl)

            def mm_xa(e):
                ins = None
                for kc in range(8):
                    e.matmul(ps_xa[:, 0:N], lhsT=wmix[GXA][:, kc, hc], rhs=hT[:, kc, 0:N],
                             start=(kc == 0), stop=(kc == 7))
                for kc in range(8):
                    ins = e.matmul(ps_hl[:, 0:3], lhsT=wmix[GXA][:, kc, hc], rhs=hT[:, kc, N:N + 3],
                                   start=(kc == 0), stop=(kc == 7))
                return ins
            P.add('pe', mm_xa, [hT_key, ('wmix', GXA)], [kxa, khl])

        @step
        def s_ext():
            if cload:
                return
            ext, kext = t_ext.next()
            V.update(ext=ext, kext=kext)
            ps_xa, ps_hl = V['ps_xa'], V['ps_hl']
            P.add('act', lambda e: e.activation(out=ext[:, 2:2 + N], in_=ps_xa[:, 0:N], func=AF.Copy),
                  [V['kxa']], [kext])
            P.add('dve', lambda e: e.tensor_copy(out=ext[:, 0:2], in_=ps_hl[:, 0:2]), [V['khl']], [kext])
            P.add('dve', lambda e: e.tensor_copy(out=ext[:, N + 2:N + 3], in_=ps_hl[:, 2:3]), [V['khl']], [kext])

        @step
        def s_q():
            if cload:
                return
            ps_q, kq_ = gen_ring.next()
            V.update(ps_q=ps_q, kq_=kq_)

            def mm_q(e):
                ins = None
                for kc in range(8):
                    ins = e.matmul(ps_q[:, 0:N], lhsT=wmix[GQ][:, kc, hc], rhs=hT[:, kc, 0:N],
                                   start=(kc == 0), stop=(kc == 7))
                return ins
            P.add('pe', mm_q, [hT_key, ('wmix', GQ)], [kq_])

        @step
        def s_sg():
            if cload:
                return
            sg, ksg = t_sg.next()
            qs, kqs = t_qs.next()
            xc, kxc = t_xc.next()
            V.update(qs=qs, kqs=kqs, xc=xc, kxc=kxc)
            ps_q, ext = V['ps_q'], V['ext']
            P.add('act', lambda e: e.activation(out=sg[:, 0:N], in_=ps_q[:, 0:N], func=AF.Sigmoid),
                  [V['kq_']], [ksg])
            P.add('dve', lambda e: e.tensor_scalar(
                out=xc[:, 0:N], in0=ext[:, 0:N], scalar1=col(pp, b0 + 0), scalar2=col(pp, b0 + 4),
                op0=ALU.mult, op1=ALU.add), [V['kext'], 'pp'], [kxc])
            P.add('dve', lambda e: e.tensor_tensor(out=qs[:, 0:N], in0=sg[:, 0:N], in1=ps_q[:, 0:N], op=ALU.mult),
                  [ksg, V['kq_']], [kqs])

        @step
        def s_f():
            ps_f, kf_ = gen_ring.next()
            V.update(ps_f=ps_f, kf_=kf_)

            def mm_f(e):
                ins = None
                for kc in range(8):
                    ins = e.matmul(ps_f[:, 0:N], lhsT=wmix[GF][:, kc, hc], rhs=hT[:, kc, 0:N],
                                   start=(kc == 0), stop=(kc == 7))
                return ins
            P.add('pe', mm_f, [hT_key, ('wmix', GF)], [kf_])

        @step
        def s_sf():
            f_, kff = t_f.next()
            V.update(f_=f_, kff=kff)
            ps_f = V['ps_f']
            P.add('act', lambda e: e.activation(out=f_[:, 0:N], in_=ps_f[:, 0:N], func=AF.Sigmoid),
                  [V['kf_']], [kff])
            if cload:
                return
            ext, xc, kxc = V['ext'], V['xc'], V['kxc']
            P.add('dve', lambda e: e.scalar_tensor_tensor(
                out=xc[:, 0:N], in0=ext[:, 1:1 + N], scalar=col(pp, b0 + 1), in1=xc[:, 0:N],
                op0=ALU.mult, op1=ALU.add), [V['kext'], kxc, 'pp'], [kxc])

        @step
        def s_v():
            if cload:
                return
            ps_v, kv_ = gen_ring.next()
            V.update(ps_v=ps_v, kv_=kv_)

            def mm_v(e):
                ins = None
                for s in range(nsub):
                    npk = min(128, N - 128 * s)
                    for kc in range(8):
                        ins = e.matmul(ps_v[:npk, s * 128:(s + 1) * 128], lhsT=hT[:, kc, 128 * s:128 * s + npk],
                                       rhs=wmix[GV][:, kc, hc], start=(kc == 0), stop=(kc == 7))
                return ins
            P.add('pe', mm_v, [hT_key, ('wmix', GV)], [kv_])

        @step
        def s_vb():
            f_, kff = V['f_'], V['kff']
            P.add('dve', lambda e: e.tensor_scalar(
                out=f_[:, 0:N], in0=f_[:, 0:N], scalar1=col(dv, DV_OML + dirn * 8 + h),
                scalar2=col(dv, DV_LB + dirn * 8 + h), op0=ALU.mult, op1=ALU.add), [kff, 'dv'], [kff])
            if cload:
                return
            vb, kvb = t_vb.next()
            V.update(vb=vb, kvb=kvb)
            ps_v, ext, xc, kxc = V['ps_v'], V['ext'], V['xc'], V['kxc']
            P.add('act', lambda e: e.activation(out=vb[:npv, 0:nsub * 128], in_=ps_v[:npv, 0:nsub * 128],
                                                func=AF.Copy), [V['kv_']], [kvb])
            P.add('dve', lambda e: e.scalar_tensor_tensor(
                out=xc[:, 0:N], in0=ext[:, 2:2 + N], scalar=col(pp, b0 + 2), in1=xc[:, 0:N],
                op0=ALU.mult, op1=ALU.add), [V['kext'], kxc, 'pp'], [kxc])

        @step
        def s_conv3():
            if cload:
                return
            ext, xc, kxc = V['ext'], V['xc'], V['kxc']
            xcb, kxcb = t_xcb.next()
            V.update(xcb=xcb, kxcb=kxcb)
            P.add('dve', lambda e: e.scalar_tensor_tensor(
                out=xc[:, 0:N], in0=ext[:, 3:3 + N], scalar=col(pp, b0 + 3), in1=xc[:, 0:N],
                op0=ALU.mult, op1=ALU.add), [V['kext'], kxc, 'pp'], [kxc])
            P.add('dve', lambda e: e.tensor_copy(out=xcb[:, 0:N], in_=xc[:, 0:N]), [kxc], [kxcb])
            if cstore:
                dma('sp', xcb_scr[h, :, pr0:pr0 + N], xcb[:, 0:N], [kxcb], [])
                dma('sp', xc_scr[h, :, pr0:pr0 + N], xc[:, 0:N], [kxc], [])
                dma('sp', qs_scr[h, :, pr0:pr0 + N], V['qs'][:, 0:N], [V['kqs']], [])
                dma('sp', vb_scr[h, pr0 // TN], V['vb'][:, 0:N], [V['kvb']], [])

        @step
        def s_gr():
            ps_r, kr_ = gen_ring.next()
            V.update(ps_r=ps_r, kr_=kr_)
            xcb = V['xcb']
            P.add('pe', lambda e: e.matmul(ps_r[:, 0:N], lhsT=wg[:, 0, h, :], rhs=xcb[:, 0:N],
                                           start=True, stop=True), [V['kxcb'], 'wg'], [kr_])

        @step
        def s_r():
            r_, krr = t_r.next()
            V.update(r_=r_, krr=krr)
            ps_r = V['ps_r']
            P.add('act', lambda e: e.activation(out=r_[:, 0:N], in_=ps_r[:, 0:N], func=AF.Sigmoid,
                                                bias=col(pp, b0 + 5 + dirn)), [V['kr_'], 'pp'], [krr])

        @step
        def s_gi():
            ps_i, ki_ = gen_ring.next()
            V.update(ps_i=ps_i, ki_=ki_)
            xcb = V['xcb']
            P.add('pe', lambda e: e.matmul(ps_i[:, 0:N], lhsT=wg[:, 1, h, :], rhs=xcb[:, 0:N],
                                           start=True, stop=True), [V['kxcb'], 'wg'], [ki_])

        @step
        def s_i():
            i_, kii = t_i.next()
            V.update(i_=i_, kii=kii)
            ps_i = V['ps_i']
            P.add('act', lambda e: e.activation(out=i_[:, 0:N], in_=ps_i[:, 0:N], func=AF.Sigmoid,
                                                bias=col(pp, b0 + 7 + dirn)), [V['ki_'], 'pp'], [kii])

        @step
        def s_g():
            g_, kgg = t_g.next()
            b_, kbb = t_b.next()
            V.update(b_=b_, kbb=kbb)
            f_ = V['f_']
            P.add('act', lambda e: e.activation(out=g_[:, 0:N], in_=f_[:, 0:N], func=AF.Ln), [V['kff']], [kgg])
            P.add('dve', lambda e: e.tensor_tensor_scan(
                out=R(b_[:, 0:N]), data0=R(msk[:, 0:N]), data1=R(g_[:, 0:N]), initial=0.0,
                op0=ALU.mult, op1=ALU.add), [kgg, mkey], [kbb])

        @step
        def s_a2():
            a2, ka2 = t_a2.next()
            V.update(a2=a2, ka2=ka2)
            r_, i_, xc = V['r_'], V['i_'], V['xc']
            P.add('act', lambda e: e.activation(out=a2[:, 0:N], in_=r_[:, 0:N], func=AF.Exp,
                                                scale=col(dv, DV_SP2 + dirn * 8 + h)), [V['krr'], 'dv'], [ka2])
            P.add('dve', lambda e: e.tensor_tensor(out=i_[:, 0:N], in0=i_[:, 0:N], in1=xc[:, 0:N], op=ALU.mult),
                  [V['kii'], V['kxc']], [V['kii']])

        @step
        def s_abs():
            a2, ka2 = V['a2'], V['ka2']
            P.add('act', lambda e: e.activation(out=a2[:, 0:N], in_=a2[:, 0:N], func=AF.Abs, scale=-1.0, bias=1.0),
                  [ka2], [ka2])

        @step
        def s_eb():
            eb, keb = t_eb.next()
            V.update(eb=eb, keb=keb)
            b_ = V['b_']
            P.add('act', lambda e: e.activation(out=eb[:, 0:N], in_=b_[:, 0:N], func=AF.Exp), [V['kbb']], [keb])

        @step
        def s_ln():
            a2, ka2 = V['a2'], V['ka2']
            P.add('act', lambda e: e.activation(out=a2[:, 0:N], in_=a2[:, 0:N], func=AF.Ln, bias=col(dv, DV_TINY)),
                  [ka2, 'dv'], [ka2])

        @step
        def s_enb():
            enb, kenb = t_enb.next()
            qt, kqt = t_qt.next()
            V.update(enb=enb, kenb=kenb, qt=qt, kqt=kqt)
            b_, qs, eb = V['b_'], V['qs'], V['eb']
            P.add('act', lambda e: e.activation(out=enb[:, 0:N], in_=b_[:, 0:N], func=AF.Exp, scale=-1.0),
                  [V['kbb']], [kenb])
            P.add('pool', lambda e: e.tensor_tensor(out=qt[:, 0:N], in0=qs[:, 0:N], in1=eb[:, 0:N], op=ALU.mult),
                  [V['kqs'], V['keb']], [kqt])

        @step
        def s_sqrt():
            a2, ka2 = V['a2'], V['ka2']
            P.add('act', lambda e: e.activation(out=a2[:, 0:N], in_=a2[:, 0:N], func=AF.Exp, scale=0.5),
                  [ka2], [ka2])

        @step
        def s_a():
            a_, kaa = t_a.next()
            kt, kkt = t_kt.next()
            V.update(a_=a_, kaa=kaa, kt=kt, kkt=kkt)
            r_, f_, enb = V['r_'], V['f_'], V['enb']
            P.add('act', lambda e: e.activation(out=a_[:, 0:N], in_=r_[:, 0:N], func=AF.Exp,
                                                scale=col(dv, DV_SP + dirn * 8 + h)), [V['krr'], 'dv'], [kaa])
            P.add('dve', lambda e: e.scalar_tensor_tensor(
                out=kt[:, 0:N], in0=f_[:, 0:N], scalar=1.0, in1=enb[:, 0:N], op0=ALU.subtract, op1=ALU.mult),
                [V['kff'], V['kenb']], [kkt])

        @step
        def s_u():
            u_, kuu = t_u.next()
            V.update(u_=u_, kuu=kuu)
            a2, i_ = V['a2'], V['i_']
            P.add('dve', lambda e: e.tensor_tensor(out=u_[:, 0:N], in0=a2[:, 0:N], in1=i_[:, 0:N], op=ALU.mult),
                  [V['ka2'], V['kii']], [kuu])

        @step
        def s_kh():
            kh, kkh = t_kh.next()
            kt, eb = V['kt'], V['eb']
            eb3 = eb[:, 0:N].rearrange("p (c t) -> p c t", t=CL)
            P.add('dve', lambda e: e.tensor_tensor(
                out=kh[:, 0:N].rearrange("p (c t) -> p c t", t=CL),
                in0=kt[:, 0:N].rearrange("p (c t) -> p c t", t=CL),
                in1=eb3[:, :, lc:lc + 1].broadcast_to([128, nch, CL]), op=ALU.mult), [V['kkt'], V['keb']], [kkh])
            st.update(qt=V['qt'], kqt=V['kqt'], kt=kt, kkt=V['kkt'], kh=kh, kkh=kkh, vb=V['vb'], kvb=V['kvb'],
                      eb=eb, keb=V['keb'])

        @step
        def s_scan():
            hh, khh = t_h.next()
            V.update(hh=hh, khh=khh)
            a_, u_ = V['a_'], V['u_']
            P.add('dve', lambda e: e.tensor_tensor_scan(
                out=R(hh[:, 0:N]), data0=R(a_[:, 0:N]), data1=R(u_[:, 0:N]), initial=col(carryA, h),
                op0=ALU.mult, op1=ALU.add), [V['kaa'], V['kuu'], ('carryA', h)], [khh])
            P.add('pool', lambda e: e.tensor_copy(out=col(carryA, h), in_=hh[:, lastc:lastc + 1]),
                  [khh], [('carryA', h)])

        @step
        def s_out():
            hh, khh = V['hh'], V['khh']
            if dirn == 1:
                dma('sp', hb_scr[h, :, pr0:pr0 + N], hh[:, 0:N], [khh], [])
            elif combine:
                hbl, khbl = V['hbl'], V['khbl']
                P.add('pool', lambda e: e.tensor_tensor(out=hh[:, 0:N], in0=hh[:, 0:N], in1=hbl[:, 0:N], op=ALU.add),
                      [khh, khbl], [khh])
                dma('sp', hs_scr[h, :, pr0:pr0 + N], hh[:, 0:N], [khh], [])
        return steps

    def stage1_pair(dirn, p0, N, hT, hT_key, meta, h0, sts):
        sa = stage1_steps(dirn, p0, N, hT, hT_key, meta, h0, sts[0])
        sb = stage1_steps(dirn, p0, N, hT, hT_key, meta, h0 + 1, sts[1])
        for fa, fb in zip(sa, sb):
            fa()
            yield
            fb()
            yield

    def stage2(dirn, p0, N, meta, h, st, chain):
        rv = (dirn == 1)
        nsub = (N + 127) // 128
        CL = min(64, N)
        pr0 = p0 - NMETA
        combine = (dirn == 0) and not meta
        lc = 0 if rv else CL - 1
        qt, kqt, kt, kkt, kh, kkh = st['qt'], st['kqt'], st['kt'], st['kkt'], st['kh'], st['kkh']
        vb, kvb, eb, keb = st['vb'], st['kvb'], st['eb'], st['keb']
        ps_o, kpo = ch_o[chain]
        mbank, kmb = ch_m[chain]
        if combine:
            ob, kob = t_ob[chain].next()
            dma('sp', ob[:, 0:N], ob_scr[h, :, pr0:pr0 + N], [], [kob])
        msc = mscR if rv else mscF
        msck = 'mscR' if rv else 'mscF'
        sub_order = range(nsub - 1, -1, -1) if rv else range(nsub)
        for s in sub_order:
            npk = min(128, N - 128 * s)
            t0 = 128 * s
            ps_kT, kkT = mbank[:, 256:320].bitcast(BF16), kmb
            P.add('pe', lambda e, ps_kT=ps_kT, t0=t0, npk=npk: e.transpose(
                out=ps_kT[:npk, 0:128], in_=kh[:, t0:t0 + npk], identity=ident), [kkh, 'ident'], [kkT])
            khT, kkhT = khT4[chain][s], ('khT4', chain, s)
            P.add('act', lambda e, khT=khT, ps_kT=ps_kT, npk=npk: e.activation(
                out=khT[:npk, :], in_=ps_kT[:npk, 0:128], func=AF.Copy), [kkT], [kkhT])
            ps_sc, ksc = mbank[:, 0:128], kmb
            P.add('pe', lambda e, ps_sc=ps_sc, t0=t0, npk=npk: e.matmul(
                ps_sc[:npk, 0:npk], lhsT=kt[:, t0:t0 + npk], rhs=qt[:, t0:t0 + npk], start=True, stop=True),
                [kkt, kqt], [ksc])
            yield
            PT, kPT = PT4[chain][s], ('PT4', chain, s)
            P.add('dve', lambda e, PT=PT, ps_sc=ps_sc, npk=npk: e.tensor_tensor(
                out=PT[:npk, 0:npk], in0=ps_sc[:npk, 0:npk], in1=msc[:npk, 0:npk], op=ALU.mult),
                [ksc, msck], [kPT])
            yield
            ncs = npk // CL
            ch_order = range(ncs - 1, -1, -1) if rv else range(ncs)
            for c in ch_order:
                c0 = c * CL

                def mm_o(e, PT=PT, s=s, t0=t0, c0=c0, npk=npk):
                    e.matmul(ps_o[:, t0 + c0:t0 + c0 + CL], lhsT=vb[:npk, s * 128:(s + 1) * 128],
                             rhs=PT[:npk, c0:c0 + CL], start=True, stop=False)
                    return e.matmul(ps_o[:, t0 + c0:t0 + c0 + CL], lhsT=S_b[:, h, :],
                                    rhs=qt[:, t0 + c0:t0 + c0 + CL], start=False, stop=True)
                P.add('pe', mm_o, [kPT, kvb, kqt, ('Sb', h)], [kpo])
                ps_dS, kdS = mbank[:, 128:256], kmb
                P.add('pe', lambda e, ps_dS=ps_dS, khT=khT, s=s, c0=c0: e.matmul(
                    ps_dS[:, 0:128], lhsT=khT[c0:c0 + CL, :], rhs=vb[c0:c0 + CL, s * 128:(s + 1) * 128],
                    start=True, stop=True), [kkhT, kvb], [kdS])
                yield
                dcol = t0 + c0 + lc
                P.add('dve', lambda e, ps_dS=ps_dS, dcol=dcol: e.scalar_tensor_tensor(
                    out=S_b[:, h, :], in0=S_f[:, h, :], scalar=eb[:, dcol:dcol + 1], in1=ps_dS[:, 0:128],
                    op0=ALU.mult, op1=ALU.subtract), [kdS, keb, ('Sf', h)], [('Sb', h)])
                P.add('dve', lambda e, ps_dS=ps_dS, dcol=dcol: e.scalar_tensor_tensor(
                    out=S_f[:, h, :], in0=S_f[:, h, :], scalar=eb[:, dcol:dcol + 1], in1=ps_dS[:, 0:128],
                    op0=ALU.mult, op1=ALU.subtract), [kdS, keb, ('Sf', h)], [('Sf', h)])
                yield
        if dirn == 1:
            ob, kob = t_ob[chain].next()
            P.add('act', lambda e: e.activation(out=ob[:, 0:N], in_=ps_o[:, 0:N], func=AF.Copy), [kpo], [kob])
            dma('sp', ob_scr[h, :, pr0:pr0 + N], ob[:, 0:N], [kob], [])
            yield
        elif combine:
            osm, kos = t_os[chain].next()
            P.add('dve', lambda e: e.tensor_tensor(out=osm[:, 0:N], in0=ps_o[:, 0:N], in1=ob[:, 0:N], op=ALU.add),
                  [kpo, kob], [kos])
            yield
            osq, kosq = t_osq[chain].next()
            P.add('act', lambda e: e.activation(out=osq[:, 0:N], in_=osm[:, 0:N], func=AF.Square), [kos], [kosq])
            yield
            ps_ss, kpss = mbank, kmb
            P.add('pe', lambda e: e.matmul(ps_ss[:, 0:N], lhsT=ones_f, rhs=osq[:, 0:N], start=True, stop=True),
                  [kosq, 'ones'], [kpss])
            yield
            rso, krso = t_rso[chain].next()
            P.add('act', lambda e: e.activation(out=rso[:, 0:N], in_=ps_ss[:, 0:N], func=AF.Ln,
                                                bias=col(dv, DV_EPS128)), [kpss, 'dv'], [krso])
            yield
            P.add('act', lambda e: e.activation(out=rso[:, 0:N], in_=rso[:, 0:N], func=AF.Exp, scale=-0.5),
                  [krso], [krso])
            yield
            P.add('pool', lambda e: e.tensor_tensor(out=osm[:, 0:N], in0=osm[:, 0:N], in1=rso[:, 0:N], op=ALU.mult),
                  [kos, krso], [kos])
            dma('sp', on_scr[h, :, pr0:pr0 + N], osm[:, 0:N], [kos], [])
            yield

    def interleave(*gens):
        gens = [g for g in gens if g is not None]
        while gens:
            for g in list(gens):
                try:
                    next(g)
                except StopIteration:
                    gens.remove(g)

    def stage_x_gen(*a, **k):
        stage_x(*a, **k)
        yield

    def init_states():
        P.add('pool', lambda e: e.memset(carryA, 0.0), [], [('carryA', h) for h in range(8)])
        P.add('pool', lambda e: e.memset(S_f.rearrange("p a b -> p (a b)"), 0.0), [], [('Sf', h) for h in range(8)])
        P.add('pool', lambda e: e.memset(S_b.rearrange("p a b -> p (a b)"), 0.0), [], [('Sb', h) for h in range(8)])

    def run_mixer_pass(dirn, tiles):
        hT, hT_key = hT_bufs[0], ('hT', 0)
        dma('pool', wg.rearrange("p g h j -> p (g h j)"), w_gate[:, dirn * 2048:(dirn + 1) * 2048], [], ['wg'])
        dma('pool', wmix[GF].rearrange("p a b -> p (a b)"), w_mix[3 if dirn == 1 else 2], [], [('wmix', GF)])

        def do_x(p0, N):
            if dirn == 0 and p0 > 0:
                dma('sp', hT[:, :, 0:N], hT_scr[(p0 - NMETA) // TN], [], [hT_key])
                return
            nsub = (N + 127) // 128
            xl = [xt_ring.next() for _ in range(nsub)]
            pb, pbk = gen_ring.next()
            stage_x(p0, N, xl, hT, hT_key, PP_GMIX, True, pb, pbk)
            if dirn == 1:
                dma('sp', hT_scr[(p0 - NMETA) // TN], hT[:, :, 0:N], [hT_key], [])

        def s1_pair(p0, N, meta, h0, sts, with_x):
            if with_x:
                do_x(p0, N)
                yield
            yield from stage1_pair(dirn, p0, N, hT, hT_key, meta, h0, sts)
        p0, N, meta = tiles[0]
        sts = [{}, {}]
        interleave(s1_pair(p0, N, meta, 0, sts, True))
        for ti, (p0, N, meta) in enumerate(tiles):
            for pr in range(4):
                sts_next = [{}, {}]
                if pr < 3:
                    nxt = s1_pair(p0, N, meta, 2 * pr + 2, sts_next, False)
                elif ti + 1 < len(tiles):
                    nxt = s1_pair(*tiles[ti + 1], 0, sts_next, True)
                else:
                    nxt = None
                interleave(stage2(dirn, p0, N, meta, 2 * pr, sts[0], 0),
                           stage2(dirn, p0, N, meta, 2 * pr + 1, sts[1], 1), nxt)
                sts = sts_next

    init_states()
    if stop_after >= 1:
        run_mixer_pass(1, [(NMETA + ti * TN, TN, False) for ti in range(NT - 1, -1, -1)])
    P.barrier()
    init_states()
    if stop_after >= 2:
        run_mixer_pass(0, [(0, NMETA, True)] + [(NMETA + ti * TN, TN, False) for ti in range(NT)])
    P.barrier()

    A.reset(base_mark)
    wout_sb = A.alloc(8 * 1024, BF16, shape3=1024)
    gfin = A.alloc(1024, F32)
    dma('sp', gfin, gfin_d, [], ['gfin'])
    P.add('pool', lambda e: e.tensor_scalar(out=gfin, in0=gfin, scalar1=32.0, scalar2=None, op0=ALU.mult),
          ['gfin'], ['gfin'])
    dma('pool', wout_sb.rearrange("p a b -> p (a b)"), w_out, [], ['wout'])
    xt4 = [A.alloc(1024, F32) for _ in range(4)]
    xn_ring = Ring([A.alloc(1024, BF16) for _ in range(2)], 'xn2')
    hT2 = A.alloc(8 * TN, BF16, shape3=TN)
    hs_ring = tmp(name='hsl')
    on_ring = tmp(name='onl')
    braT = A.alloc(8 * TN, BF16, shape3=TN)
    brbT = A.alloc(8 * TN, BF16, shape3=TN)
    mrgT = A.alloc(8 * TN, BF16, shape3=TN)
    h2T = A.alloc(8 * TN, BF16, shape3=TN)
    actT = A.alloc(32 * TN, BF16, shape3=TN)
    wring = Ring([A.alloc(1024, BF16, shape3=128) for _ in range(8)], 'wch')
    w2ring = Ring([A.alloc(1024, BF16) for _ in range(6)], 'w2ch')
    t_e = tmp(name='e')
    t_t = tmp(cnt=1, name='t')
    t_so = tmp(name='so')
    t_t2 = tmp(cnt=1, name='t2')
    t_sga = tmp(name='sga')
    t_sgb = tmp(name='sgb')
    t_m1 = tmp(cnt=1, name='m1')
    t_m2 = tmp(cnt=1, name='m2')
    t_rl = tmp(dt=BF16, name='rl')
    gen2 = Ring(banks[0:4], 'psg2')
    acc_keys = [('psacc', i) for i in range(4)]
    acc = banks[4:8]

    def wchunk(cid):
        w, wk = wring.next()
        dma('sp', w.rearrange("p a b -> p (a b)"), wf2_bf[cid], [('wf2', cid // 8 * 8)], [wk])
        return w, wk

    def proj(w, wk, src, src_key, ps, psk):
        def mm(e):
            ins = None
            for kc in range(8):
                ins = e.matmul(ps[:, 0:TN], lhsT=w[:, kc, :], rhs=src[:, kc, 0:TN], start=(kc == 0), stop=(kc == 7))
            return ins
        P.add('pe', mm, [wk, src_key], [psk])

    for ti in range(NT if stop_after >= 3 else 0):
        p0 = NMETA + ti * TN
        pr0 = ti * TN
        dma('sp', hT2, hT_scr[ti], [], ['hT2'])
        for s_ in range(4):
            dma('sp', xt4[s_], xs[p0 + 128 * s_:p0 + 128 * (s_ + 1), :], [], [('xt4', s_)])
        for j in range(8):
            w, wk = wchunk(0 + j)
            ps, psk = gen2.next()
            proj(w, wk, hT2, 'hT2', ps, psk)
            e_, ke = t_e.next()
            P.add('act', lambda e, e_=e_, ps=ps: e.activation(out=e_, in_=ps[:, 0:TN], func=AF.Gelu), [psk], [ke])
            hsl, khsl = hs_ring.next()
            dma('sp', hsl, hs_scr[j, :, pr0:pr0 + TN], [], [khsl])
            P.add('pool', lambda e, e_=e_, j=j, hsl=hsl: e.tensor_tensor(
                out=braT[:, j, :], in0=hsl, in1=e_, op=ALU.mult), [khsl, ke], [('braT', j)])
        for j in range(8):
            w, wk = wchunk(8 + j)
            ps, psk = gen2.next()
            proj(w, wk, hT2, 'hT2', ps, psk)
            so, kso = t_so.next()
            P.add('act', lambda e, so=so, ps=ps: e.activation(out=so, in_=ps[:, 0:TN], func=AF.Sigmoid),
                  [psk], [kso])
            t2, kt2 = t_t2.next()
            P.add('dve', lambda e, t2=t2, so=so, ps=ps, j=j: e.scalar_tensor_tensor(
                out=t2, in0=so, scalar=col(dv, DV_HGG + j), in1=ps[:, 0:TN], op0=ALU.mult, op1=ALU.mult),
                [kso, psk, 'dv'], [kt2])
            onl, konl = on_ring.next()
            dma('sp', onl, on_scr[j, :, pr0:pr0 + TN], [], [konl])
            P.add('pool', lambda e, t2=t2, j=j, onl=onl: e.tensor_tensor(
                out=brbT[:, j, :], in0=onl, in1=t2, op=ALU.mult), [konl, kt2], [('brbT', j)])
        bra_keys = [('braT', j) for j in range(8)]
        brb_keys = [('brbT', j) for j in range(8)]
        for j in range(8):
            w, wk = wchunk(16 + j)
            ps_ga, kga = gen2.next()
            proj(w, wk, hT2, 'hT2', ps_ga, kga)
            sga, ksga = t_sga.next()
            P.add('act', lambda e, sga=sga, ps_ga=ps_ga: e.activation(out=sga, in_=ps_ga[:, 0:TN], func=AF.Sigmoid),
                  [kga], [ksga])
            w, wk = wchunk(24 + j)
            ps_gb, kgb = gen2.next()
            proj(w, wk, hT2, 'hT2', ps_gb, kgb)
            sgb, ksgb = t_sgb.next()
            P.add('act', lambda e, sgb=sgb, ps_gb=ps_gb: e.activation(out=sgb, in_=ps_gb[:, 0:TN], func=AF.Sigmoid),
                  [kgb], [ksgb])
            w, wk = wchunk(32 + j)
            ps_pa, kpa = gen2.next()

            def mm_pa(e, w=w, ps_pa=ps_pa):
                ins = None
                for kc in range(8):
                    ins = e.matmul(ps_pa[:, 0:TN], lhsT=w[:, kc, :], rhs=braT[:, kc, :], start=(kc == 0), stop=(kc == 7))
                return ins
            P.add('pe', mm_pa, [wk] + bra_keys, [kpa])
            m1, km1 = t_m1.next()
            P.add('dve', lambda e, m1=m1, ps_pa=ps_pa, sga=sga: e.tensor_tensor(
                out=m1, in0=ps_pa[:, 0:TN], in1=sga, op=ALU.mult), [kpa, ksga], [km1])
            w, wk = wchunk(40 + j)
            ps_pb, kpb = gen2.next()

            def mm_pb(e, w=w, ps_pb=ps_pb):
                ins = None
                for kc in range(8):
                    ins = e.matmul(ps_pb[:, 0:TN], lhsT=w[:, kc, :], rhs=brbT[:, kc, :], start=(kc == 0), stop=(kc == 7))
                return ins
            P.add('pe', mm_pb, [wk] + brb_keys, [kpb])
            m2, km2 = t_m2.next()
            P.add('dve', lambda e, m2=m2, ps_pb=ps_pb, sgb=sgb: e.tensor_tensor(
                out=m2, in0=ps_pb[:, 0:TN], in1=sgb, op=ALU.mult), [kpb, ksgb], [km2])
            P.add('pool', lambda e, m1=m1, m2=m2, j=j: e.tensor_tensor(
                out=mrgT[:, j, :], in0=m1, in1=m2, op=ALU.add), [km1, km2], [('mrgT', j)])
        mrg_keys = [('mrgT', j) for j in range(8)]
        for s in range(4):
            for hf in range(2):
                pa_, pak = acc[(s * 2 + hf) % 4], acc_keys[(s * 2 + hf) % 4]

                def mm_wo(e, s=s, hf=hf, pa_=pa_):
                    ins = None
                    for kc in range(8):
                        ins = e.matmul(pa_[:, 0:512], lhsT=mrgT[:, kc, s * 128:(s + 1) * 128],
                                       rhs=wout_sb[:, kc, hf * 512:(hf + 1) * 512], start=(kc == 0), stop=(kc == 7))
                    return ins
                P.add('pe', mm_wo, mrg_keys + ['wout'], [pak])
                P.add('dve', lambda e, s=s, hf=hf, pa_=pa_: e.tensor_tensor(
                    out=xt4[s][:, hf * 512:(hf + 1) * 512], in0=pa_[:, 0:512],
                    in1=xt4[s][:, hf * 512:(hf + 1) * 512], op=ALU.add), [pak, ('xt4', s)], [('xt4', s)])
        for s in range(4):
            xt, xk = xt4[s], ('xt4', s)
            P.add('pool', lambda e, s=s: e.memset(col(ss, s), 0.0), [], [('ss', s)])
            P.add('act', lambda e, xt=xt, s=s: e.activation(out=junk, in_=xt, func=AF.Square,
                                                            accum_out=ss[:, s:s + 1]), [xk, ('ss', s)],
                  ['junk', ('ss', s)])
            P.add('act', lambda e, s=s: e.activation(out=rs[:, s:s + 1], in_=ss[:, s:s + 1], func=AF.Ln, bias=dv[:, DV_EPSD:DV_EPSD + 1]), [('ss', s), 'dv'], [('rs', s)])
            P.add('act', lambda e, s=s: e.activation(out=rs[:, s:s + 1], in_=rs[:, s:s + 1], func=AF.Exp, scale=-0.5), [('rs', s)], [('rs', s)])
            xn, xnk = xn_ring.next()
            P.add('dve', lambda e, xt=xt, xn=xn, s=s: e.tensor_scalar(
                out=xn, in0=xt, scalar1=rs[:, s:s + 1], scalar2=32.0, op0=ALU.mult, op1=ALU.mult),
                [xk, ('rs', s)], [xnk])
            pb, pbk = gen2.next()
            psT = pb.bitcast(BF16).rearrange("p (a b) -> p a b", b=128)

            def tr2(e, xn=xn, psT=psT):
                ins = None
                for kc in range(8):
                    ins = e.transpose(out=psT[:, kc, :], in_=xn[:, kc * 128:(kc + 1) * 128], identity=ident)
                return ins
            P.add('pe', tr2, [xnk, 'ident'], [pbk])
            P.add('dve', lambda e, psT=psT, s=s: e.tensor_tensor(
                out=h2T[:, :, s * 128:(s + 1) * 128], in0=psT,
                in1=pp[:, PP_GMLP:PP_GMLP + 8].unsqueeze(2).broadcast_to([128, 8, 128]), op=ALU.mult),
                [pbk, 'pp'], ['h2T'])
        for m in range(32):
            w, wk = wchunk(48 + m)
            ps, psk = gen2.next()
            proj(w, wk, h2T, 'h2T', ps, psk)
            rl, krl = t_rl.next()
            P.add('act', lambda e, rl=rl, ps=ps: e.activation(out=rl, in_=ps[:, 0:TN], func=AF.Relu), [psk], [krl])
            P.add('dve' if m % 2 == 0 else 'pool', lambda e, rl=rl, m=m: e.tensor_tensor(
                out=actT[:, m, :], in0=rl, in1=rl, op=ALU.mult), [krl], [('actT', m)])
        for hf in range(2):
            for mp in range(16):
                w2c, w2k = w2ring.next()
                q = hf * 16 + mp
                dma('sp', w2c, w2_bf[q], [('w2s', q // 8 * 8)], [w2k])
                for mm in range(2):
                    m = 2 * mp + mm

                    def mm_2(e, m=m, mm=mm, w2c=w2c):
                        ins = None
                        for s in range(4):
                            ins = e.matmul(acc[s][:, 0:512], lhsT=actT[:, m, s * 128:(s + 1) * 128],
                                           rhs=w2c[:, mm * 512:(mm + 1) * 512], start=(m == 0), stop=(m == 31))
                        return ins
                    P.add('pe', mm_2, [w2k, ('actT', m)], acc_keys)
            for s in range(4):
                P.add('dve', lambda e, s=s, hf=hf: e.tensor_tensor(
                    out=xt4[s][:, hf * 512:(hf + 1) * 512], in0=acc[s][:, 0:512],
                    in1=xt4[s][:, hf * 512:(hf + 1) * 512], op=ALU.add), [acc_keys[s], ('xt4', s)], [('xt4', s)])
        for s in range(4):
            xt, xk = xt4[s], ('xt4', s)
            P.add('pool', lambda e, s=s: e.memset(col(ss, 4 + s % 2), 0.0), [], [('ss', 4 + s % 2)])
            P.add('act', lambda e, xt=xt, s=s: e.activation(out=junk, in_=xt, func=AF.Square,
                                                            accum_out=ss[:, 4 + s % 2:5 + s % 2]),
                  [xk, ('ss', 4 + s % 2)], ['junk', ('ss', 4 + s % 2)])
            P.add('act', lambda e, s=s: e.activation(out=rs[:, 4 + s % 2:5 + s % 2], in_=ss[:, 4 + s % 2:5 + s % 2], func=AF.Ln, bias=dv[:, DV_EPSD:DV_EPSD + 1]), [('ss', 4 + s % 2), 'dv'], [('rs', 4 + s % 2)])
            P.add('act', lambda e, s=s: e.activation(out=rs[:, 4 + s % 2:5 + s % 2], in_=rs[:, 4 + s % 2:5 + s % 2], func=AF.Exp, scale=-0.5), [('rs', 4 + s % 2)], [('rs', 4 + s % 2)])
            P.add('dve', lambda e, xt=xt, s=s: e.scalar_tensor_tensor(
                out=xt, in0=xt, scalar=rs[:, 4 + s % 2:5 + s % 2], in1=gfin, op0=ALU.mult, op1=ALU.mult),
                [xk, ('rs', 4 + s % 2), 'gfin'], [xk])
            dma('sp', y_d[pr0 + s * 128:pr0 + (s + 1) * 128, :], xt, [xk], [('yout', ti, s)])
    P.barrier()

    with (nc.semaphore("s_pe") as s_pe, nc.semaphore("s_act") as s_act, nc.semaphore("s_dve") as s_dve,
          nc.semaphore("s_pool") as s_pool):
        import contextlib
        with contextlib.ExitStack() as st:
            dsems = dict(sp=[st.enter_context(nc.semaphore(f"s_dma{i}")) for i in range(NDSEM)],
                         pool=[st.enter_context(nc.semaphore(f"s_dmap{i}")) for i in range(8)])
            sems = dict(pe=s_pe, act=s_act, dve=s_dve, pool=s_pool)
            P.finalize(nc, sems, dsems)
            with nc.Block() as block:
                @block.sync
                def _(e):
                    P.emit('sp', e)

                @block.tensor
                def _(e):
                    P.emit('pe', e)

                @block.scalar
                def _(e):
                    P.emit('act', e)

                @block.vector
                def _(e):
                    P.emit('dve', e)

                @block.gpsimd
                def _(e):
                    P.emit('pool', e)
    return nc


def _pack_params(conv_w, conv_b, rg_ba, rg_bx, rg_lambda, hg_lb_logits, hg_norm_g, norm_mix_g, norm_mlp_g):
    pp = np.zeros((128, PP_N), np.float32)
    for h in range(8):
        sl = slice(h * 128, (h + 1) * 128)
        b0 = h * PP_HEAD
        for j in range(4):
            pp[:, b0 + j] = conv_w[0, j, sl]
        pp[:, b0 + 4] = conv_b[0, sl]
        for dr in range(2):
            pp[:, b0 + 5 + dr] = rg_ba[0, dr, sl]
            pp[:, b0 + 7 + dr] = rg_bx[0, dr, sl]
            pp[:, b0 + 9 + dr] = rg_lambda[0, dr, sl]
            pp[:, b0 + 11 + dr] = hg_lb_logits[0, dr, sl]
            pp[:, b0 + 13 + dr] = hg_lb_logits[1, dr, sl]
        pp[:, b0 + 15] = hg_norm_g[0, sl]
    for kc in range(8):
        pp[:, PP_GMIX + kc] = norm_mix_g[0, kc * 128:(kc + 1) * 128]
        pp[:, PP_GMLP + kc] = norm_mlp_g[0, kc * 128:(kc + 1) * 128]
    return pp


def _consts():
    c = np.zeros((128, 128 * 4 + 1024), np.float32)
    c[:, 0:128] = np.eye(128, dtype=np.float32)
    c[:, 128:256] = 1.0
    s = np.arange(128)[:, None]
    t = np.arange(128)[None, :]
    same = (s // 64) == (t // 64)
    c[:, 256:384] = -1.0 * (same & (s <= t))
    c[:, 384:512] = -1.0 * (same & (s >= t))
    mF = np.ones(512, np.float32)
    mF[0::64] = 0.0
    mR = np.ones(512, np.float32)
    mR[63::64] = 0.0
    c[:, 512:1024] = mF[None, :]
    c[:, 1024:1536] = mR[None, :]
    return c


def _kc_layout(w):
    return np.ascontiguousarray(w.reshape(8, 128, -1).transpose(1, 0, 2))


def kernel(x_prompt, x_sample, meta_tokens, hg_lb_logits, norm_mix_g, w_in, conv_w, conv_b, rg_wa, rg_ba,
           rg_wx, rg_bx, rg_lambda, hg_norm_g, w_branch_a, w_branch_b, w_out, norm_mlp_g, w_mlp1, w_mlp2,
           final_norm_g):
    f = lambda a: np.asarray(a, dtype=np.float32)
    x_prompt, x_sample, meta_tokens = f(x_prompt), f(x_sample), f(meta_tokens)
    T = x_prompt.shape[1]
    NT = T // TN
    seqs = [x_prompt[i] for i in range(x_prompt.shape[0])] + [x_sample[i] for i in range(x_sample.shape[0])]
    assert len(seqs) <= NCORES
    win = f(w_in)[0]
    grp = lambda g: win[:, g * 1024:(g + 1) * 1024]
    w_mix = np.stack([_kc_layout(grp(g)).reshape(128, 8 * 1024) for g in (0, 2, 3, 4, 5)])
    wa, wx = f(rg_wa)[0], f(rg_wx)[0]
    wgate = np.stack([wa[0], wx[0], wa[1], wx[1]])
    wgate = np.ascontiguousarray(wgate.transpose(2, 0, 1, 3)).reshape(128, 4 * 8 * 128)

    def chunks(w):
        k = _kc_layout(w)
        C = k.shape[2]
        return np.ascontiguousarray(k.reshape(128, 8, C // 128, 128).transpose(2, 0, 1, 3)).reshape(C // 128, 128, 1024)
    w_f2 = np.concatenate([chunks(grp(1)), chunks(grp(6)), chunks(grp(7)), chunks(grp(8)),
                           chunks(f(w_branch_a)[0]), chunks(f(w_branch_b)[0]), chunks(f(w_mlp1)[0])], axis=0)
    wout_l = _kc_layout(f(w_out)[0]).reshape(128, 8 * 1024)
    w2_l = np.ascontiguousarray(f(w_mlp2)[0].reshape(16, 2, 128, 2, 512).transpose(3, 0, 2, 1, 4)).reshape(32, 128, 1024)
    pp = _pack_params(f(conv_w), f(conv_b), f(rg_ba), f(rg_bx), f(rg_lambda), f(hg_lb_logits), f(hg_norm_g),
                      f(norm_mix_g), f(norm_mlp_g))
    gfin = np.ascontiguousarray(np.broadcast_to(f(final_norm_g)[None, :], (128, D)))
    consts = _consts()
    shared = dict(w_mix=w_mix, w_gate=wgate, w_f2=w_f2, w_out=wout_l, w_2=w2_l, pp=pp, gfin=gfin, consts=consts)
    in_maps = []
    for c in range(NCORES):
        if c < len(seqs):
            xs = np.concatenate([meta_tokens, seqs[c]], axis=0)
        else:
            xs = np.zeros((T + NMETA, D), np.float32)
        m = dict(shared)
        m["xs"] = np.ascontiguousarray(xs)
        in_maps.append(m)
    nc = build(NT)
    res = run_bass_kernel_spmd(nc, in_maps, core_ids=list(range(NCORES)))
    outs = [np.asarray(res.results[c]["y"], dtype=np.float32) for c in range(len(seqs))]
    nb = x_prompt.shape[0]
    y_prompt = np.stack(outs[:nb])
    y_sample = np.stack(outs[nb:])
    return (y_prompt, y_sample)
```

```python
import numpy as np
import concourse.bass as bass
import concourse.mybir as mybir
from concourse.bass_utils import run_bass_kernel_spmd
from concourse.ap import AP

F32 = mybir.dt.float32
BF16 = mybir.dt.bfloat16
U8 = mybir.dt.uint8
ALU = mybir.AluOpType
AF = mybir.ActivationFunctionType

D = 1024
NMETA = 16
TN = 512
EPS = 1e-6
RG_C = 8.0
NCORES = 8
NDSEM = 24

PP_HEAD = 16
PP_GMIX = 128
PP_GMLP = 136
PP_N = 144
DV_SP = 0
DV_SP2 = 16
DV_LB = 32
DV_OML = 48
DV_HGG = 64
DV_EPS128 = 72
DV_EPSD = 73
DV_TINY = 74
DV_N = 80


def rev_ap(a):
    aps = [list(x) for x in a.ap]
    step, cnt = aps[-1]
    aps[-1] = [-step, cnt]
    return AP(a.tensor, a.offset + step * (cnt - 1), aps)


class Prog:
    def __init__(self):
        self.ops = []
        self.last_w = {}
        self.readers = {}
        self.last_barrier = 0

    def add(self, eng, fn, reads=(), writes=(), dma=False):
        i = len(self.ops)
        deps = set()
        for r in reads:
            if r in self.last_w:
                deps.add(self.last_w[r])
        for w in writes:
            if w in self.last_w:
                deps.add(self.last_w[w])
            deps.update(self.readers.get(w, ()))
        for r in reads:
            self.readers.setdefault(r, []).append(i)
        for w in writes:
            self.last_w[w] = i
            self.readers[w] = []
        self.ops.append(dict(eng=eng, fn=fn, deps=deps, dma=dma, sig=False))
        return i

    def barrier(self):
        n = len(self.ops)
        deps = set()
        last = {}
        for i in range(self.last_barrier, n):
            op = self.ops[i]
            if op['dma']:
                deps.add(i)
            elif op['fn'] is not None:
                last[op['eng']] = i
        deps.update(last.values())
        for e in ('pe', 'act', 'dve', 'pool', 'sp'):
            self.ops.append(dict(eng=e, fn=None, deps=set(deps), dma=False, sig=False))
        self.last_barrier = len(self.ops)
        self.last_w = {}
        self.readers = {}

    def finalize(self, nc, sems, dsems):
        ops = self.ops
        for q, ring in dsems.items():
            dma_idx = [i for i, o in enumerate(ops) if o['dma'] and o['eng'] == q]
            nr = len(ring)
            for j, i in enumerate(dma_idx):
                ops[i]['sem'] = ring[j % nr]
                ops[i]['val'] = 16 * (j // nr + 1)
                ops[i]['sig'] = True
                if j >= nr:
                    ops[i]['deps'].add(dma_idx[j - nr])
        for i, o in enumerate(ops):
            nd = set()
            for d in o['deps']:
                od = ops[d]
                if od['fn'] is None:
                    continue
                if (not od['dma']) and od['eng'] == o['eng'] and o['eng'] == 'pe' and not o['dma']:
                    continue
                nd.add(d)
                od['sig'] = True
            o['deps'] = nd
        cnt = {}
        for o in ops:
            if o['dma'] or o['fn'] is None:
                continue
            if o['sig']:
                cnt[o['eng']] = cnt.get(o['eng'], 0) + 1
                o['sem'] = sems[o['eng']]
                o['val'] = cnt[o['eng']]
        self.n_sig = cnt

    def emit(self, eng_name, e):
        known = {}
        for o in self.ops:
            if o['eng'] != eng_name:
                continue
            waits = {}
            for d in o['deps']:
                od = self.ops[d]
                s, v = od['sem'], od['val']
                k = id(s)
                if known.get(k, 0) >= v:
                    continue
                if k not in waits or waits[k][1] < v:
                    waits[k] = (s, v)
            for k, (s, v) in waits.items():
                e.wait_ge(s, v)
                known[k] = v
            if o['fn'] is None:
                continue
            ins = o['fn'](e)
            if o['sig']:
                ins.then_inc(o['sem'], 16 if o['dma'] else 1)


class Arena:
    def __init__(self, base_ap, nbytes):
        self.base = base_ap
        self.nbytes = nbytes
        self.off = 0
        self.mark_ = 0

    def alloc(self, free_elems, dtype, shape3=None):
        sz = free_elems * (4 if dtype == F32 else 2)
        sz_al = (sz + 31) // 32 * 32
        assert self.off + sz_al <= self.nbytes, f"SBUF arena overflow {self.off + sz_al} > {self.nbytes}"
        a = self.base[:, self.off:self.off + sz].bitcast(dtype)
        self.off += sz_al
        if shape3 is not None:
            a = a.rearrange("p (a b) -> p a b", b=shape3)
        return a

    def mark(self):
        return self.off

    def reset(self, m):
        self.off = m


class Ring:
    def __init__(self, bufs, name, keys=None):
        self.bufs = bufs
        self.name = name
        self.keys = keys
        self.i = 0

    def next(self):
        k = self.i % len(self.bufs)
        self.i += 1
        return self.bufs[k], (self.keys[k] if self.keys else (self.name, k))


def build(NT, stop_after=3):
    T = NT * TN
    L = T + NMETA
    nc = bass.Bass("TRN2", target_bir_lowering=False)
    P = Prog()

    xs = nc.dram_tensor("xs", [L, D], F32, kind="ExternalInput").ap()
    w_mix = nc.dram_tensor("w_mix", [5, 128, 8 * 1024], F32, kind="ExternalInput").ap()
    w_gate = nc.dram_tensor("w_gate", [128, 4 * 8 * 128], F32, kind="ExternalInput").ap()
    w_f2 = nc.dram_tensor("w_f2", [80, 128, 1024], F32, kind="ExternalInput").ap()
    w_out = nc.dram_tensor("w_out", [128, 8 * 1024], F32, kind="ExternalInput").ap()
    w_2 = nc.dram_tensor("w_2", [32, 128, 1024], F32, kind="ExternalInput").ap()
    pp_d = nc.dram_tensor("pp", [128, PP_N], F32, kind="ExternalInput").ap()
    gfin_d = nc.dram_tensor("gfin", [128, D], F32, kind="ExternalInput").ap()
    consts_d = nc.dram_tensor("consts", [128, 128 * 4 + 512 * 2], F32, kind="ExternalInput").ap()
    y_d = nc.dram_tensor("y", [T, D], F32, kind="ExternalOutput").ap()

    hb_scr = nc.dram_tensor("hb_scr", [8, 128, T], F32, kind="Internal").ap()
    ob_scr = nc.dram_tensor("ob_scr", [8, 128, T], F32, kind="Internal").ap()
    hs_scr = nc.dram_tensor("hs_scr", [8, 128, T], F32, kind="Internal").ap()
    on_scr = nc.dram_tensor("on_scr", [8, 128, T], F32, kind="Internal").ap()
    xc_scr = nc.dram_tensor("xc_scr", [8, 128, T], F32, kind="Internal").ap()
    qs_scr = nc.dram_tensor("qs_scr", [8, 128, T], F32, kind="Internal").ap()
    xcb_scr = nc.dram_tensor("xcb_scr", [8, 128, T], BF16, kind="Internal").ap()
    vb_scr = nc.dram_tensor("vb_scr", [8, NT, 128, TN], BF16, kind="Internal").ap()
    hT_scr = nc.dram_tensor("hT_scr", [NT, 128, 8, TN], BF16, kind="Internal").ap()
    wf2_bf = nc.dram_tensor("wf2_bf", [80, 128, 1024], BF16, kind="Internal").ap()
    w2_bf = nc.dram_tensor("w2_bf", [32, 128, 1024], BF16, kind="Internal").ap()

    ARENA_BYTES = 206 * 1024
    arena_t = nc.alloc_sbuf_tensor("arena", [128, ARENA_BYTES], U8).ap()
    A = Arena(arena_t, ARENA_BYTES)
    banks = [nc.alloc_psum_tensor(f"bank{i}", [128, 512], F32).ap() for i in range(8)]

    pp = A.alloc(PP_N, F32)
    dv = A.alloc(DV_N, F32)
    ident = A.alloc(128, BF16)
    ones_f = A.alloc(128, F32)
    mscF = A.alloc(128, F32)
    mscR = A.alloc(128, F32)
    maskF = A.alloc(512, F32)
    maskR = A.alloc(512, F32)
    junk = A.alloc(1024, BF16)
    ss = A.alloc(8, F32)
    rs = A.alloc(8, F32)
    base_mark = A.mark()
    ctmp = A.alloc(128 * 4 + 1024, F32)

    def col(t, c):
        return t[:, c:c + 1]

    def dma(q, out, in_, reads, writes):
        return P.add(q, lambda e: e.dma_start(out=out, in_=in_), reads, writes, dma=True)

    dma('sp', pp, pp_d, [], ['pp'])
    dma('sp', ctmp, consts_d, [], ['ctmp'])
    P.add('dve', lambda e: e.tensor_copy(out=ident, in_=ctmp[:, 0:128]), ['ctmp'], ['ident'])
    P.add('dve', lambda e: e.tensor_copy(out=ones_f, in_=ctmp[:, 128:256]), ['ctmp'], ['ones'])
    P.add('dve', lambda e: e.tensor_copy(out=mscF, in_=ctmp[:, 256:384]), ['ctmp'], ['mscF'])
    P.add('dve', lambda e: e.tensor_copy(out=mscR, in_=ctmp[:, 384:512]), ['ctmp'], ['mscR'])
    P.add('dve', lambda e: e.tensor_copy(out=maskF, in_=ctmp[:, 512:1024]), ['ctmp'], ['maskF'])
    P.add('dve', lambda e: e.tensor_copy(out=maskR, in_=ctmp[:, 1024:1536]), ['ctmp'], ['maskR'])
    for c in range(0, 80, 8):
        dma('pool', wf2_bf[c:c + 8], w_f2[c:c + 8], [], [('wf2', c)])
    for c in range(0, 32, 8):
        dma('pool', w2_bf[c:c + 8], w_2[c:c + 8], [], [('w2s', c)])

    dtmp = A.alloc(64, F32)
    for h in range(8):
        b0 = h * PP_HEAD
        for dr in range(2):
            k = dr * 8 + h
            lam = col(pp, b0 + 9 + dr)
            P.add('act', lambda e, lam=lam, k=k: e.activation(out=col(dtmp, k), in_=lam, func=AF.Exp, scale=-1.0),
                  ['pp'], [('dtmp', k)])
            P.add('act', lambda e, k=k: e.activation(out=col(dtmp, k), in_=col(dtmp, k), func=AF.Ln, bias=1.0),
                  [('dtmp', k)], [('dtmp', k)])
            P.add('dve', lambda e, k=k: e.tensor_scalar(out=col(dv, DV_SP + k), in0=col(dtmp, k), scalar1=-RG_C,
                                                        scalar2=None, op0=ALU.mult), [('dtmp', k)], ['dv'])
            P.add('dve', lambda e, k=k: e.tensor_scalar(out=col(dv, DV_SP2 + k), in0=col(dtmp, k),
                                                        scalar1=-2.0 * RG_C, scalar2=None, op0=ALU.mult),
                  [('dtmp', k)], ['dv'])
            l0 = col(pp, b0 + 11 + dr)
            l1 = col(pp, b0 + 13 + dr)
            P.add('dve', lambda e, l0=l0, l1=l1, k=k: e.tensor_tensor(out=col(dtmp, 16 + k), in0=l0, in1=l1,
                                                                     op=ALU.subtract), ['pp'], [('dtmp', 16 + k)])
        P.add('dve', lambda e, h=h, b0=b0: e.tensor_scalar(out=col(dv, DV_HGG + h), in0=col(pp, b0 + 15),
                                                           scalar1=float(np.sqrt(128.0)), scalar2=None,
                                                           op0=ALU.mult), ['pp'], ['dv'])
    P.add('act', lambda e: e.activation(out=dv[:, DV_LB:DV_LB + 16], in_=dtmp[:, 16:32], func=AF.Sigmoid),
          [('dtmp', 16 + k) for k in range(16)], ['dv'])
    P.add('dve', lambda e: e.tensor_scalar(out=dv[:, DV_OML:DV_OML + 16], in0=dv[:, DV_LB:DV_LB + 16],
                                           scalar1=-1.0, scalar2=1.0, op0=ALU.mult, op1=ALU.add), ['dv'], ['dv'])
    P.add('pool', lambda e: e.memset(col(dv, DV_EPS128), float(128.0 * EPS)), [], ['dv'])
    P.add('pool', lambda e: e.memset(col(dv, DV_EPSD), float(D * EPS)), [], ['dv'])
    P.add('pool', lambda e: e.memset(col(dv, DV_TINY), 1e-30), [], ['dv'])
    P.barrier()
    A.reset(base_mark)

    def stage_x(p0, N, xt_list, hT, hT_key, g_col0, halo, psbank, psbank_key, hcol0=0):
        nsub = (N + 127) // 128
        psT = psbank.bitcast(BF16).rearrange("p (a b) -> p a b", b=128)
        gap = pp[:, g_col0:g_col0 + 8]
        for s in range(nsub):
            npk = min(128, N - 128 * s)
            xt, xk = xt_list[s]
            dma('sp', xt[:npk, :], xs[p0 + 128 * s:p0 + 128 * s + npk, :], [], [xk])
            P.add('pool', lambda e, s=s: e.memset(col(ss, s), 0.0), [], [('ss', s)])
            P.add('act', lambda e, xt=xt, npk=npk, s=s: e.activation(
                out=junk[:npk, :], in_=xt[:npk, :], func=AF.Square, accum_out=ss[:npk, s:s + 1]),
                [xk, ('ss', s)], ['junk', ('ss', s)])
            P.add('act', lambda e, npk=npk, s=s: e.activation(out=rs[:npk, s:s + 1], in_=ss[:npk, s:s + 1], func=AF.Ln, bias=dv[:npk, DV_EPSD:DV_EPSD + 1]), [('ss', s), 'dv'], [('rs', s)])
            P.add('act', lambda e, npk=npk, s=s: e.activation(out=rs[:npk, s:s + 1], in_=rs[:npk, s:s + 1], func=AF.Exp, scale=-0.5), [('rs', s)], [('rs', s)])
            xn, xnk = xn_ring.next()
            P.add('dve', lambda e, xt=xt, xn=xn, npk=npk, s=s: e.tensor_scalar(
                out=xn[:npk, :], in0=xt[:npk, :], scalar1=rs[:npk, s:s + 1], scalar2=32.0,
                op0=ALU.mult, op1=ALU.mult), [xk, ('rs', s)], [xnk])

            def tr(e, xn=xn, npk=npk):
                ins = None
                for kc in range(8):
                    ins = e.transpose(out=psT[:, kc, 0:npk], in_=xn[:npk, kc * 128:(kc + 1) * 128],
                                      identity=ident[:npk, :npk])
                return ins
            P.add('pe', tr, [xnk, 'ident'], [psbank_key])
            c0 = hcol0 + 128 * s
            P.add('dve', lambda e, npk=npk, c0=c0: e.tensor_tensor(
                out=hT[:, :, c0:c0 + npk], in0=psT[:, :, 0:npk],
                in1=gap.unsqueeze(2).broadcast_to([128, 8, npk]), op=ALU.mult),
                [psbank_key, 'pp'], [hT_key])
        if halo:
            xh, xhk = xt_ring.next()
            xnh, xnhk = xn_ring.next()
            P.add('pool', lambda e: e.memset(xh[0:3, :], 0.0), [], [xhk])
            if p0 >= 2:
                dma('sp', xh[0:2, :], xs[p0 - 2:p0, :], [], [xhk])
            if p0 + N < L:
                dma('sp', xh[2:3, :], xs[p0 + N:p0 + N + 1, :], [], [xhk])
            P.add('pool', lambda e: e.memset(ss[0:3, 7:8], 0.0), [], [('ss', 7)])
            P.add('act', lambda e: e.activation(out=junk[0:3, :], in_=xh[0:3, :], func=AF.Square,
                                                accum_out=ss[0:3, 7:8]), [xhk, ('ss', 7)], ['junk', ('ss', 7)])
            P.add('act', lambda e: e.activation(out=rs[0:3, 7:8], in_=ss[0:3, 7:8], func=AF.Ln,
                                                bias=dv[0:3, DV_EPSD:DV_EPSD + 1]), [('ss', 7), 'dv'], [('rs', 7)])
            P.add('act', lambda e: e.activation(out=rs[0:3, 7:8], in_=rs[0:3, 7:8], func=AF.Exp, scale=-0.5),
                  [('rs', 7)], [('rs', 7)])
            P.add('dve', lambda e: e.tensor_scalar(out=xnh[0:3, :], in0=xh[0:3, :], scalar1=rs[0:3, 7:8],
                                                   scalar2=32.0, op0=ALU.mult, op1=ALU.mult),
                  [xhk, ('rs', 7)], [xnhk])

            def trh(e):
                ins = None
                for kc in range(8):
                    ins = e.transpose(out=psT[:, kc, 0:3], in_=xnh[0:3, kc * 128:(kc + 1) * 128],
                                      identity=ident[0:3, 0:3])
                return ins
            P.add('pe', trh, [xnhk, 'ident'], [psbank_key])
            P.add('dve', lambda e: e.tensor_tensor(
                out=hT[:, :, N:N + 3], in0=psT[:, :, 0:3],
                in1=gap.unsqueeze(2).broadcast_to([128, 8, 3]), op=ALU.mult),
                [psbank_key, 'pp'], [hT_key])

    wmix = [A.alloc(8 * 1024, BF16, shape3=1024) for _ in range(4)]
    wg = A.alloc(2 * 8 * 128, BF16).rearrange("p (g h j) -> p g h j", g=2, h=8)
    GXA, GQ, GF, GV = range(4)
    for g, src in ((GXA, 0), (GQ, 1), (GV, 4)):
        dma('pool', wmix[g].rearrange("p a b -> p (a b)"), w_mix[src], [], [('wmix', g)])

    xt_ring = Ring([A.alloc(1024, F32) for _ in range(2)], 'xt')
    xn_ring = Ring([A.alloc(1024, BF16) for _ in range(1)], 'xn')
    hT_bufs = [A.alloc(8 * (TN + 3), BF16, shape3=TN + 3) for _ in range(1)]
    carryA = A.alloc(8, F32)
    S_f = A.alloc(8 * 128, F32, shape3=128)
    S_b = A.alloc(8 * 128, BF16, shape3=128)

    def tmp(n=TN, dt=F32, cnt=2, name=None):
        return Ring([A.alloc(n, dt) for _ in range(cnt)], name)
    t_ext = tmp(TN + 3, F32, 2, 'ext')
    t_xc = tmp(name='xc')
    t_xcb = tmp(dt=BF16, name='xcb')
    t_r = tmp(name='r')
    t_i = tmp(name='i')
    t_a = tmp(name='a')
    t_a2 = tmp(cnt=2, name='a2')
    t_u = tmp(cnt=2, name='u')
    t_h = tmp(name='h')
    t_hb = tmp(cnt=2, name='hbl')
    t_sg = tmp(cnt=1, name='sg')
    t_f = tmp(name='f')
    t_qs = tmp(cnt=2, name='qs')
    t_g = tmp(cnt=1, name='g')
    t_b = tmp(cnt=2, name='b')
    t_eb = tmp(cnt=4, name='eb')
    t_enb = tmp(cnt=2, name='enb')
    t_qt = tmp(dt=BF16, cnt=4, name='qt')
    t_kt = tmp(dt=BF16, cnt=4, name='kt')
    t_kh = tmp(dt=BF16, cnt=4, name='kh')
    t_vb = tmp(dt=BF16, cnt=4, name='vb')
    khT4 = [[A.alloc(128, BF16) for _ in range(4)] for _ in range(2)]
    PT4 = [[A.alloc(128, BF16) for _ in range(4)] for _ in range(2)]
    t_ob = [tmp(cnt=1, name='obl0'), tmp(cnt=1, name='obl1')]
    t_os = [tmp(cnt=1, name='os0'), tmp(cnt=1, name='os1')]
    t_osq = [tmp(cnt=1, name='osq0'), tmp(cnt=1, name='osq1')]
    t_rso = [tmp(cnt=1, name='rso0'), tmp(cnt=1, name='rso1')]
    mix_mark_end = A.mark()

    gen_ring = Ring(banks[0:3], 'psg')
    halo_ring = Ring([banks[3][:, 0:4], banks[3][:, 4:8]], 'pshalo', keys=['bank3', 'bank3'])
    kT_slots = [banks[3][:, 64:128].bitcast(BF16), banks[3][:, 128:192].bitcast(BF16)]
    sc_slots = [banks[3][:, 256:384], banks[3][:, 384:512]]
    ch_o = [(banks[4], 'bank4'), (banks[6], 'bank6')]
    ch_m = [(banks[5], 'bank5'), (banks[7], 'bank7')]

    def stage1_steps(dirn, p0, N, hT, hT_key, meta, h, st):
        rv = (dirn == 1)
        R = (lambda a: rev_ap(a)) if rv else (lambda a: a)
        nsub = (N + 127) // 128
        CL = min(64, N)
        pr0 = p0 - NMETA
        combine = (dirn == 0) and not meta
        hc = slice(h * 128, (h + 1) * 128)
        b0 = h * PP_HEAD
        npv = min(128, N)
        msk = maskR if rv else maskF
        mkey = 'maskR' if rv else 'maskF'
        nch = N // CL
        lc = 0 if rv else CL - 1
        lastc = 0 if rv else N - 1
        V = {}
        steps = []

        def step(f):
            steps.append(f)
            return f

        cload = combine
        cstore = (dirn == 1)

        @step
        def s_xa():
            if combine:
                V['hbl'], V['khbl'] = t_hb.next()
                dma('sp', V['hbl'][:, 0:N], hb_scr[h, :, pr0:pr0 + N], [], [V['khbl']])
            if cload:
                xcb, kxcb = t_xcb.next()
                xc, kxc = t_xc.next()
                qs, kqs = t_qs.next()
                vb, kvb = t_vb.next()
                V.update(xcb=xcb, kxcb=kxcb, xc=xc, kxc=kxc, qs=qs, kqs=kqs, vb=vb, kvb=kvb)
                dma('sp', xcb[:, 0:N], xcb_scr[h, :, pr0:pr0 + N], [], [kxcb])
                dma('sp', xc[:, 0:N], xc_scr[h, :, pr0:pr0 + N], [], [kxc])
                dma('sp', qs[:, 0:N], qs_scr[h, :, pr0:pr0 + N], [], [kqs])
                dma('sp', vb[:, 0:N], vb_scr[h, pr0 // TN], [], [kvb])
                return
            ps_xa, kxa = gen_ring.next()
            ps_hl, khl = halo_ring.next()
            V.update(ps_xa=ps_xa, kxa=kxa, ps_hl=ps_hl, khl=khl)

            def mm_xa(e):
                ins = None
                for kc in range(8):
                    e.matmul(ps_xa[:, 0:N], lhsT=wmix[GXA][:, kc, hc], rhs=hT[:, kc, 0:N],
                             start=(kc == 0), stop=(kc == 7))
                for kc in range(8):
                    ins = e.matmul(ps_hl[:, 0:3], lhsT=wmix[GXA][:, kc, hc], rhs=hT[:, kc, N:N + 3],
                                   start=(kc == 0), stop=(kc == 7))
                return ins
            P.add('pe', mm_xa, [hT_key, ('wmix', GXA)], [kxa, khl])

        @step
        def s_ext():
            if cload:
                return
            ext, kext = t_ext.next()
            V.update(ext=ext, kext=kext)
            ps_xa, ps_hl = V['ps_xa'], V['ps_hl']
            P.add('act', lambda e: e.activation(out=ext[:, 2:2 + N], in_=ps_xa[:, 0:N], func=AF.Copy),
                  [V['kxa']], [kext])
            P.add('dve', lambda e: e.tensor_copy(out=ext[:, 0:2], in_=ps_hl[:, 0:2]), [V['khl']], [kext])
            P.add('dve', lambda e: e.tensor_copy(out=ext[:, N + 2:N + 3], in_=ps_hl[:, 2:3]), [V['khl']], [kext])

        @step
        def s_q():
            if cload:
                return
            ps_q, kq_ = gen_ring.next()
            V.update(ps_q=ps_q, kq_=kq_)

            def mm_q(e):
                ins = None
                for kc in range(8):
                    ins = e.matmul(ps_q[:, 0:N], lhsT=wmix[GQ][:, kc, hc], rhs=hT[:, kc, 0:N],
                                   start=(kc == 0), stop=(kc == 7))
                return ins
            P.add('pe', mm_q, [hT_key, ('wmix', GQ)], [kq_])

        @step
        def s_sg():
            if cload:
                return
            sg, ksg = t_sg.next()
            qs, kqs = t_qs.next()
            xc, kxc = t_xc.next()
            V.update(qs=qs, kqs=kqs, xc=xc, kxc=kxc)
            ps_q, ext = V['ps_q'], V['ext']
            P.add('act', lambda e: e.activation(out=qs[:, 0:N], in_=ps_q[:, 0:N], func=AF.Silu),
                  [V['kq_']], [kqs])
            P.add('dve', lambda e: e.tensor_scalar(
                out=xc[:, 0:N], in0=ext[:, 0:N], scalar1=col(pp, b0 + 0), scalar2=col(pp, b0 + 4),
                op0=ALU.mult, op1=ALU.add), [V['kext'], 'pp'], [kxc])

        @step
        def s_f():
            ps_f, kf_ = gen_ring.next()
            V.update(ps_f=ps_f, kf_=kf_)

            def mm_f(e):
                ins = None
                for kc in range(8):
                    ins = e.matmul(ps_f[:, 0:N], lhsT=wmix[GF][:, kc, hc], rhs=hT[:, kc, 0:N],
                                   start=(kc == 0), stop=(kc == 7))
                return ins
            P.add('pe', mm_f, [hT_key, ('wmix', GF)], [kf_])

        @step
        def s_sf():
            f_, kff = t_f.next()
            V.update(f_=f_, kff=kff)
            ps_f = V['ps_f']
            P.add('act', lambda e: e.activation(out=f_[:, 0:N], in_=ps_f[:, 0:N], func=AF.Sigmoid),
                  [V['kf_']], [kff])
            if cload:
                return
            ext, xc, kxc = V['ext'], V['xc'], V['kxc']
            P.add('dve', lambda e: e.scalar_tensor_tensor(
                out=xc[:, 0:N], in0=ext[:, 1:1 + N], scalar=col(pp, b0 + 1), in1=xc[:, 0:N],
                op0=ALU.mult, op1=ALU.add), [V['kext'], kxc, 'pp'], [kxc])

        @step
        def s_v():
            if cload:
                return
            ps_v, kv_ = gen_ring.next()
            V.update(ps_v=ps_v, kv_=kv_)

            def mm_v(e):
                ins = None
                for s in range(nsub):
                    npk = min(128, N - 128 * s)
                    for kc in range(8):
                        ins = e.matmul(ps_v[:npk, s * 128:(s + 1) * 128], lhsT=hT[:, kc, 128 * s:128 * s + npk],
                                       rhs=wmix[GV][:, kc, hc], start=(kc == 0), stop=(kc == 7))
                return ins
            P.add('pe', mm_v, [hT_key, ('wmix', GV)], [kv_])

        @step
        def s_vb():
            f_, kff = V['f_'], V['kff']
            P.add('dve', lambda e: e.tensor_scalar(
                out=f_[:, 0:N], in0=f_[:, 0:N], scalar1=col(dv, DV_OML + dirn * 8 + h),
                scalar2=col(dv, DV_LB + dirn * 8 + h), op0=ALU.mult, op1=ALU.add), [kff, 'dv'], [kff])
            if cload:
                return
            vb, kvb = t_vb.next()
            V.update(vb=vb, kvb=kvb)
            ps_v, ext, xc, kxc = V['ps_v'], V['ext'], V['xc'], V['kxc']
            P.add('act', lambda e: e.activation(out=vb[:npv, 0:nsub * 128], in_=ps_v[:npv, 0:nsub * 128],
                                                func=AF.Copy), [V['kv_']], [kvb])
            P.add('dve', lambda e: e.scalar_tensor_tensor(
                out=xc[:, 0:N], in0=ext[:, 2:2 + N], scalar=col(pp, b0 + 2), in1=xc[:, 0:N],
                op0=ALU.mult, op1=ALU.add), [V['kext'], kxc, 'pp'], [kxc])

        @step
        def s_conv3():
            if cload:
                return
            ext, xc, kxc = V['ext'], V['xc'], V['kxc']
            xcb, kxcb = t_xcb.next()
            V.update(xcb=xcb, kxcb=kxcb)
            P.add('dve', lambda e: e.scalar_tensor_tensor(
                out=xc[:, 0:N], in0=ext[:, 3:3 + N], scalar=col(pp, b0 + 3), in1=xc[:, 0:N],
                op0=ALU.mult, op1=ALU.add), [V['kext'], kxc, 'pp'], [kxc])
            P.add('dve', lambda e: e.tensor_copy(out=xcb[:, 0:N], in_=xc[:, 0:N]), [kxc], [kxcb])
            if cstore:
                dma('sp', xcb_scr[h, :, pr0:pr0 + N], xcb[:, 0:N], [kxcb], [])
                dma('sp', xc_scr[h, :, pr0:pr0 + N], xc[:, 0:N], [kxc], [])
                dma('sp', qs_scr[h, :, pr0:pr0 + N], V['qs'][:, 0:N], [V['kqs']], [])
                dma('sp', vb_scr[h, pr0 // TN], V['vb'][:, 0:N], [V['kvb']], [])

        @step
        def s_gr():
            ps_r, kr_ = gen_ring.next()
            V.update(ps_r=ps_r, kr_=kr_)
            xcb = V['xcb']
            P.add('pe', lambda e: e.matmul(ps_r[:, 0:N], lhsT=wg[:, 0, h, :], rhs=xcb[:, 0:N],
                                           start=True, stop=True), [V['kxcb'], 'wg'], [kr_])

        @step
        def s_r():
            r_, krr = t_r.next()
            V.update(r_=r_, krr=krr)
            ps_r = V['ps_r']
            P.add('act', lambda e: e.activation(out=r_[:, 0:N], in_=ps_r[:, 0:N], func=AF.Sigmoid,
                                                bias=col(pp, b0 + 5 + dirn)), [V['kr_'], 'pp'], [krr])

        @step
        def s_gi():
            ps_i, ki_ = gen_ring.next()
            V.update(ps_i=ps_i, ki_=ki_)
            xcb = V['xcb']
            P.add('pe', lambda e: e.matmul(ps_i[:, 0:N], lhsT=wg[:, 1, h, :], rhs=xcb[:, 0:N],
                                           start=True, stop=True), [V['kxcb'], 'wg'], [ki_])

        @step
        def s_i():
            i_, kii = t_i.next()
            V.update(i_=i_, kii=kii)
            ps_i = V['ps_i']
            P.add('act', lambda e: e.activation(out=i_[:, 0:N], in_=ps_i[:, 0:N], func=AF.Sigmoid,
                                                bias=col(pp, b0 + 7 + dirn)), [V['ki_'], 'pp'], [kii])

        @step
        def s_g():
            g_, kgg = t_g.next()
            b_, kbb = t_b.next()
            V.update(b_=b_, kbb=kbb)
            f_ = V['f_']
            P.add('act', lambda e: e.activation(out=g_[:, 0:N], in_=f_[:, 0:N], func=AF.Ln), [V['kff']], [kgg])
            P.add('dve', lambda e: e.tensor_tensor_scan(
                out=R(b_[:, 0:N]), data0=R(msk[:, 0:N]), data1=R(g_[:, 0:N]), initial=0.0,
                op0=ALU.mult, op1=ALU.add), [kgg, mkey], [kbb])

        @step
        def s_a2():
            a2, ka2 = t_a2.next()
            V.update(a2=a2, ka2=ka2)
            r_, i_, xc = V['r_'], V['i_'], V['xc']
            P.add('act', lambda e: e.activation(out=a2[:, 0:N], in_=r_[:, 0:N], func=AF.Exp,
                                                scale=col(dv, DV_SP2 + dirn * 8 + h)), [V['krr'], 'dv'], [ka2])
            P.add('dve', lambda e: e.tensor_tensor(out=i_[:, 0:N], in0=i_[:, 0:N], in1=xc[:, 0:N], op=ALU.mult),
                  [V['kii'], V['kxc']], [V['kii']])

        @step
        def s_abs():
            a2, ka2 = V['a2'], V['ka2']
            P.add('act', lambda e: e.activation(out=a2[:, 0:N], in_=a2[:, 0:N], func=AF.Abs, scale=-1.0, bias=1.0),
                  [ka2], [ka2])

        @step
        def s_eb():
            eb, keb = t_eb.next()
            V.update(eb=eb, keb=keb)
            b_ = V['b_']
            P.add('act', lambda e: e.activation(out=eb[:, 0:N], in_=b_[:, 0:N], func=AF.Exp), [V['kbb']], [keb])

        @step
        def s_ln():
            a2, ka2 = V['a2'], V['ka2']
            P.add('act', lambda e: e.activation(out=a2[:, 0:N], in_=a2[:, 0:N], func=AF.Ln, bias=col(dv, DV_TINY)),
                  [ka2, 'dv'], [ka2])

        @step
        def s_enb():
            enb, kenb = t_enb.next()
            qt, kqt = t_qt.next()
            V.update(enb=enb, kenb=kenb, qt=qt, kqt=kqt)
            b_, qs, eb = V['b_'], V['qs'], V['eb']
            P.add('act', lambda e: e.activation(out=enb[:, 0:N], in_=b_[:, 0:N], func=AF.Exp, scale=-1.0),
                  [V['kbb']], [kenb])
            P.add('pool', lambda e: e.tensor_tensor(out=qt[:, 0:N], in0=qs[:, 0:N], in1=eb[:, 0:N], op=ALU.mult),
                  [V['kqs'], V['keb']], [kqt])

        @step
        def s_sqrt():
            a2, ka2 = V['a2'], V['ka2']
            P.add('act', lambda e: e.activation(out=a2[:, 0:N], in_=a2[:, 0:N], func=AF.Exp, scale=0.5),
                  [ka2], [ka2])

        @step
        def s_a():
            a_, kaa = t_a.next()
            kt, kkt = t_kt.next()
            V.update(a_=a_, kaa=kaa, kt=kt, kkt=kkt)
            r_, f_, enb = V['r_'], V['f_'], V['enb']
            P.add('act', lambda e: e.activation(out=a_[:, 0:N], in_=r_[:, 0:N], func=AF.Exp,
                                                scale=col(dv, DV_SP + dirn * 8 + h)), [V['krr'], 'dv'], [kaa])
            P.add('dve', lambda e: e.scalar_tensor_tensor(
                out=kt[:, 0:N], in0=f_[:, 0:N], scalar=1.0, in1=enb[:, 0:N], op0=ALU.subtract, op1=ALU.mult),
                [V['kff'], V['kenb']], [kkt])

        @step
        def s_u():
            u_, kuu = t_u.next()
            V.update(u_=u_, kuu=kuu)
            a2, i_ = V['a2'], V['i_']
            P.add('dve', lambda e: e.tensor_tensor(out=u_[:, 0:N], in0=a2[:, 0:N], in1=i_[:, 0:N], op=ALU.mult),
                  [V['ka2'], V['kii']], [kuu])

        @step
        def s_kh():
            kh, kkh = t_kh.next()
            kt, eb = V['kt'], V['eb']
            eb3 = eb[:, 0:N].rearrange("p (c t) -> p c t", t=CL)
            P.add('dve', lambda e: e.tensor_tensor(
                out=kh[:, 0:N].rearrange("p (c t) -> p c t", t=CL),
                in0=kt[:, 0:N].rearrange("p (c t) -> p c t", t=CL),
                in1=eb3[:, :, lc:lc + 1].broadcast_to([128, nch, CL]), op=ALU.mult), [V['kkt'], V['keb']], [kkh])
            st.update(qt=V['qt'], kqt=V['kqt'], kt=kt, kkt=V['kkt'], kh=kh, kkh=kkh, vb=V['vb'], kvb=V['kvb'],
                      eb=eb, keb=V['keb'])

        @step
        def s_scan():
            hh, khh = t_h.next()
            V.update(hh=hh, khh=khh)
            a_, u_ = V['a_'], V['u_']
            P.add('dve', lambda e: e.tensor_tensor_scan(
                out=R(hh[:, 0:N]), data0=R(a_[:, 0:N]), data1=R(u_[:, 0:N]), initial=col(carryA, h),
                op0=ALU.mult, op1=ALU.add), [V['kaa'], V['kuu'], ('carryA', h)], [khh])
            P.add('pool', lambda e: e.tensor_copy(out=col(carryA, h), in_=hh[:, lastc:lastc + 1]),
                  [khh], [('carryA', h)])

        @step
        def s_out():
            hh, khh = V['hh'], V['khh']
            if dirn == 1:
                dma('sp', hb_scr[h, :, pr0:pr0 + N], hh[:, 0:N], [khh], [])
            elif combine:
                hbl, khbl = V['hbl'], V['khbl']
                P.add('pool', lambda e: e.tensor_tensor(out=hh[:, 0:N], in0=hh[:, 0:N], in1=hbl[:, 0:N], op=ALU.add),
                      [khh, khbl], [khh])
                dma('sp', hs_scr[h, :, pr0:pr0 + N], hh[:, 0:N], [khh], [])
        return steps

    def stage1_pair(dirn, p0, N, hT, hT_key, meta, h0, sts):
        sa = stage1_steps(dirn, p0, N, hT, hT_key, meta, h0, sts[0])
        sb = stage1_steps(dirn, p0, N, hT, hT_key, meta, h0 + 1, sts[1])
        for fa, fb in zip(sa, sb):
            fa()
            yield
            fb()
            yield

    def stage2(dirn, p0, N, meta, h, st, chain):
        rv = (dirn == 1)
        nsub = (N + 127) // 128
        CL = min(64, N)
        pr0 = p0 - NMETA
        combine = (dirn == 0) and not meta
        lc = 0 if rv else CL - 1
        qt, kqt, kt, kkt, kh, kkh = st['qt'], st['kqt'], st['kt'], st['kkt'], st['kh'], st['kkh']
        vb, kvb, eb, keb = st['vb'], st['kvb'], st['eb'], st['keb']
        ps_o, kpo = ch_o[chain]
        mbank, kmb = ch_m[chain]
        if combine:
            ob, kob = t_ob[chain].next()
            dma('sp', ob[:, 0:N], ob_scr[h, :, pr0:pr0 + N], [], [kob])
        msc = mscR if rv else mscF
        msck = 'mscR' if rv else 'mscF'
        sub_order = range(nsub - 1, -1, -1) if rv else range(nsub)
        for s in sub_order:
            npk = min(128, N - 128 * s)
            t0 = 128 * s
            ps_kT, kkT = mbank[:, 256:320].bitcast(BF16), kmb
            P.add('pe', lambda e, ps_kT=ps_kT, t0=t0, npk=npk: e.transpose(
                out=ps_kT[:npk, 0:128], in_=kh[:, t0:t0 + npk], identity=ident), [kkh, 'ident'], [kkT])
            khT, kkhT = khT4[chain][s], ('khT4', chain, s)
            P.add('act', lambda e, khT=khT, ps_kT=ps_kT, npk=npk: e.activation(
                out=khT[:npk, :], in_=ps_kT[:npk, 0:128], func=AF.Copy), [kkT], [kkhT])
            ps_sc, ksc = mbank[:, 0:128], kmb
            P.add('pe', lambda e, ps_sc=ps_sc, t0=t0, npk=npk: e.matmul(
                ps_sc[:npk, 0:npk], lhsT=kt[:, t0:t0 + npk], rhs=qt[:, t0:t0 + npk], start=True, stop=True),
                [kkt, kqt], [ksc])
            yield
            PT, kPT = PT4[chain][s], ('PT4', chain, s)
            P.add('dve', lambda e, PT=PT, ps_sc=ps_sc, npk=npk: e.tensor_tensor(
                out=PT[:npk, 0:npk], in0=ps_sc[:npk, 0:npk], in1=msc[:npk, 0:npk], op=ALU.mult),
                [ksc, msck], [kPT])
            yield
            ncs = npk // CL
            ch_order = range(ncs - 1, -1, -1) if rv else range(ncs)
            for c in ch_order:
                c0 = c * CL

                def mm_o(e, PT=PT, s=s, t0=t0, c0=c0, npk=npk):
                    e.matmul(ps_o[:, t0 + c0:t0 + c0 + CL], lhsT=vb[:npk, s * 128:(s + 1) * 128],
                             rhs=PT[:npk, c0:c0 + CL], start=True, stop=False)
                    return e.matmul(ps_o[:, t0 + c0:t0 + c0 + CL], lhsT=S_b[:, h, :],
                                    rhs=qt[:, t0 + c0:t0 + c0 + CL], start=False, stop=True)
                P.add('pe', mm_o, [kPT, kvb, kqt, ('Sb', h)], [kpo])
                ps_dS, kdS = mbank[:, 128:256], kmb
                P.add('pe', lambda e, ps_dS=ps_dS, khT=khT, s=s, c0=c0: e.matmul(
                    ps_dS[:, 0:128], lhsT=khT[c0:c0 + CL, :], rhs=vb[c0:c0 + CL, s * 128:(s + 1) * 128],
                    start=True, stop=True), [kkhT, kvb], [kdS])
                yield
                dcol = t0 + c0 + lc
                P.add('dve', lambda e, ps_dS=ps_dS, dcol=dcol: e.scalar_tensor_tensor(
                    out=S_b[:, h, :], in0=S_f[:, h, :], scalar=eb[:, dcol:dcol + 1], in1=ps_dS[:, 0:128],
                    op0=ALU.mult, op1=ALU.subtract), [kdS, keb, ('Sf', h)], [('Sb', h)])
                P.add('dve', lambda e, ps_dS=ps_dS, dcol=dcol: e.scalar_tensor_tensor(
                    out=S_f[:, h, :], in0=S_f[:, h, :], scalar=eb[:, dcol:dcol + 1], in1=ps_dS[:, 0:128],
                    op0=ALU.mult, op1=ALU.subtract), [kdS, keb, ('Sf', h)], [('Sf', h)])
                yield
        if dirn == 1:
            ob, kob = t_ob[chain].next()
            P.add('act', lambda e: e.activation(out=ob[:, 0:N], in_=ps_o[:, 0:N], func=AF.Copy), [kpo], [kob])
            dma('sp', ob_scr[h, :, pr0:pr0 + N], ob[:, 0:N], [kob], [])
            yield
        elif combine:
            osm, kos = t_os[chain].next()
            P.add('dve', lambda e: e.tensor_tensor(out=osm[:, 0:N], in0=ps_o[:, 0:N], in1=ob[:, 0:N], op=ALU.add),
                  [kpo, kob], [kos])
            yield
            osq, kosq = t_osq[chain].next()
            P.add('act', lambda e: e.activation(out=osq[:, 0:N], in_=osm[:, 0:N], func=AF.Square), [kos], [kosq])
            yield
            ps_ss, kpss = mbank, kmb
            P.add('pe', lambda e: e.matmul(ps_ss[:, 0:N], lhsT=ones_f, rhs=osq[:, 0:N], start=True, stop=True),
                  [kosq, 'ones'], [kpss])
            yield
            rso, krso = t_rso[chain].next()
            P.add('act', lambda e: e.activation(out=rso[:, 0:N], in_=ps_ss[:, 0:N], func=AF.Ln,
                                                bias=col(dv, DV_EPS128)), [kpss, 'dv'], [krso])
            yield
            P.add('act', lambda e: e.activation(out=rso[:, 0:N], in_=rso[:, 0:N], func=AF.Exp, scale=-0.5),
                  [krso], [krso])
            yield
            P.add('pool', lambda e: e.tensor_tensor(out=osm[:, 0:N], in0=osm[:, 0:N], in1=rso[:, 0:N], op=ALU.mult),
                  [kos, krso], [kos])
            dma('sp', on_scr[h, :, pr0:pr0 + N], osm[:, 0:N], [kos], [])
            yield

    def interleave(*gens):
        gens = [g for g in gens if g is not None]
        while gens:
            for g in list(gens):
                try:
                    next(g)
                except StopIteration:
                    gens.remove(g)

    def stage_x_gen(*a, **k):
        stage_x(*a, **k)
        yield

    def init_states():
        P.add('pool', lambda e: e.memset(carryA, 0.0), [], [('carryA', h) for h in range(8)])
        P.add('pool', lambda e: e.memset(S_f.rearrange("p a b -> p (a b)"), 0.0), [], [('Sf', h) for h in range(8)])
        P.add('pool', lambda e: e.memset(S_b.rearrange("p a b -> p (a b)"), 0.0), [], [('Sb', h) for h in range(8)])

    def run_mixer_pass(dirn, tiles):
        hT, hT_key = hT_bufs[0], ('hT', 0)
        dma('pool', wg.rearrange("p g h j -> p (g h j)"), w_gate[:, dirn * 2048:(dirn + 1) * 2048], [], ['wg'])
        dma('pool', wmix[GF].rearrange("p a b -> p (a b)"), w_mix[3 if dirn == 1 else 2], [], [('wmix', GF)])

        def do_x(p0, N):
            if dirn == 0 and p0 > 0:
                dma('sp', hT[:, :, 0:N], hT_scr[(p0 - NMETA) // TN], [], [hT_key])
                return
            nsub = (N + 127) // 128
            xl = [xt_ring.next() for _ in range(nsub)]
            pb, pbk = gen_ring.next()
            stage_x(p0, N, xl, hT, hT_key, PP_GMIX, True, pb, pbk)
            if dirn == 1:
                dma('sp', hT_scr[(p0 - NMETA) // TN], hT[:, :, 0:N], [hT_key], [])

        def s1_pair(p0, N, meta, h0, sts, with_x):
            if with_x:
                do_x(p0, N)
                yield
            yield from stage1_pair(dirn, p0, N, hT, hT_key, meta, h0, sts)
        p0, N, meta = tiles[0]
        sts = [{}, {}]
        interleave(s1_pair(p0, N, meta, 0, sts, True))
        for ti, (p0, N, meta) in enumerate(tiles):
            for pr in range(4):
                sts_next = [{}, {}]
                if pr < 3:
                    nxt = s1_pair(p0, N, meta, 2 * pr + 2, sts_next, False)
                elif ti + 1 < len(tiles):
                    nxt = s1_pair(*tiles[ti + 1], 0, sts_next, True)
                else:
                    nxt = None
                interleave(stage2(dirn, p0, N, meta, 2 * pr, sts[0], 0),
                           stage2(dirn, p0, N, meta, 2 * pr + 1, sts[1], 1), nxt)
                sts = sts_next

    init_states()
    if stop_after >= 1:
        run_mixer_pass(1, [(NMETA + ti * TN, TN, False) for ti in range(NT - 1, -1, -1)])
    P.barrier()
    init_states()
    if stop_after >= 2:
        run_mixer_pass(0, [(0, NMETA, True)] + [(NMETA + ti * TN, TN, False) for ti in range(NT)])
    P.barrier()

    A.reset(base_mark)
    wout_sb = A.alloc(8 * 1024, BF16, shape3=1024)
    gfin = A.alloc(1024, F32)
    dma('sp', gfin, gfin_d, [], ['gfin'])
    P.add('pool', lambda e: e.tensor_scalar(out=gfin, in0=gfin, scalar1=32.0, scalar2=None, op0=ALU.mult),
          ['gfin'], ['gfin'])
    dma('pool', wout_sb.rearrange("p a b -> p (a b)"), w_out, [], ['wout'])
    xt4 = [A.alloc(1024, F32) for _ in range(4)]
    xn_ring = Ring([A.alloc(1024, BF16) for _ in range(2)], 'xn2')
    hT2 = A.alloc(8 * TN, BF16, shape3=TN)
    hs_ring = tmp(name='hsl')
    on_ring = tmp(name='onl')
    braT = A.alloc(8 * TN, BF16, shape3=TN)
    brbT = A.alloc(8 * TN, BF16, shape3=TN)
    mrgT = A.alloc(8 * TN, BF16, shape3=TN)
    h2T = A.alloc(8 * TN, BF16, shape3=TN)
    actT = A.alloc(32 * TN, BF16, shape3=TN)
    wring = Ring([A.alloc(1024, BF16, shape3=128) for _ in range(8)], 'wch')
    w2ring = Ring([A.alloc(1024, BF16) for _ in range(6)], 'w2ch')
    t_e = tmp(name='e')
    t_t = tmp(cnt=1, name='t')
    t_so = tmp(name='so')
    t_t2 = tmp(cnt=1, name='t2')
    t_sga = tmp(name='sga')
    t_sgb = tmp(name='sgb')
    t_m1 = tmp(cnt=1, name='m1')
    t_m2 = tmp(cnt=1, name='m2')
    t_rl = tmp(dt=BF16, name='rl')
    gen2 = Ring(banks[0:4], 'psg2')
    acc_keys = [('psacc', i) for i in range(4)]
    acc = banks[4:8]

    def wchunk(cid):
        w, wk = wring.next()
        dma('sp', w.rearrange("p a b -> p (a b)"), wf2_bf[cid], [('wf2', cid // 8 * 8)], [wk])
        return w, wk

    def proj(w, wk, src, src_key, ps, psk):
        def mm(e):
            ins = None
            for kc in range(8):
                ins = e.matmul(ps[:, 0:TN], lhsT=w[:, kc, :], rhs=src[:, kc, 0:TN], start=(kc == 0), stop=(kc == 7))
            return ins
        P.add('pe', mm, [wk, src_key], [psk])

    for ti in range(NT if stop_after >= 3 else 0):
        p0 = NMETA + ti * TN
        pr0 = ti * TN
        dma('sp', hT2, hT_scr[ti], [], ['hT2'])
        for s_ in range(4):
            dma('sp', xt4[s_], xs[p0 + 128 * s_:p0 + 128 * (s_ + 1), :], [], [('xt4', s_)])
        for j in range(8):
            w, wk = wchunk(0 + j)
            ps, psk = gen2.next()
            proj(w, wk, hT2, 'hT2', ps, psk)
            e_, ke = t_e.next()
            P.add('act', lambda e, e_=e_, ps=ps: e.activation(out=e_, in_=ps[:, 0:TN], func=AF.Gelu), [psk], [ke])
            hsl, khsl = hs_ring.next()
            dma('sp', hsl, hs_scr[j, :, pr0:pr0 + TN], [], [khsl])
            P.add('pool', lambda e, e_=e_, j=j, hsl=hsl: e.tensor_tensor(
                out=braT[:, j, :], in0=hsl, in1=e_, op=ALU.mult), [khsl, ke], [('braT', j)])
        for j in range(8):
            w, wk = wchunk(8 + j)
            ps, psk = gen2.next()
            proj(w, wk, hT2, 'hT2', ps, psk)
            so, kso = t_so.next()
            P.add('act', lambda e, so=so, ps=ps: e.activation(out=so, in_=ps[:, 0:TN], func=AF.Sigmoid),
                  [psk], [kso])
            t2, kt2 = t_t2.next()
            P.add('dve', lambda e, t2=t2, so=so, ps=ps, j=j: e.scalar_tensor_tensor(
                out=t2, in0=so, scalar=col(dv, DV_HGG + j), in1=ps[:, 0:TN], op0=ALU.mult, op1=ALU.mult),
                [kso, psk, 'dv'], [kt2])
            onl, konl = on_ring.next()
            dma('sp', onl, on_scr[j, :, pr0:pr0 + TN], [], [konl])
            P.add('pool', lambda e, t2=t2, j=j, onl=onl: e.tensor_tensor(
                out=brbT[:, j, :], in0=onl, in1=t2, op=ALU.mult), [konl, kt2], [('brbT', j)])
        bra_keys = [('braT', j) for j in range(8)]
        brb_keys = [('brbT', j) for j in range(8)]
        for j in range(8):
            w, wk = wchunk(16 + j)
            ps_ga, kga = gen2.next()
            proj(w, wk, hT2, 'hT2', ps_ga, kga)
            sga, ksga = t_sga.next()
            P.add('act', lambda e, sga=sga, ps_ga=ps_ga: e.activation(out=sga, in_=ps_ga[:, 0:TN], func=AF.Sigmoid),
                  [kga], [ksga])
            w, wk = wchunk(24 + j)
            ps_gb, kgb = gen2.next()
            proj(w, wk, hT2, 'hT2', ps_gb, kgb)
            sgb, ksgb = t_sgb.next()
            P.add('act', lambda e, sgb=sgb, ps_gb=ps_gb: e.activation(out=sgb, in_=ps_gb[:, 0:TN], func=AF.Sigmoid),
                  [kgb], [ksgb])
            w, wk = wchunk(32 + j)
            ps_pa, kpa = gen2.next()

            def mm_pa(e, w=w, ps_pa=ps_pa):
                ins = None
                for kc in range(8):
                    ins = e.matmul(ps_pa[:, 0:TN], lhsT=w[:, kc, :], rhs=braT[:, kc, :], start=(kc == 0), stop=(kc == 7))
                return ins
            P.add('pe', mm_pa, [wk] + bra_keys, [kpa])
            m1, km1 = t_m1.next()
            P.add('dve', lambda e, m1=m1, ps_pa=ps_pa, sga=sga: e.tensor_tensor(
                out=m1, in0=ps_pa[:, 0:TN], in1=sga, op=ALU.mult), [kpa, ksga], [km1])
            w, wk = wchunk(40 + j)
            ps_pb, kpb = gen2.next()

            def mm_pb(e, w=w, ps_pb=ps_pb):
                ins = None
                for kc in range(8):
                    ins = e.matmul(ps_pb[:, 0:TN], lhsT=w[:, kc, :], rhs=brbT[:, kc, :], start=(kc == 0), stop=(kc == 7))
                return ins
            P.add('pe', mm_pb, [wk] + brb_keys, [kpb])
            m2, km2 = t_m2.next()
            P.add('dve', lambda e, m2=m2, ps_pb=ps_pb, sgb=sgb: e.tensor_tensor(
                out=m2, in0=ps_pb[:, 0:TN], in1=sgb, op=ALU.mult), [kpb, ksgb], [km2])
            P.add('pool', lambda e, m1=m1, m2=m2, j=j: e.tensor_tensor(
                out=mrgT[:, j, :], in0=m1, in1=m2, op=ALU.add), [km1, km2], [('mrgT', j)])
        mrg_keys = [('mrgT', j) for j in range(8)]
        for s in range(4):
            for hf in range(2):
                pa_, pak = acc[(s * 2 + hf) % 4], acc_keys[(s * 2 + hf) % 4]

                def mm_wo(e, s=s, hf=hf, pa_=pa_):
                    ins = None
                    for kc in range(8):
                        ins = e.matmul(pa_[:, 0:512], lhsT=mrgT[:, kc, s * 128:(s + 1) * 128],
                                       rhs=wout_sb[:, kc, hf * 512:(hf + 1) * 512], start=(kc == 0), stop=(kc == 7))
                    return ins
                P.add('pe', mm_wo, mrg_keys + ['wout'], [pak])
                P.add('dve', lambda e, s=s, hf=hf, pa_=pa_: e.tensor_tensor(
                    out=xt4[s][:, hf * 512:(hf + 1) * 512], in0=pa_[:, 0:512],
                    in1=xt4[s][:, hf * 512:(hf + 1) * 512], op=ALU.add), [pak, ('xt4', s)], [('xt4', s)])
        for s in range(4):
            xt, xk = xt4[s], ('xt4', s)
            P.add('pool', lambda e, s=s: e.memset(col(ss, s), 0.0), [], [('ss', s)])
            P.add('act', lambda e, xt=xt, s=s: e.activation(out=junk, in_=xt, func=AF.Square,
                                                            accum_out=ss[:, s:s + 1]), [xk, ('ss', s)],
                  ['junk', ('ss', s)])
            P.add('act', lambda e, s=s: e.activation(out=rs[:, s:s + 1], in_=ss[:, s:s + 1], func=AF.Ln, bias=dv[:, DV_EPSD:DV_EPSD + 1]), [('ss', s), 'dv'], [('rs', s)])
            P.add('act', lambda e, s=s: e.activation(out=rs[:, s:s + 1], in_=rs[:, s:s + 1], func=AF.Exp, scale=-0.5), [('rs', s)], [('rs', s)])
            xn, xnk = xn_ring.next()
            P.add('dve', lambda e, xt=xt, xn=xn, s=s: e.tensor_scalar(
                out=xn, in0=xt, scalar1=rs[:, s:s + 1], scalar2=32.0, op0=ALU.mult, op1=ALU.mult),
                [xk, ('rs', s)], [xnk])
            pb, pbk = gen2.next()
            psT = pb.bitcast(BF16).rearrange("p (a b) -> p a b", b=128)

            def tr2(e, xn=xn, psT=psT):
                ins = None
                for kc in range(8):
                    ins = e.transpose(out=psT[:, kc, :], in_=xn[:, kc * 128:(kc + 1) * 128], identity=ident)
                return ins
            P.add('pe', tr2, [xnk, 'ident'], [pbk])
            P.add('dve', lambda e, psT=psT, s=s: e.tensor_tensor(
                out=h2T[:, :, s * 128:(s + 1) * 128], in0=psT,
                in1=pp[:, PP_GMLP:PP_GMLP + 8].unsqueeze(2).broadcast_to([128, 8, 128]), op=ALU.mult),
                [pbk, 'pp'], ['h2T'])
        for m in range(32):
            w, wk = wchunk(48 + m)
            ps, psk = gen2.next()
            proj(w, wk, h2T, 'h2T', ps, psk)
            rl, krl = t_rl.next()
            P.add('act', lambda e, rl=rl, ps=ps: e.activation(out=rl, in_=ps[:, 0:TN], func=AF.Relu), [psk], [krl])
            P.add('dve' if m % 2 == 0 else 'pool', lambda e, rl=rl, m=m: e.tensor_tensor(
                out=actT[:, m, :], in0=rl, in1=rl, op=ALU.mult), [krl], [('actT', m)])
        for hf in range(2):
            for mp in range(16):
                w2c, w2k = w2ring.next()
                q = hf * 16 + mp
                dma('sp', w2c, w2_bf[q], [('w2s', q // 8 * 8)], [w2k])
                for mm in range(2):
                    m = 2 * mp + mm

                    def mm_2(e, m=m, mm=mm, w2c=w2c):
                        ins = None
                        for s in range(4):
                            ins = e.matmul(acc[s][:, 0:512], lhsT=actT[:, m, s * 128:(s + 1) * 128],
                                           rhs=w2c[:, mm * 512:(mm + 1) * 512], start=(m == 0), stop=(m == 31))
                        return ins
                    P.add('pe', mm_2, [w2k, ('actT', m)], acc_keys)
            for s in range(4):
                P.add('dve', lambda e, s=s, hf=hf: e.tensor_tensor(
                    out=xt4[s][:, hf * 512:(hf + 1) * 512], in0=acc[s][:, 0:512],
                    in1=xt4[s][:, hf * 512:(hf + 1) * 512], op=ALU.add), [acc_keys[s], ('xt4', s)], [('xt4', s)])
        for s in range(4):
            xt, xk = xt4[s], ('xt4', s)
            P.add('pool', lambda e, s=s: e.memset(col(ss, 4 + s % 2), 0.0), [], [('ss', 4 + s % 2)])
            P.add('act', lambda e, xt=xt, s=s: e.activation(out=junk, in_=xt, func=AF.Square,
                                                            accum_out=ss[:, 4 + s % 2:5 + s % 2]),
                  [xk, ('ss', 4 + s % 2)], ['junk', ('ss', 4 + s % 2)])
            P.add('act', lambda e, s=s: e.activation(out=rs[:, 4 + s % 2:5 + s % 2], in_=ss[:, 4 + s % 2:5 + s % 2], func=AF.Ln, bias=dv[:, DV_EPSD:DV_EPSD + 1]), [('ss', 4 + s % 2), 'dv'], [('rs', 4 + s % 2)])
            P.add('act', lambda e, s=s: e.activation(out=rs[:, 4 + s % 2:5 + s % 2], in_=rs[:, 4 + s % 2:5 + s % 2], func=AF.Exp, scale=-0.5), [('rs', 4 + s % 2)], [('rs', 4 + s % 2)])
            P.add('dve', lambda e, xt=xt, s=s: e.scalar_tensor_tensor(
                out=xt, in0=xt, scalar=rs[:, 4 + s % 2:5 + s % 2], in1=gfin, op0=ALU.mult, op1=ALU.mult),
                [xk, ('rs', 4 + s % 2), 'gfin'], [xk])
            dma('sp', y_d[pr0 + s * 128:pr0 + (s + 1) * 128, :], xt, [xk], [('yout', ti, s)])
    P.barrier()

    with (nc.semaphore("s_pe") as s_pe, nc.semaphore("s_act") as s_act, nc.semaphore("s_dve") as s_dve,
          nc.semaphore("s_pool") as s_pool):
        import contextlib
        with contextlib.ExitStack() as st:
            dsems = dict(sp=[st.enter_context(nc.semaphore(f"s_dma{i}")) for i in range(NDSEM)],
                         pool=[st.enter_context(nc.semaphore(f"s_dmap{i}")) for i in range(8)])
            sems = dict(pe=s_pe, act=s_act, dve=s_dve, pool=s_pool)
            P.finalize(nc, sems, dsems)
            with nc.Block() as block:
                @block.sync
                def _(e):
                    P.emit('sp', e)

                @block.tensor
                def _(e):
                    P.emit('pe', e)

                @block.scalar
                def _(e):
                    P.emit('act', e)

                @block.vector
                def _(e):
                    P.emit('dve', e)

                @block.gpsimd
                def _(e):
                    P.emit('pool', e)
    return nc


def _pack_params(conv_w, conv_b, rg_ba, rg_bx, rg_lambda, hg_lb_logits, hg_norm_g, norm_mix_g, norm_mlp_g):
    pp = np.zeros((128, PP_N), np.float32)
    for h in range(8):
        sl = slice(h * 128, (h + 1) * 128)
        b0 = h * PP_HEAD
        for j in range(4):
            pp[:, b0 + j] = conv_w[0, j, sl]
        pp[:, b0 + 4] = conv_b[0, sl]
        for dr in range(2):
            pp[:, b0 + 5 + dr] = rg_ba[0, dr, sl]
            pp[:, b0 + 7 + dr] = rg_bx[0, dr, sl]
            pp[:, b0 + 9 + dr] = rg_lambda[0, dr, sl]
            pp[:, b0 + 11 + dr] = hg_lb_logits[0, dr, sl]
            pp[:, b0 + 13 + dr] = hg_lb_logits[1, dr, sl]
        pp[:, b0 + 15] = hg_norm_g[0, sl]
    for kc in range(8):
        pp[:, PP_GMIX + kc] = norm_mix_g[0, kc * 128:(kc + 1) * 128]
        pp[:, PP_GMLP + kc] = norm_mlp_g[0, kc * 128:(kc + 1) * 128]
    return pp


def _consts():
    c = np.zeros((128, 128 * 4 + 1024), np.float32)
    c[:, 0:128] = np.eye(128, dtype=np.float32)
    c[:, 128:256] = 1.0
    s = np.arange(128)[:, None]
    t = np.arange(128)[None, :]
    same = (s // 64) == (t // 64)
    c[:, 256:384] = -1.0 * (same & (s <= t))
    c[:, 384:512] = -1.0 * (same & (s >= t))
    mF = np.ones(512, np.float32)
    mF[0::64] = 0.0
    mR = np.ones(512, np.float32)
    mR[63::64] = 0.0
    c[:, 512:1024] = mF[None, :]
    c[:, 1024:1536] = mR[None, :]
    return c


def _kc_layout(w):
    return np.ascontiguousarray(w.reshape(8, 128, -1).transpose(1, 0, 2))


def kernel(x_prompt, x_sample, meta_tokens, hg_lb_logits, norm_mix_g, w_in, conv_w, conv_b, rg_wa, rg_ba,
           rg_wx, rg_bx, rg_lambda, hg_norm_g, w_branch_a, w_branch_b, w_out, norm_mlp_g, w_mlp1, w_mlp2,
           final_norm_g):
    f = lambda a: np.asarray(a, dtype=np.float32)
    x_prompt, x_sample, meta_tokens = f(x_prompt), f(x_sample), f(meta_tokens)
    T = x_prompt.shape[1]
    NT = T // TN
    seqs = [x_prompt[i] for i in range(x_prompt.shape[0])] + [x_sample[i] for i in range(x_sample.shape[0])]
    assert len(seqs) <= NCORES
    win = f(w_in)[0]
    grp = lambda g: win[:, g * 1024:(g + 1) * 1024]
    w_mix = np.stack([_kc_layout(grp(g)).reshape(128, 8 * 1024) for g in (0, 2, 3, 4, 5)])
    wa, wx = f(rg_wa)[0], f(rg_wx)[0]
    wgate = np.stack([wa[0], wx[0], wa[1], wx[1]])
    wgate = np.ascontiguousarray(wgate.transpose(2, 0, 1, 3)).reshape(128, 4 * 8 * 128)

    def chunks(w):
        k = _kc_layout(w)
        C = k.shape[2]
        return np.ascontiguousarray(k.reshape(128, 8, C // 128, 128).transpose(2, 0, 1, 3)).reshape(C // 128, 128, 1024)
    w_f2 = np.concatenate([chunks(grp(1)), chunks(grp(6)), chunks(grp(7)), chunks(grp(8)),
                           chunks(f(w_branch_a)[0]), chunks(f(w_branch_b)[0]), chunks(f(w_mlp1)[0])], axis=0)
    wout_l = _kc_layout(f(w_out)[0]).reshape(128, 8 * 1024)
    w2_l = np.ascontiguousarray(f(w_mlp2)[0].reshape(16, 2, 128, 2, 512).transpose(3, 0, 2, 1, 4)).reshape(32, 128, 1024)
    pp = _pack_params(f(conv_w), f(conv_b), f(rg_ba), f(rg_bx), f(rg_lambda), f(hg_lb_logits), f(hg_norm_g),
                      f(norm_mix_g), f(norm_mlp_g))
    gfin = np.ascontiguousarray(np.broadcast_to(f(final_norm_g)[None, :], (128, D)))
    consts = _consts()
    shared = dict(w_mix=w_mix, w_gate=wgate, w_f2=w_f2, w_out=wout_l, w_2=w2_l, pp=pp, gfin=gfin, consts=consts)
    in_maps = []
    for c in range(NCORES):
        if c < len(seqs):
            xs = np.concatenate([meta_tokens, seqs[c]], axis=0)
        else:
            xs = np.zeros((T + NMETA, D), np.float32)
        m = dict(shared)
        m["xs"] = np.ascontiguousarray(xs)
        in_maps.append(m)
    nc = build(NT)
    res = run_bass_kernel_spmd(nc, in_maps, core_ids=list(range(NCORES)))
    outs = [np.asarray(res.results[c]["y"], dtype=np.float32) for c in range(len(seqs))]
    nb = x_prompt.shape[0]
    y_prompt = np.stack(outs[:nb])
    y_sample = np.stack(outs[nb:])
    return (y_prompt, y_sample)
```

```python
import numpy as np
import concourse.bass as bass
import concourse.mybir as mybir
from concourse.bass_utils import run_bass_kernel_spmd
from concourse.ap import AP

F32 = mybir.dt.float32
BF16 = mybir.dt.bfloat16
U8 = mybir.dt.uint8
ALU = mybir.AluOpType
AF = mybir.ActivationFunctionType

D = 1024
NMETA = 16
TN = 512
EPS = 1e-6
RG_C = 8.0
NCORES = 8
NDSEM = 24

PP_HEAD = 16
PP_GMIX = 128
PP_GMLP = 136
PP_N = 144
DV_SP = 0
DV_SP2 = 16
DV_LB = 32
DV_OML = 48
DV_HGG = 64
DV_EPS128 = 72
DV_EPSD = 73
DV_TINY = 74
DV_N = 80


def rev_ap(a):
    aps = [list(x) for x in a.ap]
    step, cnt = aps[-1]
    aps[-1] = [-step, cnt]
    return AP(a.tensor, a.offset + step * (cnt - 1), aps)


class Prog:
    def __init__(self):
        self.ops = []
        self.last_w = {}
        self.readers = {}
        self.last_barrier = 0

    def add(self, eng, fn, reads=(), writes=(), dma=False):
        i = len(self.ops)
        deps = set()
        for r in reads:
            if r in self.last_w:
                deps.add(self.last_w[r])
        for w in writes:
            if w in self.last_w:
                deps.add(self.last_w[w])
            deps.update(self.readers.get(w, ()))
        for r in reads:
            self.readers.setdefault(r, []).append(i)
        for w in writes:
            self.last_w[w] = i
            self.readers[w] = []
        self.ops.append(dict(eng=eng, fn=fn, deps=deps, dma=dma, sig=False))
        return i

    def barrier(self):
        n = len(self.ops)
        deps = set()
        last = {}
        for i in range(self.last_barrier, n):
            op = self.ops[i]
            if op['dma']:
                deps.add(i)
            elif op['fn'] is not None:
                last[op['eng']] = i
        deps.update(last.values())
        for e in ('pe', 'act', 'dve', 'pool', 'sp'):
            self.ops.append(dict(eng=e, fn=None, deps=set(deps), dma=False, sig=False))
        self.last_barrier = len(self.ops)
        self.last_w = {}
        self.readers = {}

    def finalize(self, nc, sems, dsems):
        ops = self.ops
        for q, ring in dsems.items():
            dma_idx = [i for i, o in enumerate(ops) if o['dma'] and o['eng'] == q]
            nr = len(ring)
            for j, i in enumerate(dma_idx):
                ops[i]['sem'] = ring[j % nr]
                ops[i]['val'] = 16 * (j // nr + 1)
                ops[i]['sig'] = True
                if j >= nr:
                    ops[i]['deps'].add(dma_idx[j - nr])
        for i, o in enumerate(ops):
            nd = set()
            for d in o['deps']:
                od = ops[d]
                if od['fn'] is None:
                    continue
                if (not od['dma']) and od['eng'] == o['eng'] and o['eng'] == 'pe' and not o['dma']:
                    continue
                nd.add(d)
                od['sig'] = True
            o['deps'] = nd
        cnt = {}
        for o in ops:
            if o['dma'] or o['fn'] is None:
                continue
            if o['sig']:
                cnt[o['eng']] = cnt.get(o['eng'], 0) + 1
                o['sem'] = sems[o['eng']]
                o['val'] = cnt[o['eng']]
        self.n_sig = cnt

    def emit(self, eng_name, e):
        known = {}
        for o in self.ops:
            if o['eng'] != eng_name:
                continue
            waits = {}
            for d in o['deps']:
                od = self.ops[d]
                s, v = od['sem'], od['val']
                k = id(s)
                if known.get(k, 0) >= v:
                    continue
                if k not in waits or waits[k][1] < v:
                    waits[k] = (s, v)
            for k, (s, v) in waits.items():
                e.wait_ge(s, v)
                known[k] = v
            if o['fn'] is None:
                continue
            ins = o['fn'](e)
            if o['sig']:
                ins.then_inc(o['sem'], 16 if o['dma'] else 1)


class Arena:
    def __init__(self, base_ap, nbytes):
        self.base = base_ap
        self.nbytes = nbytes
        self.off = 0
        self.mark_ = 0

    def alloc(self, free_elems, dtype, shape3=None):
        sz = free_elems * (4 if dtype == F32 else 2)
        sz_al = (sz + 31) // 32 * 32
        assert self.off + sz_al <= self.nbytes, f"SBUF arena overflow {self.off + sz_al} > {self.nbytes}"
        a = self.base[:, self.off:self.off + sz].bitcast(dtype)
        self.off += sz_al
        if shape3 is not None:
            a = a.rearrange("p (a b) -> p a b", b=shape3)
        return a

    def mark(self):
        return self.off

    def reset(self, m):
        self.off = m


class Ring:
    def __init__(self, bufs, name, keys=None):
        self.bufs = bufs
        self.name = name
        self.keys = keys
        self.i = 0

    def next(self):
        k = self.i % len(self.bufs)
        self.i += 1
        return self.bufs[k], (self.keys[k] if self.keys else (self.name, k))


def build(NT, stop_after=3):
    T = NT * TN
    L = T + NMETA
    nc = bass.Bass("TRN2", target_bir_lowering=False)
    P = Prog()

    xs = nc.dram_tensor("xs", [L, D], F32, kind="ExternalInput").ap()
    w_mix = nc.dram_tensor("w_mix", [5, 128, 8 * 1024], F32, kind="ExternalInput").ap()
    w_gate = nc.dram_tensor("w_gate", [128, 4 * 8 * 128], F32, kind="ExternalInput").ap()
    w_f2 = nc.dram_tensor("w_f2", [80, 128, 1024], F32, kind="ExternalInput").ap()
    w_out = nc.dram_tensor("w_out", [128, 8 * 1024], F32, kind="ExternalInput").ap()
    w_2 = nc.dram_tensor("w_2", [32, 128, 1024], F32, kind="ExternalInput").ap()
    pp_d = nc.dram_tensor("pp", [128, PP_N], F32, kind="ExternalInput").ap()
    gfin_d = nc.dram_tensor("gfin", [128, D], F32, kind="ExternalInput").ap()
    consts_d = nc.dram_tensor("consts", [128, 128 * 4 + 512 * 2], F32, kind="ExternalInput").ap()
    y_d = nc.dram_tensor("y", [T, D], F32, kind="ExternalOutput").ap()

    hb_scr = nc.dram_tensor("hb_scr", [8, 128, T], F32, kind="Internal").ap()
    ob_scr = nc.dram_tensor("ob_scr", [8, 128, T], F32, kind="Internal").ap()
    hs_scr = nc.dram_tensor("hs_scr", [8, 128, T], F32, kind="Internal").ap()
    on_scr = nc.dram_tensor("on_scr", [8, 128, T], F32, kind="Internal").ap()
    xc_scr = nc.dram_tensor("xc_scr", [8, 128, T], F32, kind="Internal").ap()
    qs_scr = nc.dram_tensor("qs_scr", [8, 128, T], F32, kind="Internal").ap()
    xcb_scr = nc.dram_tensor("xcb_scr", [8, 128, T], BF16, kind="Internal").ap()
    vb_scr = nc.dram_tensor("vb_scr", [8, NT, 128, TN], BF16, kind="Internal").ap()
    hT_scr = nc.dram_tensor("hT_scr", [NT, 128, 8, TN], BF16, kind="Internal").ap()
    wf2_bf = nc.dram_tensor("wf2_bf", [80, 128, 1024], BF16, kind="Internal").ap()
    w2_bf = nc.dram_tensor("w2_bf", [32, 128, 1024], BF16, kind="Internal").ap()

    ARENA_BYTES = 206 * 1024
    arena_t = nc.alloc_sbuf_tensor("arena", [128, ARENA_BYTES], U8).ap()
    A = Arena(arena_t, ARENA_BYTES)
    banks = [nc.alloc_psum_tensor(f"bank{i}", [128, 512], F32).ap() for i in range(8)]

    pp = A.alloc(PP_N, F32)
    dv = A.alloc(DV_N, F32)
    ident = A.alloc(128, BF16)
    ones_f = A.alloc(128, F32)
    mscF = A.alloc(128, F32)
    mscR = A.alloc(128, F32)
    maskF = A.alloc(512, F32)
    maskR = A.alloc(512, F32)
    junk = A.alloc(1024, BF16)
    ss = A.alloc(8, F32)
    rs = A.alloc(8, F32)
    base_mark = A.mark()
    ctmp = A.alloc(128 * 4 + 1024, F32)

    def col(t, c):
        return t[:, c:c + 1]

    def dma(q, out, in_, reads, writes):
        return P.add(q, lambda e: e.dma_start(out=out, in_=in_), reads, writes, dma=True)

    dma('sp', pp, pp_d, [], ['pp'])
    dma('sp', ctmp, consts_d, [], ['ctmp'])
    P.add('dve', lambda e: e.tensor_copy(out=ident, in_=ctmp[:, 0:128]), ['ctmp'], ['ident'])
    P.add('dve', lambda e: e.tensor_copy(out=ones_f, in_=ctmp[:, 128:256]), ['ctmp'], ['ones'])
    P.add('dve', lambda e: e.tensor_copy(out=mscF, in_=ctmp[:, 256:384]), ['ctmp'], ['mscF'])
    P.add('dve', lambda e: e.tensor_copy(out=mscR, in_=ctmp[:, 384:512]), ['ctmp'], ['mscR'])
    P.add('dve', lambda e: e.tensor_copy(out=maskF, in_=ctmp[:, 512:1024]), ['ctmp'], ['maskF'])
    P.add('dve', lambda e: e.tensor_copy(out=maskR, in_=ctmp[:, 1024:1536]), ['ctmp'], ['maskR'])
    for c in range(0, 80, 8):
        dma('pool', wf2_bf[c:c + 8], w_f2[c:c + 8], [], [('wf2', c)])
    for c in range(0, 32, 8):
        dma('pool', w2_bf[c:c + 8], w_2[c:c + 8], [], [('w2s', c)])

    dtmp = A.alloc(64, F32)
    for h in range(8):
        b0 = h * PP_HEAD
        for dr in range(2):
            k = dr * 8 + h
            lam = col(pp, b0 + 9 + dr)
            P.add('act', lambda e, lam=lam, k=k: e.activation(out=col(dtmp, k), in_=lam, func=AF.Exp, scale=-1.0),
                  ['pp'], [('dtmp', k)])
            P.add('act', lambda e, k=k: e.activation(out=col(dtmp, k), in_=col(dtmp, k), func=AF.Ln, bias=1.0),
                  [('dtmp', k)], [('dtmp', k)])
            P.add('dve', lambda e, k=k: e.tensor_scalar(out=col(dv, DV_SP + k), in0=col(dtmp, k), scalar1=-RG_C,
                                                        scalar2=None, op0=ALU.mult), [('dtmp', k)], ['dv'])
            P.add('dve', lambda e, k=k: e.tensor_scalar(out=col(dv, DV_SP2 + k), in0=col(dtmp, k),
                                                        scalar1=-2.0 * RG_C, scalar2=None, op0=ALU.mult),
                  [('dtmp', k)], ['dv'])
            l0 = col(pp, b0 + 11 + dr)
            l1 = col(pp, b0 + 13 + dr)
            P.add('dve', lambda e, l0=l0, l1=l1, k=k: e.tensor_tensor(out=col(dtmp, 16 + k), in0=l0, in1=l1,
                                                                     op=ALU.subtract), ['pp'], [('dtmp', 16 + k)])
        P.add('dve', lambda e, h=h, b0=b0: e.tensor_scalar(out=col(dv, DV_HGG + h), in0=col(pp, b0 + 15),
                                                           scalar1=float(np.sqrt(128.0)), scalar2=None,
                                                           op0=ALU.mult), ['pp'], ['dv'])
    P.add('act', lambda e: e.activation(out=dv[:, DV_LB:DV_LB + 16], in_=dtmp[:, 16:32], func=AF.Sigmoid),
          [('dtmp', 16 + k) for k in range(16)], ['dv'])
    P.add('dve', lambda e: e.tensor_scalar(out=dv[:, DV_OML:DV_OML + 16], in0=dv[:, DV_LB:DV_LB + 16],
                                           scalar1=-1.0, scalar2=1.0, op0=ALU.mult, op1=ALU.add), ['dv'], ['dv'])
    P.add('pool', lambda e: e.memset(col(dv, DV_EPS128), float(128.0 * EPS)), [], ['dv'])
    P.add('pool', lambda e: e.memset(col(dv, DV_EPSD), float(D * EPS)), [], ['dv'])
    P.add('pool', lambda e: e.memset(col(dv, DV_TINY), 1e-30), [], ['dv'])
    P.barrier()
    A.reset(base_mark)

    def stage_x(p0, N, xt_list, hT, hT_key, g_col0, halo, psbank, psbank_key, hcol0=0):
        nsub = (N + 127) // 128
        psT = psbank.bitcast(BF16).rearrange("p (a b) -> p a b", b=128)
        gap = pp[:, g_col0:g_col0 + 8]
        for s in range(nsub):
            npk = min(128, N - 128 * s)
            xt, xk = xt_list[s]
            dma('sp', xt[:npk, :], xs[p0 + 128 * s:p0 + 128 * s + npk, :], [], [xk])
            P.add('pool', lambda e, s=s: e.memset(col(ss, s), 0.0), [], [('ss', s)])
            P.add('act', lambda e, xt=xt, npk=npk, s=s: e.activation(
                out=junk[:npk, :], in_=xt[:npk, :], func=AF.Square, accum_out=ss[:npk, s:s + 1]),
                [xk, ('ss', s)], ['junk', ('ss', s)])
            P.add('act', lambda e, npk=npk, s=s: e.activation(out=rs[:npk, s:s + 1], in_=ss[:npk, s:s + 1], func=AF.Ln, bias=dv[:npk, DV_EPSD:DV_EPSD + 1]), [('ss', s), 'dv'], [('rs', s)])
            P.add('act', lambda e, npk=npk, s=s: e.activation(out=rs[:npk, s:s + 1], in_=rs[:npk, s:s + 1], func=AF.Exp, scale=-0.5), [('rs', s)], [('rs', s)])
            xn, xnk = xn_ring.next()
            P.add('dve', lambda e, xt=xt, xn=xn, npk=npk, s=s: e.tensor_scalar(
                out=xn[:npk, :], in0=xt[:npk, :], scalar1=rs[:npk, s:s + 1], scalar2=32.0,
                op0=ALU.mult, op1=ALU.mult), [xk, ('rs', s)], [xnk])

            def tr(e, xn=xn, npk=npk):
                ins = None
                for kc in range(8):
                    ins = e.transpose(out=psT[:, kc, 0:npk], in_=xn[:npk, kc * 128:(kc + 1) * 128],
                                      identity=ident[:npk, :npk])
                return ins
            P.add('pe', tr, [xnk, 'ident'], [psbank_key])
            c0 = hcol0 + 128 * s
            P.add('dve', lambda e, npk=npk, c0=c0: e.tensor_tensor(
                out=hT[:, :, c0:c0 + npk], in0=psT[:, :, 0:npk],
                in1=gap.unsqueeze(2).broadcast_to([128, 8, npk]), op=ALU.mult),
                [psbank_key, 'pp'], [hT_key])
        if halo:
            xh, xhk = xt_ring.next()
            xnh, xnhk = xn_ring.next()
            P.add('pool', lambda e: e.memset(xh[0:3, :], 0.0), [], [xhk])
            if p0 >= 2:
                dma('sp', xh[0:2, :], xs[p0 - 2:p0, :], [], [xhk])
            if p0 + N < L:
                dma('sp', xh[2:3, :], xs[p0 + N:p0 + N + 1, :], [], [xhk])
            P.add('pool', lambda e: e.memset(ss[0:3, 7:8], 0.0), [], [('ss', 7)])
            P.add('act', lambda e: e.activation(out=junk[0:3, :], in_=xh[0:3, :], func=AF.Square,
                                                accum_out=ss[0:3, 7:8]), [xhk, ('ss', 7)], ['junk', ('ss', 7)])
            P.add('act', lambda e: e.activation(out=rs[0:3, 7:8], in_=ss[0:3, 7:8], func=AF.Ln,
                                                bias=dv[0:3, DV_EPSD:DV_EPSD + 1]), [('ss', 7), 'dv'], [('rs', 7)])
            P.add('act', lambda e: e.activation(out=rs[0:3, 7:8], in_=rs[0:3, 7:8], func=AF.Exp, scale=-0.5),
                  [('rs', 7)], [('rs', 7)])
            P.add('dve', lambda e: e.tensor_scalar(out=xnh[0:3, :], in0=xh[0:3, :], scalar1=rs[0:3, 7:8],
                                                   scalar2=32.0, op0=ALU.mult, op1=ALU.mult),
                  [xhk, ('rs', 7)], [xnhk])

            def trh(e):
                ins = None
                for kc in range(8):
                    ins = e.transpose(out=psT[:, kc, 0:3], in_=xnh[0:3, kc * 128:(kc + 1) * 128],
                                      identity=ident[0:3, 0:3])
                return ins
            P.add('pe', trh, [xnhk, 'ident'], [psbank_key])
            P.add('dve', lambda e: e.tensor_tensor(
                out=hT[:, :, N:N + 3], in0=psT[:, :, 0:3],
                in1=gap.unsqueeze(2).broadcast_to([128, 8, 3]), op=ALU.mult),
                [psbank_key, 'pp'], [hT_key])

    wmix = [A.alloc(8 * 1024, BF16, shape3=1024) for _ in range(4)]
    wg = A.alloc(2 * 8 * 128, BF16).rearrange("p (g h j) -> p g h j", g=2, h=8)
    GXA, GQ, GF, GV = range(4)
    for g, src in ((GXA, 0), (GQ, 1), (GV, 4)):
        dma('pool', wmix[g].rearrange("p a b -> p (a b)"), w_mix[src], [], [('wmix', g)])

    xt_ring = Ring([A.alloc(1024, F32) for _ in range(2)], 'xt')
    xn_ring = Ring([A.alloc(1024, BF16) for _ in range(1)], 'xn')
    hT_bufs = [A.alloc(8 * (TN + 3), BF16, shape3=TN + 3) for _ in range(1)]
    carryA = A.alloc(8, F32)
    S_f = A.alloc(8 * 128, F32, shape3=128)
    S_b = A.alloc(8 * 128, BF16, shape3=128)

    def tmp(n=TN, dt=F32, cnt=2, name=None):
        return Ring([A.alloc(n, dt) for _ in range(cnt)], name)
    t_ext = tmp(TN + 3, F32, 2, 'ext')
    t_xc = tmp(name='xc')
    t_xcb = tmp(dt=BF16, name='xcb')
    t_r = tmp(name='r')
    t_i = tmp(name='i')
    t_a = tmp(name='a')
    t_a2 = tmp(cnt=2, name='a2')
    t_u = tmp(cnt=2, name='u')
    t_h = tmp(name='h')
    t_hb = tmp(cnt=2, name='hbl')
    t_sg = tmp(cnt=1, name='sg')
    t_f = tmp(name='f')
    t_qs = tmp(cnt=2, name='qs')
    t_g = tmp(cnt=1, name='g')
    t_b = tmp(cnt=2, name='b')
    t_eb = tmp(cnt=4, name='eb')
    t_enb = tmp(cnt=2, name='enb')
    t_qt = tmp(dt=BF16, cnt=4, name='qt')
    t_kt = tmp(dt=BF16, cnt=4, name='kt')
    t_kh = tmp(dt=BF16, cnt=4, name='kh')
    t_vb = tmp(dt=BF16, cnt=4, name='vb')
    khT4 = [[A.alloc(128, BF16) for _ in range(4)] for _ in range(2)]
    PT4 = [[A.alloc(128, BF16) for _ in range(4)] for _ in range(2)]
    t_ob = [tmp(cnt=1, name='obl0'), tmp(cnt=1, name='obl1')]
    t_os = [tmp(cnt=1, name='os0'), tmp(cnt=1, name='os1')]
    t_osq = [tmp(cnt=1, name='osq0'), tmp(cnt=1, name='osq1')]
    t_rso = [tmp(cnt=1, name='rso0'), tmp(cnt=1, name='rso1')]
    mix_mark_end = A.mark()

    gen_ring = Ring(banks[0:3], 'psg')
    halo_ring = Ring([banks[3][:, 0:4], banks[3][:, 4:8]], 'pshalo', keys=['bank3', 'bank3'])
    kT_slots = [banks[3][:, 64:128].bitcast(BF16), banks[3][:, 128:192].bitcast(BF16)]
    sc_slots = [banks[3][:, 256:384], banks[3][:, 384:512]]
    ch_o = [(banks[4], 'bank4'), (banks[6], 'bank6')]
    ch_m = [(banks[5], 'bank5'), (banks[7], 'bank7')]

    def stage1_steps(dirn, p0, N, hT, hT_key, meta, h, st):
        rv = (dirn == 1)
        R = (lambda a: rev_ap(a)) if rv else (lambda a: a)
        nsub = (N + 127) // 128
        CL = min(64, N)
        pr0 = p0 - NMETA
        combine = (dirn == 0) and not meta
        hc = slice(h * 128, (h + 1) * 128)
        b0 = h * PP_HEAD
        npv = min(128, N)
        msk = maskR if rv else maskF
        mkey = 'maskR' if rv else 'maskF'
        nch = N // CL
        lc = 0 if rv else CL - 1
        lastc = 0 if rv else N - 1
        V = {}
        steps = []

        def step(f):
            steps.append(f)
            return f

        cload = combine
        cstore = (dirn == 1)

        @step
        def s_xa():
            if combine:
                V['hbl'], V['khbl'] = t_hb.next()
                dma('sp', V['hbl'][:, 0:N], hb_scr[h, :, pr0:pr0 + N], [], [V['khbl']])
            if cload:
                xcb, kxcb = t_xcb.next()
                xc, kxc = t_xc.next()
                qs, kqs = t_qs.next()
                vb, kvb = t_vb.next()
                V.update(xcb=xcb, kxcb=kxcb, xc=xc, kxc=kxc, qs=qs, kqs=kqs, vb=vb, kvb=kvb)
                dma('sp', xcb[:, 0:N], xcb_scr[h, :, pr0:pr0 + N], [], [kxcb])
                dma('sp', xc[:, 0:N], xc_scr[h, :, pr0:pr0 + N], [], [kxc])
                dma('sp', qs[:, 0:N], qs_scr[h, :, pr0:pr0 + N], [], [kqs])
                dma('sp', vb[:, 0:N], vb_scr[h, pr0 // TN], [], [kvb])
                return
            ps_xa, kxa = gen_ring.next()
            ps_hl, khl = halo_ring.next()
            V.update(ps_xa=ps_xa, kxa=kxa, ps_hl=ps_hl, khl=khl)

            def mm_xa(e):
                ins = None
                for kc in range(8):
                    e.matmul(ps_xa[:, 0:N], lhsT=wmix[GXA][:, kc, hc], rhs=hT[:, kc, 0:N],
                             start=(kc == 0), stop=(kc == 7))
                for kc in range(8):
                    ins = e.matmul(ps_hl[:, 0:3], lhsT=wmix[GXA][:, kc, hc], rhs=hT[:, kc, N:N + 3],
                                   start=(kc == 0), stop=(kc == 7))
                return ins
            P.add('pe', mm_xa, [hT_key, ('wmix', GXA)], [kxa, khl])

        @step
        def s_ext():
            if cload:
                return
            ext, kext = t_ext.next()
            V.update(ext=ext, kext=kext)
            ps_xa, ps_hl = V['ps_xa'], V['ps_hl']
            P.add('act', lambda e: e.activation(out=ext[:, 2:2 + N], in_=ps_xa[:, 0:N], func=AF.Copy),
                  [V['kxa']], [kext])
            P.add('dve', lambda e: e.tensor_copy(out=ext[:, 0:2], in_=ps_hl[:, 0:2]), [V['khl']], [kext])
            P.add('dve', lambda e: e.tensor_copy(out=ext[:, N + 2:N + 3], in_=ps_hl[:, 2:3]), [V['khl']], [kext])
            xc, kxc = t_xc.next()
            V.update(xc=xc, kxc=kxc)
            P.add('act', lambda e: e.activation(out=xc[:, 0:N], in_=ps_xa[:, 0:N], func=AF.Identity,
                                                bias=col(pp, b0 + 4), scale=col(pp, b0 + 2)),
                  [V['kxa'], 'pp'], [kxc])

        @step
        def s_q():
            if cload:
                return
            ps_q, kq_ = gen_ring.next()
            V.update(ps_q=ps_q, kq_=kq_)

            def mm_q(e):
                ins = None
                for kc in range(8):
                    ins = e.matmul(ps_q[:, 0:N], lhsT=wmix[GQ][:, kc, hc], rhs=hT[:, kc, 0:N],
                                   start=(kc == 0), stop=(kc == 7))
                return ins
            P.add('pe', mm_q, [hT_key, ('wmix', GQ)], [kq_])

        @step
        def s_sg():
            if cload:
                return
            sg, ksg = t_sg.next()
            qs, kqs = t_qs.next()
            V.update(qs=qs, kqs=kqs)
            ps_q, ext, xc, kxc = V['ps_q'], V['ext'], V['xc'], V['kxc']
            P.add('act', lambda e: e.activation(out=qs[:, 0:N], in_=ps_q[:, 0:N], func=AF.Silu),
                  [V['kq_']], [kqs])
            P.add('dve', lambda e: e.scalar_tensor_tensor(
                out=xc[:, 0:N], in0=ext[:, 0:N], scalar=col(pp, b0 + 0), in1=xc[:, 0:N],
                op0=ALU.mult, op1=ALU.add), [V['kext'], kxc, 'pp'], [kxc])

        @step
        def s_f():
            ps_f, kf_ = gen_ring.next()
            V.update(ps_f=ps_f, kf_=kf_)

            def mm_f(e):
                ins = None
                for kc in range(8):
                    ins = e.matmul(ps_f[:, 0:N], lhsT=wmix[GF][:, kc, hc], rhs=hT[:, kc, 0:N],
                                   start=(kc == 0), stop=(kc == 7))
                return ins
            P.add('pe', mm_f, [hT_key, ('wmix', GF)], [kf_])

        @step
        def s_sf():
            f_, kff = t_f.next()
            V.update(f_=f_, kff=kff)
            ps_f = V['ps_f']
            P.add('act', lambda e: e.activation(out=f_[:, 0:N], in_=ps_f[:, 0:N], func=AF.Sigmoid),
                  [V['kf_']], [kff])
            if cload:
                return
            ext, xc, kxc = V['ext'], V['xc'], V['kxc']
            P.add('dve', lambda e: e.scalar_tensor_tensor(
                out=xc[:, 0:N], in0=ext[:, 1:1 + N], scalar=col(pp, b0 + 1), in1=xc[:, 0:N],
                op0=ALU.mult, op1=ALU.add), [V['kext'], kxc, 'pp'], [kxc])

        @step
        def s_v():
            if cload:
                return
            ps_v, kv_ = gen_ring.next()
            V.update(ps_v=ps_v, kv_=kv_)

            def mm_v(e):
                ins = None
                for s in range(nsub):
                    npk = min(128, N - 128 * s)
                    for kc in range(8):
                        ins = e.matmul(ps_v[:npk, s * 128:(s + 1) * 128], lhsT=hT[:, kc, 128 * s:128 * s + npk],
                                       rhs=wmix[GV][:, kc, hc], start=(kc == 0), stop=(kc == 7))
                return ins
            P.add('pe', mm_v, [hT_key, ('wmix', GV)], [kv_])

        @step
        def s_vb():
            f_, kff = V['f_'], V['kff']
            P.add('dve', lambda e: e.tensor_scalar(
                out=f_[:, 0:N], in0=f_[:, 0:N], scalar1=col(dv, DV_OML + dirn * 8 + h),
                scalar2=col(dv, DV_LB + dirn * 8 + h), op0=ALU.mult, op1=ALU.add), [kff, 'dv'], [kff])
            if cload:
                return
            vb, kvb = t_vb.next()
            V.update(vb=vb, kvb=kvb)
            ps_v, ext, xc, kxc = V['ps_v'], V['ext'], V['xc'], V['kxc']
            P.add('act', lambda e: e.activation(out=vb[:npv, 0:nsub * 128], in_=ps_v[:npv, 0:nsub * 128],
                                                func=AF.Copy), [V['kv_']], [kvb])

        @step
        def s_conv3():
            if cload:
                return
            ext, xc, kxc = V['ext'], V['xc'], V['kxc']
            xcb, kxcb = t_xcb.next()
            V.update(xcb=xcb, kxcb=kxcb)
            P.add('dve', lambda e: e.scalar_tensor_tensor(
                out=xc[:, 0:N], in0=ext[:, 3:3 + N], scalar=col(pp, b0 + 3), in1=xc[:, 0:N],
                op0=ALU.mult, op1=ALU.add), [V['kext'], kxc, 'pp'], [kxc])
            P.add('dve', lambda e: e.tensor_copy(out=xcb[:, 0:N], in_=xc[:, 0:N]), [kxc], [kxcb])
            if cstore:
                dma('sp', xcb_scr[h, :, pr0:pr0 + N], xcb[:, 0:N], [kxcb], [])
                dma('sp', xc_scr[h, :, pr0:pr0 + N], xc[:, 0:N], [kxc], [])
                dma('sp', qs_scr[h, :, pr0:pr0 + N], V['qs'][:, 0:N], [V['kqs']], [])
                dma('sp', vb_scr[h, pr0 // TN], V['vb'][:, 0:N], [V['kvb']], [])

        @step
        def s_gr():
            ps_r, kr_ = gen_ring.next()
            V.update(ps_r=ps_r, kr_=kr_)
            xcb = V['xcb']
            P.add('pe', lambda e: e.matmul(ps_r[:, 0:N], lhsT=wg[:, 0, h, :], rhs=xcb[:, 0:N],
                                           start=True, stop=True), [V['kxcb'], 'wg'], [kr_])

        @step
        def s_r():
            r_, krr = t_r.next()
            V.update(r_=r_, krr=krr)
            ps_r = V['ps_r']
            P.add('act', lambda e: e.activation(out=r_[:, 0:N], in_=ps_r[:, 0:N], func=AF.Sigmoid,
                                                bias=col(pp, b0 + 5 + dirn)), [V['kr_'], 'pp'], [krr])

        @step
        def s_gi():
            ps_i, ki_ = gen_ring.next()
            V.update(ps_i=ps_i, ki_=ki_)
            xcb = V['xcb']
            P.add('pe', lambda e: e.matmul(ps_i[:, 0:N], lhsT=wg[:, 1, h, :], rhs=xcb[:, 0:N],
                                           start=True, stop=True), [V['kxcb'], 'wg'], [ki_])

        @step
        def s_i():
            i_, kii = t_i.next()
            V.update(i_=i_, kii=kii)
            ps_i = V['ps_i']
            P.add('act', lambda e: e.activation(out=i_[:, 0:N], in_=ps_i[:, 0:N], func=AF.Sigmoid,
                                                bias=col(pp, b0 + 7 + dirn)), [V['ki_'], 'pp'], [kii])

        @step
        def s_g():
            g_, kgg = t_g.next()
            b_, kbb = t_b.next()
            V.update(b_=b_, kbb=kbb)
            f_ = V['f_']
            P.add('act', lambda e: e.activation(out=g_[:, 0:N], in_=f_[:, 0:N], func=AF.Ln), [V['kff']], [kgg])
            P.add('dve', lambda e: e.tensor_tensor_scan(
                out=R(b_[:, 0:N]), data0=R(msk[:, 0:N]), data1=R(g_[:, 0:N]), initial=0.0,
                op0=ALU.mult, op1=ALU.add), [kgg, mkey], [kbb])

        @step
        def s_a2():
            a2, ka2 = t_a2.next()
            V.update(a2=a2, ka2=ka2)
            r_, i_, xc = V['r_'], V['i_'], V['xc']
            P.add('act', lambda e: e.activation(out=a2[:, 0:N], in_=r_[:, 0:N], func=AF.Exp,
                                                scale=col(dv, DV_SP2 + dirn * 8 + h)), [V['krr'], 'dv'], [ka2])
            P.add('dve', lambda e: e.tensor_tensor(out=i_[:, 0:N], in0=i_[:, 0:N], in1=xc[:, 0:N], op=ALU.mult),
                  [V['kii'], V['kxc']], [V['kii']])

        @step
        def s_abs():
            a2, ka2 = V['a2'], V['ka2']
            P.add('act', lambda e: e.activation(out=a2[:, 0:N], in_=a2[:, 0:N], func=AF.Abs, scale=-1.0, bias=1.0),
                  [ka2], [ka2])

        @step
        def s_eb():
            eb, keb = t_eb.next()
            V.update(eb=eb, keb=keb)
            b_ = V['b_']
            P.add('act', lambda e: e.activation(out=eb[:, 0:N], in_=b_[:, 0:N], func=AF.Exp), [V['kbb']], [keb])

        @step
        def s_ln():
            a2, ka2 = V['a2'], V['ka2']
            P.add('act', lambda e: e.activation(out=a2[:, 0:N], in_=a2[:, 0:N], func=AF.Ln, bias=col(dv, DV_TINY)),
                  [ka2, 'dv'], [ka2])

        @step
        def s_enb():
            enb, kenb = t_enb.next()
            qt, kqt = t_qt.next()
            V.update(enb=enb, kenb=kenb, qt=qt, kqt=kqt)
            b_, qs, eb = V['b_'], V['qs'], V['eb']
            P.add('act', lambda e: e.activation(out=enb[:, 0:N], in_=b_[:, 0:N], func=AF.Exp, scale=-1.0),
                  [V['kbb']], [kenb])
            P.add('pool', lambda e: e.tensor_tensor(out=qt[:, 0:N], in0=qs[:, 0:N], in1=eb[:, 0:N], op=ALU.mult),
                  [V['kqs'], V['keb']], [kqt])

        @step
        def s_sqrt():
            a2, ka2 = V['a2'], V['ka2']
            P.add('act', lambda e: e.activation(out=a2[:, 0:N], in_=a2[:, 0:N], func=AF.Exp, scale=0.5),
                  [ka2], [ka2])

        @step
        def s_a():
            a_, kaa = t_a.next()
            kt, kkt = t_kt.next()
            V.update(a_=a_, kaa=kaa, kt=kt, kkt=kkt)
            r_, f_, enb = V['r_'], V['f_'], V['enb']
            P.add('act', lambda e: e.activation(out=a_[:, 0:N], in_=r_[:, 0:N], func=AF.Exp,
                                                scale=col(dv, DV_SP + dirn * 8 + h)), [V['krr'], 'dv'], [kaa])
            P.add('dve', lambda e: e.scalar_tensor_tensor(
                out=kt[:, 0:N], in0=f_[:, 0:N], scalar=1.0, in1=enb[:, 0:N], op0=ALU.subtract, op1=ALU.mult),
                [V['kff'], V['kenb']], [kkt])

        @step
        def s_u():
            u_, kuu = t_u.next()
            V.update(u_=u_, kuu=kuu)
            a2, i_ = V['a2'], V['i_']
            P.add('dve', lambda e: e.tensor_tensor(out=u_[:, 0:N], in0=a2[:, 0:N], in1=i_[:, 0:N], op=ALU.mult),
                  [V['ka2'], V['kii']], [kuu])

        @step
        def s_kh():
            kh, kkh = t_kh.next()
            kt, eb = V['kt'], V['eb']
            eb3 = eb[:, 0:N].rearrange("p (c t) -> p c t", t=CL)
            P.add('dve', lambda e: e.tensor_tensor(
                out=kh[:, 0:N].rearrange("p (c t) -> p c t", t=CL),
                in0=kt[:, 0:N].rearrange("p (c t) -> p c t", t=CL),
                in1=eb3[:, :, lc:lc + 1].broadcast_to([128, nch, CL]), op=ALU.mult), [V['kkt'], V['keb']], [kkh])
            st.update(qt=V['qt'], kqt=V['kqt'], kt=kt, kkt=V['kkt'], kh=kh, kkh=kkh, vb=V['vb'], kvb=V['kvb'],
                      eb=eb, keb=V['keb'])

        @step
        def s_scan():
            hh, khh = t_h.next()
            V.update(hh=hh, khh=khh)
            a_, u_ = V['a_'], V['u_']
            P.add('dve', lambda e: e.tensor_tensor_scan(
                out=R(hh[:, 0:N]), data0=R(a_[:, 0:N]), data1=R(u_[:, 0:N]), initial=col(carryA, h),
                op0=ALU.mult, op1=ALU.add), [V['kaa'], V['kuu'], ('carryA', h)], [khh])
            P.add('pool', lambda e: e.tensor_copy(out=col(carryA, h), in_=hh[:, lastc:lastc + 1]),
                  [khh], [('carryA', h)])

        @step
        def s_out():
            hh, khh = V['hh'], V['khh']
            if dirn == 1:
                dma('sp', hb_scr[h, :, pr0:pr0 + N], hh[:, 0:N], [khh], [])
            elif combine:
                hbl, khbl = V['hbl'], V['khbl']
                P.add('pool', lambda e: e.tensor_tensor(out=hh[:, 0:N], in0=hh[:, 0:N], in1=hbl[:, 0:N], op=ALU.add),
                      [khh, khbl], [khh])
                dma('sp', hs_scr[h, :, pr0:pr0 + N], hh[:, 0:N], [khh], [])
        return steps

    def stage1_pair(dirn, p0, N, hT, hT_key, meta, h0, sts):
        sa = stage1_steps(dirn, p0, N, hT, hT_key, meta, h0, sts[0])
        sb = stage1_steps(dirn, p0, N, hT, hT_key, meta, h0 + 1, sts[1])
        for fa, fb in zip(sa, sb):
            fa()
            yield
            fb()
            yield

    def stage2(dirn, p0, N, meta, h, st, chain):
        rv = (dirn == 1)
        nsub = (N + 127) // 128
        CL = min(64, N)
        pr0 = p0 - NMETA
        combine = (dirn == 0) and not meta
        lc = 0 if rv else CL - 1
        qt, kqt, kt, kkt, kh, kkh = st['qt'], st['kqt'], st['kt'], st['kkt'], st['kh'], st['kkh']
        vb, kvb, eb, keb = st['vb'], st['kvb'], st['eb'], st['keb']
        ps_o, kpo = ch_o[chain]
        mbank, kmb = ch_m[chain]
        if combine:
            ob, kob = t_ob[chain].next()
            dma('sp', ob[:, 0:N], ob_scr[h, :, pr0:pr0 + N], [], [kob])
        msc = mscR if rv else mscF
        msck = 'mscR' if rv else 'mscF'
        sub_order = range(nsub - 1, -1, -1) if rv else range(nsub)
        for s in sub_order:
            npk = min(128, N - 128 * s)
            t0 = 128 * s
            ps_kT, kkT = mbank[:, 256:320].bitcast(BF16), kmb
            P.add('pe', lambda e, ps_kT=ps_kT, t0=t0, npk=npk: e.transpose(
                out=ps_kT[:npk, 0:128], in_=kh[:, t0:t0 + npk], identity=ident), [kkh, 'ident'], [kkT])
            khT, kkhT = khT4[chain][s], ('khT4', chain, s)
            P.add('act', lambda e, khT=khT, ps_kT=ps_kT, npk=npk: e.activation(
                out=khT[:npk, :], in_=ps_kT[:npk, 0:128], func=AF.Copy), [kkT], [kkhT])
            ps_sc, ksc = mbank[:, 0:128], kmb
            P.add('pe', lambda e, ps_sc=ps_sc, t0=t0, npk=npk: e.matmul(
                ps_sc[:npk, 0:npk], lhsT=kt[:, t0:t0 + npk], rhs=qt[:, t0:t0 + npk], start=True, stop=True),
                [kkt, kqt], [ksc])
            yield
            PT, kPT = PT4[chain][s], ('PT4', chain, s)
            P.add('dve', lambda e, PT=PT, ps_sc=ps_sc, npk=npk: e.tensor_tensor(
                out=PT[:npk, 0:npk], in0=ps_sc[:npk, 0:npk], in1=msc[:npk, 0:npk], op=ALU.mult),
                [ksc, msck], [kPT])
            yield
            ncs = npk // CL
            ch_order = range(ncs - 1, -1, -1) if rv else range(ncs)
            for c in ch_order:
                c0 = c * CL

                def mm_o(e, PT=PT, s=s, t0=t0, c0=c0, npk=npk):
                    e.matmul(ps_o[:, t0 + c0:t0 + c0 + CL], lhsT=vb[:npk, s * 128:(s + 1) * 128],
                             rhs=PT[:npk, c0:c0 + CL], start=True, stop=False)
                    return e.matmul(ps_o[:, t0 + c0:t0 + c0 + CL], lhsT=S_b[:, h, :],
                                    rhs=qt[:, t0 + c0:t0 + c0 + CL], start=False, stop=True)
                P.add('pe', mm_o, [kPT, kvb, kqt, ('Sb', h)], [kpo])
                ps_dS, kdS = mbank[:, 128:256], kmb
                P.add('pe', lambda e, ps_dS=ps_dS, khT=khT, s=s, c0=c0: e.matmul(
                    ps_dS[:, 0:128], lhsT=khT[c0:c0 + CL, :], rhs=vb[c0:c0 + CL, s * 128:(s + 1) * 128],
                    start=True, stop=True), [kkhT, kvb], [kdS])
                yield
                dcol = t0 + c0 + lc
                P.add('dve', lambda e, ps_dS=ps_dS, dcol=dcol: e.scalar_tensor_tensor(
                    out=S_b[:, h, :], in0=S_f[:, h, :], scalar=eb[:, dcol:dcol + 1], in1=ps_dS[:, 0:128],
                    op0=ALU.mult, op1=ALU.subtract), [kdS, keb, ('Sf', h)], [('Sb', h)])
                P.add('dve', lambda e, ps_dS=ps_dS, dcol=dcol: e.scalar_tensor_tensor(
                    out=S_f[:, h, :], in0=S_f[:, h, :], scalar=eb[:, dcol:dcol + 1], in1=ps_dS[:, 0:128],
                    op0=ALU.mult, op1=ALU.subtract), [kdS, keb, ('Sf', h)], [('Sf', h)])
                yield
        if dirn == 1:
            ob, kob = t_ob[chain].next()
            P.add('act', lambda e: e.activation(out=ob[:, 0:N], in_=ps_o[:, 0:N], func=AF.Copy), [kpo], [kob])
            dma('sp', ob_scr[h, :, pr0:pr0 + N], ob[:, 0:N], [kob], [])
            yield
        elif combine:
            osm, kos = t_os[chain].next()
            P.add('dve', lambda e: e.tensor_tensor(out=osm[:, 0:N], in0=ps_o[:, 0:N], in1=ob[:, 0:N], op=ALU.add),
                  [kpo, kob], [kos])
            yield
            osq, kosq = t_osq[chain].next()
            P.add('act', lambda e: e.activation(out=osq[:, 0:N], in_=osm[:, 0:N], func=AF.Square), [kos], [kosq])
            yield
            ps_ss, kpss = mbank, kmb
            P.add('pe', lambda e: e.matmul(ps_ss[:, 0:N], lhsT=ones_f, rhs=osq[:, 0:N], start=True, stop=True),
                  [kosq, 'ones'], [kpss])
            yield
            rso, krso = t_rso[chain].next()
            P.add('act', lambda e: e.activation(out=rso[:, 0:N], in_=ps_ss[:, 0:N], func=AF.Ln,
                                                bias=col(dv, DV_EPS128)), [kpss, 'dv'], [krso])
            yield
            P.add('act', lambda e: e.activation(out=rso[:, 0:N], in_=rso[:, 0:N], func=AF.Exp, scale=-0.5),
                  [krso], [krso])
            yield
            P.add('pool', lambda e: e.tensor_tensor(out=osm[:, 0:N], in0=osm[:, 0:N], in1=rso[:, 0:N], op=ALU.mult),
                  [kos, krso], [kos])
            dma('sp', on_scr[h, :, pr0:pr0 + N], osm[:, 0:N], [kos], [])
            yield

    def interleave(*gens):
        gens = [g for g in gens if g is not None]
        while gens:
            for g in list(gens):
                try:
                    next(g)
                except StopIteration:
                    gens.remove(g)

    def stage_x_gen(*a, **k):
        stage_x(*a, **k)
        yield

    def init_states():
        P.add('pool', lambda e: e.memset(carryA, 0.0), [], [('carryA', h) for h in range(8)])
        P.add('pool', lambda e: e.memset(S_f.rearrange("p a b -> p (a b)"), 0.0), [], [('Sf', h) for h in range(8)])
        P.add('pool', lambda e: e.memset(S_b.rearrange("p a b -> p (a b)"), 0.0), [], [('Sb', h) for h in range(8)])

    def run_mixer_pass(dirn, tiles):
        hT, hT_key = hT_bufs[0], ('hT', 0)
        dma('pool', wg.rearrange("p g h j -> p (g h j)"), w_gate[:, dirn * 2048:(dirn + 1) * 2048], [], ['wg'])
        dma('pool', wmix[GF].rearrange("p a b -> p (a b)"), w_mix[3 if dirn == 1 else 2], [], [('wmix', GF)])

        def do_x(p0, N):
            if dirn == 0 and p0 > 0:
                dma('sp', hT[:, :, 0:N], hT_scr[(p0 - NMETA) // TN], [], [hT_key])
                return
            nsub = (N + 127) // 128
            xl = [xt_ring.next() for _ in range(nsub)]
            pb, pbk = gen_ring.next()
            stage_x(p0, N, xl, hT, hT_key, PP_GMIX, True, pb, pbk)
            if dirn == 1:
                dma('sp', hT_scr[(p0 - NMETA) // TN], hT[:, :, 0:N], [hT_key], [])

        def s1_pair(p0, N, meta, h0, sts, with_x):
            if with_x:
                do_x(p0, N)
                yield
            yield from stage1_pair(dirn, p0, N, hT, hT_key, meta, h0, sts)
        p0, N, meta = tiles[0]
        sts = [{}, {}]
        interleave(s1_pair(p0, N, meta, 0, sts, True))
        for ti, (p0, N, meta) in enumerate(tiles):
            for pr in range(4):
                sts_next = [{}, {}]
                if pr < 3:
                    nxt = s1_pair(p0, N, meta, 2 * pr + 2, sts_next, False)
                elif ti + 1 < len(tiles):
                    nxt = s1_pair(*tiles[ti + 1], 0, sts_next, True)
                else:
                    nxt = None
                interleave(stage2(dirn, p0, N, meta, 2 * pr, sts[0], 0),
                           stage2(dirn, p0, N, meta, 2 * pr + 1, sts[1], 1), nxt)
                sts = sts_next

    init_states()
    if stop_after >= 1:
        run_mixer_pass(1, [(NMETA + ti * TN, TN, False) for ti in range(NT - 1, -1, -1)])
    P.barrier()
    init_states()
    if stop_after >= 2:
        run_mixer_pass(0, [(0, NMETA, True)] + [(NMETA + ti * TN, TN, False) for ti in range(NT)])
    P.barrier()

    A.reset(base_mark)
    wout_sb = A.alloc(8 * 1024, BF16, shape3=1024)
    gfin = A.alloc(1024, F32)
    dma('sp', gfin, gfin_d, [], ['gfin'])
    P.add('pool', lambda e: e.tensor_scalar(out=gfin, in0=gfin, scalar1=32.0, scalar2=None, op0=ALU.mult),
          ['gfin'], ['gfin'])
    dma('pool', wout_sb.rearrange("p a b -> p (a b)"), w_out, [], ['wout'])
    xt4 = [A.alloc(1024, F32) for _ in range(4)]
    xn_ring = Ring([A.alloc(1024, BF16) for _ in range(2)], 'xn2')
    hT2 = A.alloc(8 * TN, BF16, shape3=TN)
    hs_ring = tmp(name='hsl')
    on_ring = tmp(name='onl')
    braT = A.alloc(8 * TN, BF16, shape3=TN)
    brbT = A.alloc(8 * TN, BF16, shape3=TN)
    mrgT = A.alloc(8 * TN, BF16, shape3=TN)
    h2T = A.alloc(8 * TN, BF16, shape3=TN)
    actT = A.alloc(32 * TN, BF16, shape3=TN)
    wring = Ring([A.alloc(1024, BF16, shape3=128) for _ in range(8)], 'wch')
    w2ring = Ring([A.alloc(1024, BF16) for _ in range(6)], 'w2ch')
    t_e = tmp(name='e')
    t_t = tmp(cnt=1, name='t')
    t_so = tmp(name='so')
    t_t2 = tmp(cnt=1, name='t2')
    t_sga = tmp(name='sga')
    t_sgb = tmp(name='sgb')
    t_m1 = tmp(cnt=1, name='m1')
    t_m2 = tmp(cnt=1, name='m2')
    t_rl = tmp(dt=BF16, name='rl')
    gen2 = Ring(banks[0:4], 'psg2')
    acc_keys = [('psacc', i) for i in range(4)]
    acc = banks[4:8]

    def wchunk(cid):
        w, wk = wring.next()
        dma('sp', w.rearrange("p a b -> p (a b)"), wf2_bf[cid], [('wf2', cid // 8 * 8)], [wk])
        return w, wk

    def proj(w, wk, src, src_key, ps, psk):
        def mm(e):
            ins = None
            for kc in range(8):
                ins = e.matmul(ps[:, 0:TN], lhsT=w[:, kc, :], rhs=src[:, kc, 0:TN], start=(kc == 0), stop=(kc == 7))
            return ins
        P.add('pe', mm, [wk, src_key], [psk])

    for ti in range(NT if stop_after >= 3 else 0):
        p0 = NMETA + ti * TN
        pr0 = ti * TN
        dma('sp', hT2, hT_scr[ti], [], ['hT2'])
        for s_ in range(4):
            dma('sp', xt4[s_], xs[p0 + 128 * s_:p0 + 128 * (s_ + 1), :], [], [('xt4', s_)])
        for j in range(8):
            w, wk = wchunk(0 + j)
            ps, psk = gen2.next()
            proj(w, wk, hT2, 'hT2', ps, psk)
            e_, ke = t_e.next()
            P.add('act', lambda e, e_=e_, ps=ps: e.activation(out=e_, in_=ps[:, 0:TN], func=AF.Gelu), [psk], [ke])
            hsl, khsl = hs_ring.next()
            dma('sp', hsl, hs_scr[j, :, pr0:pr0 + TN], [], [khsl])
            P.add('pool', lambda e, e_=e_, j=j, hsl=hsl: e.tensor_tensor(
                out=braT[:, j, :], in0=hsl, in1=e_, op=ALU.mult), [khsl, ke], [('braT', j)])
        for j in range(8):
            w, wk = wchunk(8 + j)
            ps, psk = gen2.next()
            proj(w, wk, hT2, 'hT2', ps, psk)
            so, kso = t_so.next()
            P.add('act', lambda e, so=so, ps=ps: e.activation(out=so, in_=ps[:, 0:TN], func=AF.Sigmoid),
                  [psk], [kso])
            t2, kt2 = t_t2.next()
            P.add('dve', lambda e, t2=t2, so=so, ps=ps, j=j: e.scalar_tensor_tensor(
                out=t2, in0=so, scalar=col(dv, DV_HGG + j), in1=ps[:, 0:TN], op0=ALU.mult, op1=ALU.mult),
                [kso, psk, 'dv'], [kt2])
            onl, konl = on_ring.next()
            dma('sp', onl, on_scr[j, :, pr0:pr0 + TN], [], [konl])
            P.add('pool', lambda e, t2=t2, j=j, onl=onl: e.tensor_tensor(
                out=brbT[:, j, :], in0=onl, in1=t2, op=ALU.mult), [konl, kt2], [('brbT', j)])
        bra_keys = [('braT', j) for j in range(8)]
        brb_keys = [('brbT', j) for j in range(8)]
        for j in range(8):
            w, wk = wchunk(16 + j)
            ps_ga, kga = gen2.next()
            proj(w, wk, hT2, 'hT2', ps_ga, kga)
            sga, ksga = t_sga.next()
            P.add('act', lambda e, sga=sga, ps_ga=ps_ga: e.activation(out=sga, in_=ps_ga[:, 0:TN], func=AF.Sigmoid),
                  [kga], [ksga])
            w, wk = wchunk(24 + j)
            ps_gb, kgb = gen2.next()
            proj(w, wk, hT2, 'hT2', ps_gb, kgb)
            sgb, ksgb = t_sgb.next()
            P.add('act', lambda e, sgb=sgb, ps_gb=ps_gb: e.activation(out=sgb, in_=ps_gb[:, 0:TN], func=AF.Sigmoid),
                  [kgb], [ksgb])
            w, wk = wchunk(32 + j)
            ps_pa, kpa = gen2.next()

            def mm_pa(e, w=w, ps_pa=ps_pa):
                ins = None
                for kc in range(8):
                    ins = e.matmul(ps_pa[:, 0:TN], lhsT=w[:, kc, :], rhs=braT[:, kc, :], start=(kc == 0), stop=(kc == 7))
                return ins
            P.add('pe', mm_pa, [wk] + bra_keys, [kpa])
            m1, km1 = t_m1.next()
            P.add('dve', lambda e, m1=m1, ps_pa=ps_pa, sga=sga: e.tensor_tensor(
                out=m1, in0=ps_pa[:, 0:TN], in1=sga, op=ALU.mult), [kpa, ksga], [km1])
            w, wk = wchunk(40 + j)
            ps_pb, kpb = gen2.next()

            def mm_pb(e, w=w, ps_pb=ps_pb):
                ins = None
                for kc in range(8):
                    ins = e.matmul(ps_pb[:, 0:TN], lhsT=w[:, kc, :], rhs=brbT[:, kc, :], start=(kc == 0), stop=(kc == 7))
                return ins
            P.add('pe', mm_pb, [wk] + brb_keys, [kpb])
            m2, km2 = t_m2.next()
            P.add('dve', lambda e, m2=m2, ps_pb=ps_pb, sgb=sgb: e.tensor_tensor(
                out=m2, in0=ps_pb[:, 0:TN], in1=sgb, op=ALU.mult), [kpb, ksgb], [km2])
            P.add('pool', lambda e, m1=m1, m2=m2, j=j: e.tensor_tensor(
                out=mrgT[:, j, :], in0=m1, in1=m2, op=ALU.add), [km1, km2], [('mrgT', j)])
        mrg_keys = [('mrgT', j) for j in range(8)]
        for s in range(4):
            for hf in range(2):
                pa_, pak = acc[(s * 2 + hf) % 4], acc_keys[(s * 2 + hf) % 4]

                def mm_wo(e, s=s, hf=hf, pa_=pa_):
                    ins = None
                    for kc in range(8):
                        ins = e.matmul(pa_[:, 0:512], lhsT=mrgT[:, kc, s * 128:(s + 1) * 128],
                                       rhs=wout_sb[:, kc, hf * 512:(hf + 1) * 512], start=(kc == 0), stop=(kc == 7))
                    return ins
                P.add('pe', mm_wo, mrg_keys + ['wout'], [pak])
                P.add('dve', lambda e, s=s, hf=hf, pa_=pa_: e.tensor_tensor(
                    out=xt4[s][:, hf * 512:(hf + 1) * 512], in0=pa_[:, 0:512],
                    in1=xt4[s][:, hf * 512:(hf + 1) * 512], op=ALU.add), [pak, ('xt4', s)], [('xt4', s)])
        for s in range(4):
            xt, xk = xt4[s], ('xt4', s)
            P.add('pool', lambda e, s=s: e.memset(col(ss, s), 0.0), [], [('ss', s)])
            P.add('act', lambda e, xt=xt, s=s: e.activation(out=junk, in_=xt, func=AF.Square,
                                                            accum_out=ss[:, s:s + 1]), [xk, ('ss', s)],
                  ['junk', ('ss', s)])
            P.add('act', lambda e, s=s: e.activation(out=rs[:, s:s + 1], in_=ss[:, s:s + 1], func=AF.Ln, bias=dv[:, DV_EPSD:DV_EPSD + 1]), [('ss', s), 'dv'], [('rs', s)])
            P.add('act', lambda e, s=s: e.activation(out=rs[:, s:s + 1], in_=rs[:, s:s + 1], func=AF.Exp, scale=-0.5), [('rs', s)], [('rs', s)])
            xn, xnk = xn_ring.next()
            P.add('dve', lambda e, xt=xt, xn=xn, s=s: e.tensor_scalar(
                out=xn, in0=xt, scalar1=rs[:, s:s + 1], scalar2=32.0, op0=ALU.mult, op1=ALU.mult),
                [xk, ('rs', s)], [xnk])
            pb, pbk = gen2.next()
            psT = pb.bitcast(BF16).rearrange("p (a b) -> p a b", b=128)

            def tr2(e, xn=xn, psT=psT):
                ins = None
                for kc in range(8):
                    ins = e.transpose(out=psT[:, kc, :], in_=xn[:, kc * 128:(kc + 1) * 128], identity=ident)
                return ins
            P.add('pe', tr2, [xnk, 'ident'], [pbk])
            P.add('dve', lambda e, psT=psT, s=s: e.tensor_tensor(
                out=h2T[:, :, s * 128:(s + 1) * 128], in0=psT,
                in1=pp[:, PP_GMLP:PP_GMLP + 8].unsqueeze(2).broadcast_to([128, 8, 128]), op=ALU.mult),
                [pbk, 'pp'], ['h2T'])
        for m in range(32):
            w, wk = wchunk(48 + m)
            ps, psk = gen2.next()
            proj(w, wk, h2T, 'h2T', ps, psk)
            rl, krl = t_rl.next()
            P.add('act', lambda e, rl=rl, ps=ps: e.activation(out=rl, in_=ps[:, 0:TN], func=AF.Relu), [psk], [krl])
            P.add('dve' if m % 2 == 0 else 'pool', lambda e, rl=rl, m=m: e.tensor_tensor(
                out=actT[:, m, :], in0=rl, in1=rl, op=ALU.mult), [krl], [('actT', m)])
        for hf in range(2):
            for mp in range(16):
                w2c, w2k = w2ring.next()
                q = hf * 16 + mp
                dma('sp', w2c, w2_bf[q], [('w2s', q // 8 * 8)], [w2k])
                for mm in range(2):
                    m = 2 * mp + mm

                    def mm_2(e, m=m, mm=mm, w2c=w2c):
                        ins = None
                        for s in range(4):
                            ins = e.matmul(acc[s][:, 0:512], lhsT=actT[:, m, s * 128:(s + 1) * 128],
                                           rhs=w2c[:, mm * 512:(mm + 1) * 512], start=(m == 0), stop=(m == 31))
                        return ins
                    P.add('pe', mm_2, [w2k, ('actT', m)], acc_keys)
            for s in range(4):
                P.add('dve', lambda e, s=s, hf=hf: e.tensor_tensor(
                    out=xt4[s][:, hf * 512:(hf + 1) * 512], in0=acc[s][:, 0:512],
                    in1=xt4[s][:, hf * 512:(hf + 1) * 512], op=ALU.add), [acc_keys[s], ('xt4', s)], [('xt4', s)])
        for s in range(4):
            xt, xk = xt4[s], ('xt4', s)
            P.add('pool', lambda e, s=s: e.memset(col(ss, 4 + s % 2), 0.0), [], [('ss', 4 + s % 2)])
            P.add('act', lambda e, xt=xt, s=s: e.activation(out=junk, in_=xt, func=AF.Square,
                                                            accum_out=ss[:, 4 + s % 2:5 + s % 2]),
                  [xk, ('ss', 4 + s % 2)], ['junk', ('ss', 4 + s % 2)])
            P.add('act', lambda e, s=s: e.activation(out=rs[:, 4 + s % 2:5 + s % 2], in_=ss[:, 4 + s % 2:5 + s % 2], func=AF.Ln, bias=dv[:, DV_EPSD:DV_EPSD + 1]), [('ss', 4 + s % 2), 'dv'], [('rs', 4 + s % 2)])
            P.add('act', lambda e, s=s: e.activation(out=rs[:, 4 + s % 2:5 + s % 2], in_=rs[:, 4 + s % 2:5 + s % 2], func=AF.Exp, scale=-0.5), [('rs', 4 + s % 2)], [('rs', 4 + s % 2)])
            P.add('dve', lambda e, xt=xt, s=s: e.scalar_tensor_tensor(
                out=xt, in0=xt, scalar=rs[:, 4 + s % 2:5 + s % 2], in1=gfin, op0=ALU.mult, op1=ALU.mult),
                [xk, ('rs', 4 + s % 2), 'gfin'], [xk])
            dma('sp', y_d[pr0 + s * 128:pr0 + (s + 1) * 128, :], xt, [xk], [('yout', ti, s)])
    P.barrier()

    with (nc.semaphore("s_pe") as s_pe, nc.semaphore("s_act") as s_act, nc.semaphore("s_dve") as s_dve,
          nc.semaphore("s_pool") as s_pool):
        import contextlib
        with contextlib.ExitStack() as st:
            dsems = dict(sp=[st.enter_context(nc.semaphore(f"s_dma{i}")) for i in range(NDSEM)],
                         pool=[st.enter_context(nc.semaphore(f"s_dmap{i}")) for i in range(8)])
            sems = dict(pe=s_pe, act=s_act, dve=s_dve, pool=s_pool)
            P.finalize(nc, sems, dsems)
            with nc.Block() as block:
                @block.sync
                def _(e):
                    P.emit('sp', e)

                @block.tensor
                def _(e):
                    P.emit('pe', e)

                @block.scalar
                def _(e):
                    P.emit('act', e)

                @block.vector
                def _(e):
                    P.emit('dve', e)

                @block.gpsimd
                def _(e):
                    P.emit('pool', e)
    return nc


def _pack_params(conv_w, conv_b, rg_ba, rg_bx, rg_lambda, hg_lb_logits, hg_norm_g, norm_mix_g, norm_mlp_g):
    pp = np.zeros((128, PP_N), np.float32)
    for h in range(8):
        sl = slice(h * 128, (h + 1) * 128)
        b0 = h * PP_HEAD
        for j in range(4):
            pp[:, b0 + j] = conv_w[0, j, sl]
        pp[:, b0 + 4] = conv_b[0, sl]
        for dr in range(2):
            pp[:, b0 + 5 + dr] = rg_ba[0, dr, sl]
            pp[:, b0 + 7 + dr] = rg_bx[0, dr, sl]
            pp[:, b0 + 9 + dr] = rg_lambda[0, dr, sl]
            pp[:, b0 + 11 + dr] = hg_lb_logits[0, dr, sl]
            pp[:, b0 + 13 + dr] = hg_lb_logits[1, dr, sl]
        pp[:, b0 + 15] = hg_norm_g[0, sl]
    for kc in range(8):
        pp[:, PP_GMIX + kc] = norm_mix_g[0, kc * 128:(kc + 1) * 128]
        pp[:, PP_GMLP + kc] = norm_mlp_g[0, kc * 128:(kc + 1) * 128]
    return pp


def _consts():
    c = np.zeros((128, 128 * 4 + 1024), np.float32)
    c[:, 0:128] = np.eye(128, dtype=np.float32)
    c[:, 128:256] = 1.0
    s = np.arange(128)[:, None]
    t = np.arange(128)[None, :]
    same = (s // 64) == (t // 64)
    c[:, 256:384] = -1.0 * (same & (s <= t))
    c[:, 384:512] = -1.0 * (same & (s >= t))
    mF = np.ones(512, np.float32)
    mF[0::64] = 0.0
    mR = np.ones(512, np.float32)
    mR[63::64] = 0.0
    c[:, 512:1024] = mF[None, :]
    c[:, 1024:1536] = mR[None, :]
    return c


def _kc_layout(w):
    return np.ascontiguousarray(w.reshape(8, 128, -1).transpose(1, 0, 2))


def kernel(x_prompt, x_sample, meta_tokens, hg_lb_logits, norm_mix_g, w_in, conv_w, conv_b, rg_wa, rg_ba,
           rg_wx, rg_bx, rg_lambda, hg_norm_g, w_branch_a, w_branch_b, w_out, norm_mlp_g, w_mlp1, w_mlp2,
           final_norm_g):
    f = lambda a: np.asarray(a, dtype=np.float32)
    x_prompt, x_sample, meta_tokens = f(x_prompt), f(x_sample), f(meta_tokens)
    T = x_prompt.shape[1]
    NT = T // TN
    seqs = [x_prompt[i] for i in range(x_prompt.shape[0])] + [x_sample[i] for i in range(x_sample.shape[0])]
    assert len(seqs) <= NCORES
    win = f(w_in)[0]
    grp = lambda g: win[:, g * 1024:(g + 1) * 1024]
    w_mix = np.stack([_kc_layout(grp(g)).reshape(128, 8 * 1024) for g in (0, 2, 3, 4, 5)])
    wa, wx = f(rg_wa)[0], f(rg_wx)[0]
    wgate = np.stack([wa[0], wx[0], wa[1], wx[1]])
    wgate = np.ascontiguousarray(wgate.transpose(2, 0, 1, 3)).reshape(128, 4 * 8 * 128)

    def chunks(w):
        k = _kc_layout(w)
        C = k.shape[2]
        return np.ascontiguousarray(k.reshape(128, 8, C // 128, 128).transpose(2, 0, 1, 3)).reshape(C // 128, 128, 1024)
    w_f2 = np.concatenate([chunks(grp(1)), chunks(grp(6)), chunks(grp(7)), chunks(grp(8)),
                           chunks(f(w_branch_a)[0]), chunks(f(w_branch_b)[0]), chunks(f(w_mlp1)[0])], axis=0)
    wout_l = _kc_layout(f(w_out)[0]).reshape(128, 8 * 1024)
    w2_l = np.ascontiguousarray(f(w_mlp2)[0].reshape(16, 2, 128, 2, 512).transpose(3, 0, 2, 1, 4)).reshape(32, 128, 1024)
    pp = _pack_params(f(conv_w), f(conv_b), f(rg_ba), f(rg_bx), f(rg_lambda), f(hg_lb_logits), f(hg_norm_g),
                      f(norm_mix_g), f(norm_mlp_g))
    gfin = np.ascontiguousarray(np.broadcast_to(f(final_norm_g)[None, :], (128, D)))
    consts = _consts()
    shared = dict(w_mix=w_mix, w_gate=wgate, w_f2=w_f2, w_out=wout_l, w_2=w2_l, pp=pp, gfin=gfin, consts=consts)
    in_maps = []
    for c in range(NCORES):
        if c < len(seqs):
            xs = np.concatenate([meta_tokens, seqs[c]], axis=0)
        else:
            xs = np.zeros((T + NMETA, D), np.float32)
        m = dict(shared)
        m["xs"] = np.ascontiguousarray(xs)
        in_maps.append(m)
    nc = build(NT)
    res = run_bass_kernel_spmd(nc, in_maps, core_ids=list(range(NCORES)))
    outs = [np.asarray(res.results[c]["y"], dtype=np.float32) for c in range(len(seqs))]
    nb = x_prompt.shape[0]
    y_prompt = np.stack(outs[:nb])
    y_sample = np.stack(outs[nb:])
    return (y_prompt, y_sample)
```
